# Optimizing a Trainium2 kernel written in Bass

```python
import math
import jax, jax.numpy as jnp
from jax import lax
import numpy as np

D_MODEL = 4096
BATCH = 1
SEQ = 8192
DEPTH = 4

MIXER_ORDER = ("ssd", "pool", "moba")
RMS_EPS = 1e-6

SSD_EXPAND = 2
SSD_D_INNER = SSD_EXPAND * D_MODEL
SSD_HEAD_DIM = 64
SSD_N_HEADS = SSD_D_INNER // SSD_HEAD_DIM
SSD_N_GROUPS = 8
SSD_HEADS_PER_GROUP = SSD_N_HEADS // SSD_N_GROUPS
SSD_D_STATE = 128
SSD_CONV_WIDTH = 4
SSD_CONV_DIM = SSD_D_INNER + 2 * SSD_N_GROUPS * SSD_D_STATE
SSD_IN_DIM = SSD_D_INNER + SSD_CONV_DIM + SSD_N_HEADS
SSD_CHUNK = 128

POOL_WIDTH = 2 * D_MODEL
POOL_WINDOWS = (2, 4, 8, 16)
POOL_GROUPS = len(POOL_WINDOWS)
POOL_GROUP_DIM = POOL_WIDTH // POOL_GROUPS

MOBA_HEAD_DIM = 128
MOBA_WIDTH = D_MODEL
MOBA_N_HEADS = MOBA_WIDTH // MOBA_HEAD_DIM
MOBA_BLOCK = 256
MOBA_TOPK = 3
MOBA_Q_CHUNK = 16

kernel_name = "hybrid_ssd_pool_moba_trunk"


def rms_norm(x, w):
    xf = x.astype(jnp.float32)
    y = xf * lax.rsqrt(jnp.mean(xf * xf, axis=-1, keepdims=True) + RMS_EPS)
    return (y * w.astype(jnp.float32)).astype(x.dtype)


def causal_dwconv(x, w, b):
    k_w = w.shape[0]
    s = x.shape[1]
    xp = jnp.pad(x, ((0, 0), (k_w - 1, 0), (0, 0)))
    y = b
    for k in range(k_w):
        y = y + xp[:, k:k + s] * w[k]
    return y


def ssd_scan(x, a, bm, cm):
    b, s, g, r, p = x.shape
    n = bm.shape[-1]
    l = SSD_CHUNK
    c = s // l
    x = x.reshape(b, c, l, g, r, p)
    bm = bm.reshape(b, c, l, g, n)
    cm = cm.reshape(b, c, l, g, n)
    a = a.reshape(b, c, l, g, r).transpose(0, 1, 3, 4, 2)
    a_cs = jnp.cumsum(a, axis=-1)
    seg = a_cs[..., :, None] - a_cs[..., None, :]
    tril = jnp.tril(jnp.ones((l, l), dtype=bool))
    decay_in = jnp.exp(jnp.where(tril, seg, -jnp.inf))
    cb = jnp.einsum("bclgn,bcsgn->bcgls", cm, bm)
    y_diag = jnp.einsum("bcgls,bcgrls,bcsgrp->bclgrp", cb, decay_in, x)
    decay_to_end = jnp.exp(a_cs[..., -1:] - a_cs)
    chunk_states = jnp.einsum("bclgn,bcgrl,bclgrp->bcgrpn", bm, decay_to_end, x)
    chunk_decay = jnp.exp(a_cs[..., -1])

    def step(h, inp):
        st, dec = inp
        return h * dec[..., None, None] + st, h

    h0 = jnp.zeros((b, g, r, p, n), dtype=chunk_states.dtype)
    _, prev = lax.scan(step, h0, (jnp.moveaxis(chunk_states, 1, 0),
                                  jnp.moveaxis(chunk_decay, 1, 0)))
    prev = jnp.moveaxis(prev, 0, 1)
    y_off = jnp.einsum("bclgn,bcgrpn,bcgrl->bclgrp", cm, prev, jnp.exp(a_cs))
    return (y_diag + y_off).reshape(b, s, g, r, p)


def ssd_mixer(u, w_in, conv_w, conv_b, dt_bias, a_log, d_skip, norm_w, w_out):
    b, s, _ = u.shape
    G, R, P, N = SSD_N_GROUPS, SSD_HEADS_PER_GROUP, SSD_HEAD_DIM, SSD_D_STATE
    zxbcdt = u @ w_in
    z = zxbcdt[..., :SSD_D_INNER]
    xbc = zxbcdt[..., SSD_D_INNER:SSD_D_INNER + SSD_CONV_DIM]
    dt = zxbcdt[..., SSD_D_INNER + SSD_CONV_DIM:]
    xbc = jax.nn.silu(causal_dwconv(xbc, conv_w, conv_b))
    xs = xbc[..., :SSD_D_INNER].reshape(b, s, G, R, P)
    bm = xbc[..., SSD_D_INNER:SSD_D_INNER + G * N].reshape(b, s, G, N)
    cm = xbc[..., SSD_D_INNER + G * N:].reshape(b, s, G, N)
    dt = jax.nn.softplus(dt.astype(jnp.float32) + dt_bias.astype(jnp.float32)).reshape(b, s, G, R)
    a = -jnp.exp(a_log.astype(jnp.float32)).reshape(G, R)
    y = ssd_scan(xs * dt[..., None], dt * a, bm, cm)
    y = y + xs * d_skip.reshape(G, R)[:, :, None]
    y = y.reshape(b, s, SSD_D_INNER)
    yz = (y * jax.nn.silu(z.astype(y.dtype))).reshape(b, s, G, SSD_D_INNER // G)
    yf = yz.astype(jnp.float32)
    yf = yf * lax.rsqrt(jnp.mean(yf * yf, axis=-1, keepdims=True) + RMS_EPS)
    y = (yf.reshape(b, s, SSD_D_INNER) * norm_w.astype(jnp.float32)).astype(u.dtype)
    return y @ w_out


def pool_mixer(u, w_in, w_grp, b_grp, scale, w_out):
    b, s, _ = u.shape
    proj = u @ w_in
    v, z = proj[..., :POOL_WIDTH], proj[..., POOL_WIDTH:]
    cs = jnp.cumsum(v.astype(jnp.float32), axis=1)
    pos = jnp.arange(s)
    groups = []
    for gi, w in enumerate(POOL_WINDOWS):
        sl = slice(gi * POOL_GROUP_DIM, (gi + 1) * POOL_GROUP_DIM)
        cs_g = cs[..., sl]
        prev = jnp.pad(cs_g, ((0, 0), (w, 0), (0, 0)))[:, :s]
        count = jnp.minimum(pos + 1, w).astype(jnp.float32)[None, :, None]
        groups.append((cs_g - prev) / count - v[..., sl].astype(jnp.float32))
    d = jnp.stack(groups, axis=2).astype(u.dtype)
    mixed = jnp.einsum("bsgc,gcd->bsgd", d, w_grp) + b_grp
    mixed = mixed.reshape(b, s, POOL_WIDTH) * scale
    return (mixed * jax.nn.silu(z)) @ w_out


def moba_mixer(u, w_in, q_norm_w, k_norm_w, w_out):
    b, s, _ = u.shape
    H, Dh, BLK, Q = MOBA_N_HEADS, MOBA_HEAD_DIM, MOBA_BLOCK, MOBA_Q_CHUNK
    proj = u @ w_in
    q = rms_norm(proj[..., 0 * MOBA_WIDTH:1 * MOBA_WIDTH].reshape(b, s, H, Dh), q_norm_w)
    k = rms_norm(proj[..., 1 * MOBA_WIDTH:2 * MOBA_WIDTH].reshape(b, s, H, Dh), k_norm_w)
    v = proj[..., 2 * MOBA_WIDTH:3 * MOBA_WIDTH].reshape(b, s, H, Dh)
    gate = proj[..., 3 * MOBA_WIDTH:]
    s_pad = -(-s // BLK) * BLK
    pad = ((0, 0), (0, s_pad - s), (0, 0), (0, 0))
    q = jnp.pad(q, pad).transpose(0, 2, 1, 3)
    k = jnp.pad(k, pad).transpose(0, 2, 1, 3)
    v = jnp.pad(v, pad).transpose(0, 2, 1, 3)
    nb = s_pad // BLK
    kb = k.reshape(b, H, nb, BLK, Dh)
    vb = v.reshape(b, H, nb, BLK, Dh)
    k_mean = jnp.mean(kb.astype(jnp.float32), axis=3).astype(k.dtype)
    gscore = jnp.einsum("bhsd,bhnd->bhsn", q, k_mean).astype(jnp.float32)
    q_blk = jnp.arange(s_pad) // BLK
    past = jnp.arange(nb)[None, :] < q_blk[:, None]
    gscore = jnp.where(past, gscore, -jnp.inf)
    k_sel = min(MOBA_TOPK, nb)
    _, sel = lax.top_k(gscore, k_sel)
    sel_valid = sel < q_blk[None, None, :, None]
    scale = 1.0 / math.sqrt(Dh)
    bi = jnp.arange(b)[:, None, None, None]
    hi = jnp.arange(H)[None, :, None, None]

    def chunk_fn(c):
        start = c * Q
        qc = lax.dynamic_slice_in_dim(q, start, Q, axis=2)
        sel_c = lax.dynamic_slice_in_dim(sel, start, Q, axis=2)
        valid_c = lax.dynamic_slice_in_dim(sel_valid, start, Q, axis=2)
        own = start // BLK
        k_own = lax.dynamic_index_in_dim(kb, own, axis=2, keepdims=False)
        v_own = lax.dynamic_index_in_dim(vb, own, axis=2, keepdims=False)
        k_g = kb[bi, hi, sel_c]
        v_g = vb[bi, hi, sel_c]
        s_sel = jnp.einsum("bhqd,bhqkjd->bhqkj", qc, k_g).astype(jnp.float32) * scale
        s_sel = jnp.where(valid_c[..., None], s_sel, -jnp.inf).reshape(b, H, Q, k_sel * BLK)
        s_own = jnp.einsum("bhqd,bhjd->bhqj", qc, k_own).astype(jnp.float32) * scale
        qpos = start + jnp.arange(Q)
        kpos = own * BLK + jnp.arange(BLK)
        s_own = jnp.where(kpos[None, :] <= qpos[:, None], s_own, -jnp.inf)
        p = jax.nn.softmax(jnp.concatenate([s_sel, s_own], axis=-1), axis=-1).astype(v.dtype)
        p_sel = p[..., :k_sel * BLK].reshape(b, H, Q, k_sel, BLK)
        p_own = p[..., k_sel * BLK:]
        return (jnp.einsum("bhqkj,bhqkjd->bhqd", p_sel, v_g)
                + jnp.einsum("bhqj,bhjd->bhqd", p_own, v_own))

    outs = lax.map(chunk_fn, jnp.arange(s_pad // Q))
    o = outs.transpose(1, 0, 3, 2, 4).reshape(b, s_pad, H * Dh)[:, :s]
    return (o * jax.nn.silu(gate)) @ w_out


MIXERS = {"ssd": ssd_mixer, "pool": pool_mixer, "moba": moba_mixer}


def setup_inputs(seed: int = 0) -> dict:
    key = jax.random.key(seed)
    keys = jax.random.split(key, 64)
    counter = iter(range(64))

    def nk():
        return keys[next(counter)]

    def nrm(shape, sc):
        return jax.random.normal(nk(), shape, jnp.float32) * sc

    p = {}
    p["x"] = nrm((BATCH, SEQ, D_MODEL), 1.0)

    def add_ssd(name):
        H = SSD_N_HEADS
        p[name + "_w_in"] = nrm((D_MODEL, SSD_IN_DIM), D_MODEL ** -0.5)
        p[name + "_conv_w"] = nrm((SSD_CONV_WIDTH, SSD_CONV_DIM), SSD_CONV_WIDTH ** -0.5)
        p[name + "_conv_b"] = nrm((SSD_CONV_DIM,), 0.02)
        dt = jnp.exp(jax.random.uniform(nk(), (H,), jnp.float32)
                     * (math.log(0.1) - math.log(0.001)) + math.log(0.001))
        p[name + "_dt_bias"] = dt + jnp.log(-jnp.expm1(-dt))
        p[name + "_a_log"] = jnp.log(jax.random.uniform(nk(), (H,), jnp.float32, minval=1.0, maxval=16.0))
        p[name + "_d"] = 1.0 + nrm((H,), 0.02)
        p[name + "_norm_w"] = 1.0 + nrm((SSD_D_INNER,), 0.02)
        p[name + "_w_out"] = nrm((SSD_D_INNER, D_MODEL), SSD_D_INNER ** -0.5)

    p["norm0"] = 1.0 + nrm((D_MODEL,), 0.02)
    add_ssd("ssd0")
    p["norm1"] = 1.0 + nrm((D_MODEL,), 0.02)
    p["pool1_w_in"] = nrm((D_MODEL, 2 * POOL_WIDTH), D_MODEL ** -0.5)
    p["pool1_w_grp"] = nrm((POOL_GROUPS, POOL_GROUP_DIM, POOL_GROUP_DIM), POOL_GROUP_DIM ** -0.5)
    p["pool1_b_grp"] = nrm((POOL_GROUPS, POOL_GROUP_DIM), 0.02)
    p["pool1_scale"] = 1.0 + nrm((POOL_WIDTH,), 0.02)
    p["pool1_w_out"] = nrm((POOL_WIDTH, D_MODEL), POOL_WIDTH ** -0.5)
    p["norm2"] = 1.0 + nrm((D_MODEL,), 0.02)
    p["moba2_w_in"] = nrm((D_MODEL, 4 * MOBA_WIDTH), D_MODEL ** -0.5)
    p["moba2_q_norm"] = 1.0 + nrm((MOBA_HEAD_DIM,), 0.02)
    p["moba2_k_norm"] = 1.0 + nrm((MOBA_HEAD_DIM,), 0.02)
    p["moba2_w_out"] = nrm((MOBA_WIDTH, D_MODEL), MOBA_WIDTH ** -0.5)
    p["norm3"] = 1.0 + nrm((D_MODEL,), 0.02)
    add_ssd("ssd3")
    return p


def reference(x,
              norm0, ssd0_w_in, ssd0_conv_w, ssd0_conv_b, ssd0_dt_bias, ssd0_a_log, ssd0_d, ssd0_norm_w, ssd0_w_out,
              norm1, pool1_w_in, pool1_w_grp, pool1_b_grp, pool1_scale, pool1_w_out,
              norm2, moba2_w_in, moba2_q_norm, moba2_k_norm, moba2_w_out,
              norm3, ssd3_w_in, ssd3_conv_w, ssd3_conv_b, ssd3_dt_bias, ssd3_a_log, ssd3_d, ssd3_norm_w, ssd3_w_out):
    layer_params = [
        (norm0, (ssd0_w_in, ssd0_conv_w, ssd0_conv_b, ssd0_dt_bias, ssd0_a_log, ssd0_d, ssd0_norm_w, ssd0_w_out)),
        (norm1, (pool1_w_in, pool1_w_grp, pool1_b_grp, pool1_scale, pool1_w_out)),
        (norm2, (moba2_w_in, moba2_q_norm, moba2_k_norm, moba2_w_out)),
        (norm3, (ssd3_w_in, ssd3_conv_w, ssd3_conv_b, ssd3_dt_bias, ssd3_a_log, ssd3_d, ssd3_norm_w, ssd3_w_out)),
    ]
    h = x
    for i in range(DEPTH):
        g, params = layer_params[i]
        mixer = MIXERS[MIXER_ORDER[i % len(MIXER_ORDER)]]
        h = h + mixer(rms_norm(h, g), *params)
    return h
```

```python
import os
import numpy as np
import ml_dtypes
from contextlib import ExitStack
import concourse.bass as bass
import concourse.mybir as mybir
from concourse.bass_utils import run_bass_kernel_spmd

F32 = mybir.dt.float32
BF16 = mybir.dt.bfloat16
AF = mybir.ActivationFunctionType
ALU = mybir.AluOpType
AX = mybir.AxisListType
NPBF = ml_dtypes.bfloat16

NCORES = 8
D = 4096
SEQ = 8192
TPC = SEQ // NCORES
EPS = 1e-6
SAME_SYNC = True


class B:
    def __init__(self, t, k):
        self.t = t
        self.k = k

    def __getitem__(self, idx):
        return self.t[idx]


class Sched:
    NDMA = 32

    def __init__(self, nc, es):
        self.nc = nc
        self.es = es
        self.eng = dict(pe=nc.tensor, dve=nc.vector, act=nc.scalar, pool=nc.gpsimd, sp=nc.sync)
        self.sem = {k: es.enter_context(nc.semaphore("sem_" + k)) for k in self.eng}
        self.cnt = {k: 0 for k in self.eng}
        self.dsem = [es.enter_context(nc.semaphore(f"dsem{i}")) for i in range(self.NDMA)]
        self.duse = [0] * self.NDMA
        self.drr = 0
        self.seen = {k: {} for k in self.eng}
        self.lastw = {}
        self.readers = {}
        self.nbuf = 0

    def sb(self, shape, dtype, name=None):
        self.nbuf += 1
        name = name or f"sb{self.nbuf}"
        t = self.es.enter_context(self.nc.sbuf_tensor(name, list(shape), dtype))
        return B(t, name)

    def ps(self, shape, dtype=F32, name=None):
        self.nbuf += 1
        name = name or f"ps{self.nbuf}"
        t = self.es.enter_context(self.nc.psum_tensor(name, list(shape), dtype))
        return B(t, name)

    def _semof(self, sk):
        return self.dsem[sk[1]] if isinstance(sk, tuple) else self.sem[sk]

    def _wait(self, e, sk, val):
        if val <= 0:
            return
        if sk == e:
            if e == "pe" or e == "sp" or not SAME_SYNC:
                return
        if self.seen[e].get(sk, 0) >= val:
            return
        self.eng[e].wait_ge(self._semof(sk), val)
        self.seen[e][sk] = val

    def _deps(self, e, r, w):
        evs = {}
        for x in r:
            ev = self.lastw.get(x)
            if ev:
                evs[ev[0]] = max(evs.get(ev[0], 0), ev[1])
        for x in w:
            ev = self.lastw.get(x)
            if ev:
                evs[ev[0]] = max(evs.get(ev[0], 0), ev[1])
            for sk, v in self.readers.get(x, {}).items():
                evs[sk] = max(evs.get(sk, 0), v)
        for sk, v in evs.items():
            self._wait(e, sk, v)

    def _record(self, ev, r, w):
        for x in r:
            d = self.readers.setdefault(x, {})
            d[ev[0]] = max(d.get(ev[0], 0), ev[1])
        for x in w:
            self.lastw[x] = ev
            self.readers[x] = {}

    def op(self, e, meth, *args, r=(), w=(), **kw):
        self._deps(e, r, w)
        ins = getattr(self.eng[e], meth)(*args, **kw)
        self.cnt[e] += 1
        ins.then_inc(self.sem[e], 1)
        self._record((e, self.cnt[e]), r, w)
        return ins

    def dma(self, q, out, in_, r=(), w=()):
        k = self.drr
        self.drr = (k + 1) % self.NDMA
        self._deps(q, r, w)
        self._wait(q, ("d", k), 16 * self.duse[k])
        ins = self.eng[q].dma_start(out=out, in_=in_)
        ins.then_inc(self.dsem[k], 16)
        self.duse[k] += 1
        self._record((("d", k), 16 * self.duse[k]), r, w)
        return ins

    def uk(self):
        self.nbuf += 1
        return ("u", self.nbuf)

    def barrier(self):
        for e in ("pe", "dve", "act", "sp"):
            for k in range(self.NDMA):
                if self.duse[k]:
                    self._wait(e, ("d", k), 16 * self.duse[k])
            for e2 in ("pe", "dve", "act", "pool"):
                if e2 != e and self.cnt[e2]:
                    self._wait(e, e2, self.cnt[e2])
        self.lastw = {}
        self.readers = {}

    def finish(self):
        for k in range(self.NDMA):
            if self.duse[k]:
                self._wait("sp", ("d", k), 16 * self.duse[k])
        for e in ("pe", "dve", "act", "pool"):
            if self.cnt[e]:
                self._wait("sp", e, self.cnt[e])


def mk_consts(S):
    c = {}
    c["ones_f"] = S.sb([128, 128], F32, "ones_f")
    S.op("dve", "memset", c["ones_f"][:], 1.0, w=["ones_f"])
    c["ones_b"] = S.sb([128, 128], BF16, "ones_b")
    S.op("dve", "memset", c["ones_b"][:], 1.0, w=["ones_b"])
    return c


def emit_norm(S, C, hT, gw, uT, ucol0, T=TPC, TT=256):
    hv = hT.rearrange("(kc p) t -> p kc t", p=128)
    uv = uT.rearrange("(kc p) t -> p kc t", p=128)
    hb = [S.sb([128, 32, TT], F32) for _ in range(2)]
    ub = [S.sb([128, 32, TT], BF16) for _ in range(2)]
    sq = [S.sb([128, TT], F32) for _ in range(2)]
    shi = [S.sb([128, TT], BF16) for _ in range(2)]
    slo = [S.sb([128, TT], BF16) for _ in range(2)]
    rs = S.sb([128, TT], F32)
    pb = S.ps([128, 512], F32)
    for tt in range(T // TT):
        h = hb[tt % 2]
        u = ub[tt % 2]
        S.dma("sp", h[:], hv[:, :, tt * TT:(tt + 1) * TT], r=["hT"], w=[h.k])
        for kc in range(32):
            s = sq[kc % 2]
            S.op("act", "activation", out=s[:], in_=h[:, kc, :], func=AF.Square, r=[h.k], w=[s.k])
            hi_ = shi[kc % 2]
            lo_ = slo[kc % 2]
            S.op("dve", "tensor_copy", out=hi_[:], in_=s[:], r=[s.k], w=[hi_.k])
            S.op("dve", "tensor_tensor", out=lo_[:], in0=s[:], in1=hi_[:], op=ALU.subtract, r=[s.k, hi_.k], w=[lo_.k])
            S.op("pe", "matmul", pb[:, :TT], lhsT=C["ones_b"][:], rhs=hi_[:], start=(kc == 0), stop=False,
                 r=[hi_.k, "ones_b"], w=[pb.k])
            S.op("pe", "matmul", pb[:, :TT], lhsT=C["ones_b"][:], rhs=lo_[:], start=False, stop=(kc == 31),
                 r=[lo_.k, "ones_b"], w=[pb.k])
        S.op("act", "activation", out=rs[:], in_=pb[:, :TT], func=AF.Sqrt, scale=1.0 / D, bias=C["eps"][:],
             r=[pb.k, "eps"], w=[rs.k])
        S.op("dve", "reciprocal", out=rs[:], in_=rs[:], r=[rs.k], w=[rs.k])
        for kc in range(32):
            S.op("dve", "scalar_tensor_tensor", out=u[:, kc, :], in0=h[:, kc, :], scalar=gw[:, kc:kc + 1],
                 in1=rs[:], op0=ALU.mult, op1=ALU.mult, r=[h.k, rs.k, gw.k], w=[u.k])
        S.dma("act", uv[:, :, ucol0 + tt * TT:ucol0 + (tt + 1) * TT], u[:], r=[u.k], w=[S.uk()])


def emit_outproj(S, yT, KC, w_out, hT_in, hT_out, T=TPC):
    yv = yT.rearrange("(kc p) t -> p kc t", p=128)
    wv = w_out.rearrange("(kc p) n -> p kc n", p=128)
    yb = S.sb([128, KC, T], BF16)
    NP = 4
    KP = KC // NP
    for q in range(NP):
        S.dma("sp", yb[:, q * KP:(q + 1) * KP, :], yv[:, q * KP:(q + 1) * KP, :], r=["yT"], w=[(yb.k, q)])
    wb = [S.sb([128, KC, 128], BF16) for _ in range(2)]
    hb = [S.sb([128, T], F32) for _ in range(2)]
    pb = [S.ps([128, 512], F32) for _ in range(4)]
    ip = 0
    for nch in range(32):
        w = wb[nch % 2]
        h = hb[nch % 2]
        load_cast(S, w, wv[:, :, nch * 128:(nch + 1) * 128], KC, 128, 2048)
        S.dma("sp", h[:], hT_in[nch * 128:(nch + 1) * 128, :], r=["hT"], w=[h.k])
        for hf in range(T // 512):
            p = pb[ip % 4]
            ip += 1
            for kc in range(KC):
                S.op("pe", "matmul", p[:], lhsT=w[:, kc, :], rhs=yb[:, kc, hf * 512:(hf + 1) * 512], start=(kc == 0),
                     stop=(kc == KC - 1), r=[w.k, (yb.k, kc // KP)], w=[p.k])
            S.op("dve", "tensor_tensor", out=h[:, hf * 512:(hf + 1) * 512], in0=p[:], in1=h[:, hf * 512:(hf + 1) * 512],
                 op=ALU.add, r=[p.k, h.k], w=[h.k])
        S.dma("act", hT_out[nch * 128:(nch + 1) * 128, :], h[:], r=[h.k], w=[S.uk()])


def load_cast(S, dst, src, nk, ncols, selems=4096):
    if not hasattr(S, "stg") or S.stg_es is not S.es or S.stg_n != selems:
        S.stg = [S.sb([128, selems], F32) for _ in range(2)]
        S.stg_es = S.es
        S.stg_n = selems
        S.stg_i = 0
    per = max(1, selems // ncols)
    for k0 in range(0, nk, per):
        k1 = min(nk, k0 + per)
        st = S.stg[S.stg_i % 2]
        S.stg_i += 1
        sv = st[:, 0:(k1 - k0) * ncols].rearrange("p (k n) -> p k n", n=ncols)
        S.dma("sp", sv, src[:, k0:k1, :], w=[st.k])
        S.op("act", "activation", out=dst[:, k0:k1, 0:ncols], in_=sv, func=AF.Copy, r=[st.k], w=[dst.k])


def new_nc():
    return bass.Bass("TRN2", target_bir_lowering=False)


def load_small(S, dram_ap, shape, dtype=F32, key=None, q="sp"):
    b = S.sb(shape, dtype)
    S.dma(q, b[:], dram_ap, w=[b.k])
    return b


def mk_eps(S, C):
    C["eps"] = S.sb([128, 1], F32, "eps")
    S.op("dve", "memset", C["eps"][:], EPS, w=["eps"])


def build_norm_only():
    nc = new_nc()
    hT = nc.dram_tensor("hT", [D, TPC], F32, kind="ExternalInput").ap()
    gw_d = nc.dram_tensor("gw", [128, 32], F32, kind="ExternalInput").ap()
    uT = nc.dram_tensor("uT", [D, TPC], BF16, kind="ExternalOutput").ap()
    with ExitStack() as es:
        es.enter_context(nc.Block())
        S = Sched(nc, es)
        C = mk_consts(S)
        mk_eps(S, C)
        gw = load_small(S, gw_d, [128, 32])
        emit_norm(S, C, hT, gw, uT, 0)
        S.finish()
    return nc


def build_outproj_norm(KC, do_norm=True):
    nc = new_nc()
    yT = nc.dram_tensor("yT", [KC * 128, TPC], BF16, kind="ExternalInput").ap()
    w_out = nc.dram_tensor("w_out", [KC * 128, D], F32, kind="ExternalInput").ap()
    hT = nc.dram_tensor("hT", [D, TPC], F32, kind="ExternalInput").ap()
    hT2 = nc.dram_tensor("hT2", [D, TPC], F32, kind="ExternalOutput").ap()
    if do_norm:
        gw_d = nc.dram_tensor("gw", [128, 32], F32, kind="ExternalInput").ap()
        uT = nc.dram_tensor("uT", [D, TPC], BF16, kind="ExternalOutput").ap()
    with ExitStack() as es:
        es.enter_context(nc.Block())
        S = Sched(nc, es)
        with ExitStack() as es2:
            S.es = es2
            emit_outproj(S, yT, KC, w_out, hT, hT2)
        S.es = es
        S.barrier()
        if do_norm:
            C = mk_consts(S)
            mk_eps(S, C)
            gw = load_small(S, gw_d, [128, 32])
            emit_norm(S, C, hT2, gw, uT, 0)
        S.finish()
    return nc


_TIMES = []


def run(nc, in_maps):
    if os.environ.get("KTRACE"):
        res = run_bass_kernel_spmd(nc, in_maps, core_ids=list(range(NCORES)), trace=True)
        _TIMES.append(res.exec_time_ns)
        print("KTRACE exec_time_ns", res.exec_time_ns, flush=True)
        return res.results
    res = run_bass_kernel_spmd(nc, in_maps, core_ids=list(range(NCORES)))
    return res.results


def lay_gw(g):
    return np.ascontiguousarray(g.reshape(32, 128).T)


def bc3(ap, n, d):
    return ap.unsqueeze(2).to_broadcast([128, n, d])


def v3(ap, d):
    return ap.rearrange("p (r d) -> p r d", d=d)


def emit_ssd_inproj(S, C, uT, wg, cw, cb, dtb, nA, xbcT, zs, dta, NT=SEQ // 512):
    uv = uT.rearrange("(kc p) t -> p kc t", p=128)
    wv = wg.rearrange("(kc p) n -> p kc n", p=128)
    TT = 512
    ub = [S.sb([128, 32, TT], BF16) for _ in range(2)]
    wb = [S.sb([128, 32, 512], BF16) for _ in range(2)]
    xr = [S.sb([128, TT + 3], F32) for _ in range(4)]
    acc = [S.sb([128, TT], F32) for _ in range(2)]
    ob = [S.sb([128, TT], BF16) for _ in range(2)]
    zt = [S.sb([128, 512], F32) for _ in range(2)]
    dt1 = S.sb([128, 16], F32)
    dto = [S.sb([128, 32], F32) for _ in range(2)]
    pb = [S.ps([128, 512], F32) for _ in range(4)]
    blocks = [(0, 512, "cm"), (512, 512, "cm"), (1024, 256, "cm"), (2304, 16, "dt")]
    iu = 0
    ip = 0
    io = 0
    for bi, (c0, ncol, kind) in enumerate(blocks):
        w = wb[bi % 2]
        load_cast(S, w, wv[:, :, c0:c0 + ncol], 32, ncol)
        for tt in range(NT):
            u = ub[iu % 2]
            iu += 1
            S.dma("sp", u[:], uv[:, :, tt * TT:(tt + 1) * TT], r=["uT"], w=[u.k])
            if kind == "cm":
                for j in range(ncol // 128):
                    ch = c0 // 128 + j
                    p = pb[ip % 4]
                    ip += 1
                    for kc in range(32):
                        S.op("pe", "matmul", p[:], lhsT=w[:, kc, j * 128:(j + 1) * 128], rhs=u[:, kc, :],
                             start=(kc == 0), stop=(kc == 31), r=[w.k, u.k], w=[p.k])
                    x = xr[j]
                    if tt == 0:
                        S.op("dve", "memset", x[:, 0:3], 0.0, w=[x.k])
                    S.op("act", "activation", out=x[:, 3:TT + 3], in_=p[:], func=AF.Copy, r=[p.k], w=[x.k])
                    a = acc[io % 2]
                    o = ob[io % 2]
                    io += 1
                    S.op("act", "activation", out=a[:], in_=x[:, 3:TT + 3], func=AF.Identity,
                         scale=cw[:, ch, 3:4], bias=cb[:, ch:ch + 1], r=[x.k, cw.k, cb.k], w=[a.k])
                    for k in (2, 1, 0):
                        S.op("dve", "scalar_tensor_tensor", out=a[:], in0=x[:, k:k + TT], scalar=cw[:, ch, k:k + 1],
                             in1=a[:], op0=ALU.mult, op1=ALU.add, r=[x.k, a.k, cw.k], w=[a.k])
                    S.op("act", "activation", out=o[:], in_=a[:], func=AF.Silu, r=[a.k], w=[o.k])
                    S.op("dve", "tensor_copy", out=x[:, 0:3], in_=x[:, TT:TT + 3], r=[x.k], w=[x.k])
                    S.dma("act", xbcT[ch * 128:(ch + 1) * 128, tt * TT:(tt + 1) * TT], o[:], r=[o.k], w=[S.uk()])
            elif kind == "z":
                zc0 = c0 - 1280
                for sub in range(4):
                    p = pb[ip % 4]
                    ip += 1
                    for kc in range(32):
                        S.op("pe", "matmul", p[:], lhsT=u[:, kc, sub * 128:(sub + 1) * 128], rhs=w[:, kc, :],
                             start=(kc == 0), stop=(kc == 31), r=[w.k, u.k], w=[p.k])
                    z = zt[io % 2]
                    io += 1
                    S.op("act", "activation", out=z[:], in_=p[:], func=AF.Silu, r=[p.k], w=[z.k])
                    t0 = tt * TT + sub * 128
                    S.dma("act", zs[t0:t0 + 128, zc0:zc0 + 512], z[:], r=[z.k], w=[S.uk()])
            else:
                for sub in range(4):
                    p = pb[ip % 4]
                    ip += 1
                    for kc in range(32):
                        S.op("pe", "matmul", p[:, 0:16], lhsT=u[:, kc, sub * 128:(sub + 1) * 128], rhs=w[:, kc, 0:16],
                             start=(kc == 0), stop=(kc == 31), r=[w.k, u.k], w=[p.k])
                    d = dto[io % 2]
                    io += 1
                    S.op("dve", "tensor_tensor", out=dt1[:], in0=p[:, 0:16], in1=dtb[:], op=ALU.add,
                         r=[p.k, dtb.k], w=[dt1.k])
                    S.op("act", "activation", out=dt1[:], in_=dt1[:], func=AF.Exp, r=[dt1.k], w=[dt1.k])
                    S.op("act", "activation", out=d[:, 0:16], in_=dt1[:], func=AF.Ln, bias=C["ones_f"][:, 0:1],
                         r=[dt1.k, "ones_f"], w=[d.k])
                    S.op("dve", "tensor_tensor", out=d[:, 16:32], in0=d[:, 0:16], in1=nA[:], op=ALU.mult,
                         r=[d.k, nA.k], w=[d.k])
                    t0 = tt * TT + sub * 128
                    S.dma("act", dta[t0:t0 + 128, :], d[:], r=[d.k], w=[S.uk()])


def emit_ssd_scan(S, C, xbcT, uT, wg, dta, dsk, nw, yT, NCH=SEQ // 128):
    U = C["U"]
    NEG = C["NEG"]
    ID = C["ident"]
    xv = xbcT[0:1024, :].rearrange("(j p) t -> p j t", p=128)
    bcv = xbcT[1024:1280, :].rearrange("(j p) t -> p j t", p=128)
    uv = uT.rearrange("(kc p) t -> p kc t", p=128)
    wv = wg.rearrange("(kc p) n -> p kc n", p=128)
    wz = [S.sb([128, 32, 512], BF16) for _ in range(2)]
    for blk in range(2):
        load_cast(S, wz[blk], wv[:, :, 1280 + blk * 512:1280 + (blk + 1) * 512], 32, 512, 2048)
    ubz = [S.sb([128, 32, 128], BF16) for _ in range(2)]
    xT = [S.sb([128, 8, 128], BF16) for _ in range(2)]
    bcT = [S.sb([128, 2, 128], BF16) for _ in range(2)]
    da = [S.sb([128, 32], F32) for _ in range(2)]
    zt = [S.sb([128, 1024], F32) for _ in range(2)]
    xs = S.sb([128, 1024], F32)
    Btm = S.sb([128, 128], BF16)
    acs = S.sb([128, 32], F32)
    e32 = S.sb([128, 32], F32)
    nacs = S.sb([128, 16], F32)
    dte = S.sb([128, 16], F32)
    dd = S.sb([128, 16], F32)
    xdt = S.sb([128, 1024], BF16)
    xde = S.sb([128, 1024], BF16)
    a3 = S.sb([128, 3, 16], BF16)
    r1 = S.sb([128, 16], F32)
    a_b = S.sb([128, 48, 128], BF16)
    CBm = S.sb([128, 128], F32)
    y = S.sb([128, 1024], F32)
    tmp = S.sb([128, 1024], F32)
    tb = [S.sb([128, 512], F32) for _ in range(2)]
    dec = [S.sb([128, 512], F32) for _ in range(2)]
    MT = [S.sb([128, 512], BF16) for _ in range(2)]
    hst = S.sb([128, 1024], F32)
    prev = S.sb([128, 1024], BF16)
    ss = S.sb([128, 1], F32)
    rstd = S.sb([128, 1], F32)
    yn = S.sb([128, 1024], BF16)
    yo = [S.sb([128, 4, 128], BF16) for _ in range(2)]
    bigA = [S.ps([128, 512], F32) for _ in range(2)]
    bigB = [S.ps([128, 512], F32) for _ in range(2)]
    seg = [S.ps([128, 512], F32) for _ in range(2)]
    misc = S.ps([128, 512], F32)
    pout = misc
    pz = S.ps([128, 512], F32)

    def emit_z(c):
        u_ = ubz[c % 2]
        zz = zt[c % 2]
        S.dma("sp", u_[:], uv[:, :, c * 128:(c + 1) * 128], r=["uT"], w=[u_.k])
        for blk in range(2):
            for kc in range(32):
                S.op("pe", "matmul", pz[:], lhsT=u_[:, kc, :], rhs=wz[blk][:, kc, :], start=(kc == 0), stop=(kc == 31),
                     r=[u_.k, wz[blk].k], w=[pz.k])
            S.op("act", "activation", out=zz[:, blk * 512:(blk + 1) * 512], in_=pz[:], func=AF.Silu, r=[pz.k],
                 w=[zz.k])

    emit_z(0)
    S.op("dve", "memset", hst[:], 0.0, w=[hst.k])
    S.op("dve", "memset", prev[:], 0.0, w=[prev.k])
    for c in range(NCH):
        t0 = c * 128
        x_ = xT[c % 2]
        bc = bcT[c % 2]
        d_ = da[c % 2]
        z_ = zt[c % 2]
        S.dma("sp", x_[:], xv[:, :, t0:t0 + 128], r=["xbcT"], w=[x_.k])
        S.dma("sp", bc[:], bcv[:, :, t0:t0 + 128], r=["xbcT"], w=[bc.k])
        S.dma("sp", d_[:], dta[t0:t0 + 128, :], r=["dta"], w=[d_.k])
        BT = bc[:, 0, :]
        CT = bc[:, 1, :]
        for j in range(8):
            S.op("pe", "matmul", bigA[j // 4][:, (j % 4) * 128:(j % 4 + 1) * 128], lhsT=x_[:, j, :], rhs=ID[:],
                 start=True, stop=True, r=[x_.k, "ident"], w=[bigA[j // 4].k])
        for hf in range(2):
            S.op("act", "activation", out=xs[:, hf * 512:(hf + 1) * 512], in_=bigA[hf][:], func=AF.Copy,
                 r=[bigA[hf].k], w=[xs.k])
        S.op("pe", "matmul", misc[:, 128:256], lhsT=BT, rhs=ID[:], start=True, stop=True, r=[bc.k, "ident"],
             w=[misc.k])
        S.op("act", "activation", out=Btm[:], in_=misc[:, 128:256], func=AF.Copy, r=[misc.k], w=[Btm.k])
        S.op("dve", "tensor_copy", out=a3[:, 0, :], in_=d_[:, 16:32], r=[d_.k], w=[a3.k])
        S.op("dve", "tensor_tensor", out=r1[:], in0=d_[:, 16:32], in1=a3[:, 0, :], op=ALU.subtract, r=[d_.k, a3.k],
             w=[r1.k])
        S.op("dve", "tensor_copy", out=a3[:, 1, :], in_=r1[:], r=[r1.k], w=[a3.k])
        S.op("dve", "tensor_tensor", out=r1[:], in0=r1[:], in1=a3[:, 1, :], op=ALU.subtract, r=[r1.k, a3.k], w=[r1.k])
        S.op("dve", "tensor_copy", out=a3[:, 2, :], in_=r1[:], r=[r1.k], w=[a3.k])
        for i3 in range(3):
            S.op("pe", "matmul", misc[:, 0:16], lhsT=C["U_b"][:], rhs=a3[:, i3, :], start=(i3 == 0), stop=(i3 == 2),
                 r=[a3.k, "U_b"], w=[misc.k])
        for i3 in range(3):
            S.op("pe", "matmul", misc[:, 16:32], lhsT=C["ones_b"][:], rhs=a3[:, i3, :], start=(i3 == 0),
                 stop=(i3 == 2), r=[a3.k, "ones_b"], w=[misc.k])
        S.op("dve", "tensor_copy", out=acs[:], in_=misc[:, 0:32], r=[misc.k], w=[acs.k])
        S.op("act", "activation", out=e32[:], in_=acs[:], func=AF.Exp, r=[acs.k], w=[e32.k])
        S.op("dve", "tensor_scalar", out=nacs[:], in0=acs[:, 0:16], scalar1=-1.0, scalar2=None, op0=ALU.mult,
             r=[acs.k], w=[nacs.k])
        S.op("dve", "tensor_tensor", out=dte[:], in0=acs[:, 16:32], in1=acs[:, 0:16], op=ALU.subtract, r=[acs.k],
             w=[dte.k])
        S.op("act", "activation", out=dte[:], in_=dte[:], func=AF.Exp, r=[dte.k], w=[dte.k])
        S.op("dve", "tensor_tensor", out=dd[:], in0=dte[:], in1=d_[:, 0:16], op=ALU.mult, r=[dte.k, d_.k], w=[dd.k])
        S.op("dve", "tensor_tensor", out=v3(xdt[:], 64), in0=v3(xs[:], 64), in1=bc3(d_[:, 0:16], 16, 64), op=ALU.mult,
             r=[xs.k, d_.k], w=[xdt.k])
        S.op("dve", "tensor_tensor", out=v3(xde[:], 64), in0=v3(xs[:], 64), in1=bc3(dd[:], 16, 64), op=ALU.mult,
             r=[xs.k, dd.k], w=[xde.k])
        S.op("dve", "tensor_copy", out=a_b[:], in_=bc3(a3[:].rearrange("p i r -> p (i r)"), 48, 128), r=[a3.k],
             w=[a_b.k])
        S.op("pe", "matmul", misc[:, 256:384], lhsT=BT, rhs=CT, start=True, stop=True, r=[bc.k], w=[misc.k])
        S.op("dve", "tensor_tensor", out=CBm[:], in0=misc[:, 256:384], in1=U[:], op=ALU.mult, r=[misc.k, "U"],
             w=[CBm.k])
        for hf in range(2):
            S.op("pe", "matmul", bigB[hf][:], lhsT=CT, rhs=prev[:, hf * 512:(hf + 1) * 512], start=True, stop=True,
                 r=[bc.k, prev.k], w=[bigB[hf].k])
            S.op("dve", "tensor_tensor", out=v3(y[:, hf * 512:(hf + 1) * 512], 64), in0=v3(bigB[hf][:], 64),
                 in1=bc3(e32[:, hf * 8:hf * 8 + 8], 8, 64), op=ALU.mult, r=[bigB[hf].k, e32.k], w=[y.k])
        for hg in range(4):
            sg = seg[hg % 2]
            t_ = tb[hg % 2]
            de = dec[hg % 2]
            m_ = MT[hg % 2]
            for q in range(4):
                r_ = hg * 4 + q
                for i3 in range(3):
                    S.op("pe", "matmul", sg[:, q * 128:(q + 1) * 128], lhsT=a_b[:, i3 * 16 + r_, :], rhs=C["U_b"][:],
                         start=(i3 == 0), stop=(i3 == 2), r=[a_b.k, "U_b"], w=[sg.k])
            for q in range(4):
                r_ = hg * 4 + q
                S.op("dve", "scalar_tensor_tensor", out=t_[:, q * 128:(q + 1) * 128], in0=sg[:, q * 128:(q + 1) * 128],
                     scalar=nacs[:, r_:r_ + 1], in1=NEG[:], op0=ALU.add, op1=ALU.add, r=[sg.k, nacs.k, "NEG"],
                     w=[t_.k])
            S.op("act", "activation", out=de[:], in_=t_[:], func=AF.Exp, r=[t_.k], w=[de.k])
            S.op("dve", "tensor_tensor", out=v3(m_[:], 128), in0=v3(de[:], 128),
                 in1=CBm[:].unsqueeze(1).to_broadcast([128, 4, 128]), op=ALU.mult, r=[de.k, CBm.k], w=[m_.k])
            for q in range(4):
                r_ = hg * 4 + q
                S.op("pe", "matmul", bigA[r_ // 8][:, (r_ % 8) * 64:(r_ % 8 + 1) * 64], lhsT=m_[:, q * 128:(q + 1) * 128],
                     rhs=xdt[:, r_ * 64:(r_ + 1) * 64], start=True, stop=True, r=[m_.k, xdt.k], w=[bigA[r_ // 8].k])
        for hf in range(2):
            S.op("dve", "tensor_tensor", out=y[:, hf * 512:(hf + 1) * 512], in0=bigA[hf][:],
                 in1=y[:, hf * 512:(hf + 1) * 512], op=ALU.add, r=[bigA[hf].k, y.k], w=[y.k])
        S.op("dve", "tensor_tensor", out=v3(tmp[:], 64), in0=v3(xs[:], 64), in1=bc3(dsk[:], 16, 64), op=ALU.mult,
             r=[xs.k, dsk.k], w=[tmp.k])
        S.op("dve", "tensor_tensor", out=y[:], in0=y[:], in1=tmp[:], op=ALU.add, r=[y.k, tmp.k], w=[y.k])
        for hf in range(2):
            S.op("pe", "matmul", bigB[hf][:], lhsT=Btm[:], rhs=xde[:, hf * 512:(hf + 1) * 512], start=True, stop=True,
                 r=[Btm.k, xde.k], w=[bigB[hf].k])
        S.op("dve", "tensor_tensor", out=v3(hst[:], 64), in0=v3(hst[:], 64), in1=bc3(e32[:, 16:32], 16, 64),
             op=ALU.mult, r=[hst.k, e32.k], w=[hst.k])
        for hf in range(2):
            S.op("dve", "tensor_tensor", out=hst[:, hf * 512:(hf + 1) * 512], in0=bigB[hf][:],
                 in1=hst[:, hf * 512:(hf + 1) * 512], op=ALU.add, r=[bigB[hf].k, hst.k], w=[hst.k])
        S.op("act", "activation", out=prev[:], in_=hst[:], func=AF.Copy, r=[hst.k], w=[prev.k])
        S.op("dve", "tensor_tensor", out=y[:], in0=y[:], in1=z_[:], op=ALU.mult, r=[y.k, z_.k], w=[y.k])
        S.op("dve", "memset", ss[:], 0.0, w=[ss.k])
        S.op("act", "activation", out=tmp[:], in_=y[:], func=AF.Square, accum_out=ss[:], r=[y.k, ss.k], w=[tmp.k, ss.k])
        S.op("act", "activation", out=rstd[:], in_=ss[:], func=AF.Sqrt, scale=1.0 / 1024, bias=C["eps"][:],
             r=[ss.k, "eps"], w=[rstd.k])
        S.op("dve", "reciprocal", out=rstd[:], in_=rstd[:], r=[rstd.k], w=[rstd.k])
        S.op("dve", "scalar_tensor_tensor", out=yn[:], in0=y[:], scalar=rstd[:, 0:1], in1=nw[:], op0=ALU.mult,
             op1=ALU.mult, r=[y.k, rstd.k, nw.k], w=[yn.k])
        for q in range(2):
            o = yo[q]
            for j in range(4):
                S.op("pe", "matmul", pout[:, j * 128:(j + 1) * 128], lhsT=yn[:, (q * 4 + j) * 128:(q * 4 + j + 1) * 128],
                     rhs=ID[:], start=True, stop=True, r=[yn.k, "ident"], w=[pout.k])
            S.op("act", "activation", out=o[:], in_=v3(pout[:], 128), func=AF.Copy, r=[pout.k], w=[o.k])
            S.dma("act", yT[q * 512:(q + 1) * 512, :].rearrange("(j p) t -> p j t", p=128)[:, :, t0:t0 + 128], o[:],
                  r=[o.k], w=[S.uk()])
        if c + 1 < NCH:
            emit_z(c + 1)


def load_consts(S, C, cst_d):
    cst = S.sb([128, 384], F32, "cst_sb")
    S.dma("sp", cst[:], cst_d, w=["cst"])
    C["U"] = B(cst.t[:, 0:128], "cst")
    C["NEG"] = B(cst.t[:, 128:256], "cst")
    C["cst_sb"] = cst
    idb = S.sb([128, 128], BF16, "ident")
    S.op("dve", "tensor_copy", out=idb[:], in_=cst[:, 256:384], r=["cst"], w=["ident"])
    C["ident"] = idb
    ub_ = S.sb([128, 128], BF16, "U_b")
    S.op("dve", "tensor_copy", out=ub_[:], in_=cst[:, 0:128], r=["cst"], w=["U_b"])
    C["U_b"] = ub_
    S.lastw["U"] = S.lastw["cst"]
    S.lastw["NEG"] = S.lastw["cst"]


def host_consts():
    j = np.arange(128)
    U = (j[:, None] <= j[None, :]).astype(np.float32)
    NEG = (U - 1.0) * 1e4
    I = np.eye(128, dtype=np.float32)
    return np.ascontiguousarray(np.concatenate([U, NEG, I], 1))


def build_ssd(NCH=SEQ // 128, phases="is"):
    nc = new_nc()
    uT = nc.dram_tensor("uT", [D, SEQ], BF16, kind="ExternalInput").ap()
    wg = nc.dram_tensor("wg", [D, 2320], F32, kind="ExternalInput").ap()
    cw_d = nc.dram_tensor("cw", [128, 10, 4], F32, kind="ExternalInput").ap()
    cb_d = nc.dram_tensor("cb", [128, 10], F32, kind="ExternalInput").ap()
    sm_d = nc.dram_tensor("sm", [128, 48], F32, kind="ExternalInput").ap()
    nw_d = nc.dram_tensor("nw", [128, 1024], F32, kind="ExternalInput").ap()
    cst_d = nc.dram_tensor("cst", [128, 384], F32, kind="ExternalInput").ap()
    yT = nc.dram_tensor("yT", [1024, SEQ], BF16, kind="ExternalOutput").ap()
    xbcT = nc.dram_tensor("xbcT", [1280, SEQ], BF16).ap()
    zs = nc.dram_tensor("zs", [SEQ, 1024], F32).ap()
    dta = nc.dram_tensor("dta", [SEQ, 32], F32).ap()
    with ExitStack() as es:
        es.enter_context(nc.Block())
        S = Sched(nc, es)
        C = mk_consts(S)
        mk_eps(S, C)
        load_consts(S, C, cst_d)
        cw = load_small(S, cw_d, [128, 10, 4])
        cb = load_small(S, cb_d, [128, 10])
        sm = load_small(S, sm_d, [128, 48])
        nw = load_small(S, nw_d, [128, 1024])
        nA = S.sb([128, 16], F32)
        S.op("act", "activation", out=nA[:], in_=sm[:, 16:32], func=AF.Exp, r=[sm.k], w=[nA.k])
        S.op("dve", "tensor_scalar", out=nA[:], in0=nA[:], scalar1=-1.0, scalar2=None, op0=ALU.mult, r=[nA.k], w=[nA.k])
        dtb = B(sm.t[:, 0:16], sm.k)
        dsk = B(sm.t[:, 32:48], sm.k)
        with ExitStack() as es2:
            S.es = es2
            if "i" in phases:
                emit_ssd_inproj(S, C, uT, wg, cw, cb, dtb, nA, xbcT, zs, dta, NT=NCH // 4)
        S.es = es
        S.barrier()
        with ExitStack() as es2:
            S.es = es2
            if "s" in phases:
                emit_ssd_scan(S, C, xbcT, uT, wg, dta, dsk, nw, yT, NCH)
        S.es = es
        S.finish()
    return nc


def ssd_core_inputs(g, p, pre):
    w_in = p[pre + "_w_in"]
    DI = 8192
    xs = slice(DI + g * 1024, DI + (g + 1) * 1024)
    bs = slice(2 * DI + g * 128, 2 * DI + (g + 1) * 128)
    cs = slice(2 * DI + 1024 + g * 128, 2 * DI + 1024 + (g + 1) * 128)
    z_ = slice(g * 1024, (g + 1) * 1024)
    dts = slice(2 * DI + 2048 + g * 16, 2 * DI + 2048 + (g + 1) * 16)
    wg = np.ascontiguousarray(np.concatenate([w_in[:, xs], w_in[:, bs], w_in[:, cs], w_in[:, z_], w_in[:, dts]], 1))
    ccols = np.concatenate([np.arange(g * 1024, (g + 1) * 1024), 8192 + np.arange(g * 128, (g + 1) * 128),
                            8192 + 1024 + np.arange(g * 128, (g + 1) * 128)])
    cwf = p[pre + "_conv_w"][:, ccols]
    cw = np.ascontiguousarray(cwf.T.reshape(10, 128, 4).transpose(1, 0, 2))
    cb = np.ascontiguousarray(p[pre + "_conv_b"][ccols].reshape(10, 128).T)
    hs = slice(g * 16, (g + 1) * 16)
    sm = np.concatenate([p[pre + "_dt_bias"][hs], p[pre + "_a_log"][hs], p[pre + "_d"][hs]])[None, :]
    sm = np.ascontiguousarray(np.broadcast_to(sm, (128, 48))).astype(np.float32)
    nw = np.ascontiguousarray(np.broadcast_to(p[pre + "_norm_w"][None, g * 1024:(g + 1) * 1024], (128, 1024)))
    return dict(wg=wg, cw=cw, cb=cb, sm=sm, nw=nw, cst=host_consts())


def emit_pool(S, C, uTh, w_in, w_grp, bgs, rc, gT):
    uv = uTh.rearrange("(kc p) t -> p kc t", p=128)
    wv = w_in.rearrange("(kc p) n -> p kc n", p=128)
    TH = TPC + 16
    ub = S.sb([128, 32, TH], BF16)
    S.dma("sp", ub[:], uv, r=["uTh"], w=[ub.k])
    dT = S.sb([128, 16, TPC], BF16)
    wb = [S.sb([128, 32, 256], BF16) for _ in range(2)]
    gb = [S.sb([128, 16, 512], BF16) for _ in range(1)]
    vt = [S.sb([128, TH], F32) for _ in range(2)]
    sb_ = [S.sb([128, TH], F32) for _ in range(2)]
    tmp = S.sb([128, TPC], F32)
    sz = [S.sb([128, 512], F32) for _ in range(2)]
    mx = [S.sb([128, 512], F32) for _ in range(2)]
    go = [S.sb([128, 512], BF16) for _ in range(2)]
    for b_ in sb_:
        S.op("dve", "memset", b_[:], 0.0, w=[b_.k])
    bs = S.sb([128, 64], F32)
    S.op("dve", "tensor_tensor", out=bs[:], in0=bgs[:, :, 0], in1=bgs[:, :, 1], op=ALU.mult, r=[bgs.k], w=[bs.k])
    ph = S.ps([128, 512], F32)
    pv = [S.ps([128, 512], F32) for _ in range(2)]
    pm = [S.ps([128, 512], F32) for _ in range(2)]
    pz = [S.ps([128, 512], F32) for _ in range(2)]
    iw = 0
    ig = 0
    io = 0
    for g in range(4):
        for blk in range(8):
            w = wb[iw % 2]
            iw += 1
            c0 = g * 2048 + blk * 256
            load_cast(S, w, wv[:, :, c0:c0 + 256], 32, 256, 2048)
            for j in range(2):
                ic = blk * 2 + j
                lw = lambda kc: w[:, kc, j * 128:(j + 1) * 128]
                for kc in range(32):
                    S.op("pe", "matmul", ph[:, 0:16], lhsT=lw(kc), rhs=ub[:, kc, 0:16], start=(kc == 0), stop=(kc == 31),
                         r=[w.k, ub.k], w=[ph.k])
                for hf in range(2):
                    for kc in range(32):
                        S.op("pe", "matmul", pv[hf][:], lhsT=lw(kc), rhs=ub[:, kc, 16 + hf * 512:16 + (hf + 1) * 512],
                             start=(kc == 0), stop=(kc == 31), r=[w.k, ub.k], w=[pv[hf].k])
                v = vt[ic % 2]
                S.op("act", "activation", out=v[:, 0:16], in_=ph[:, 0:16], func=AF.Copy, r=[ph.k], w=[v.k])
                for hf in range(2):
                    S.op("act", "activation", out=v[:, 16 + hf * 512:16 + (hf + 1) * 512], in_=pv[hf][:], func=AF.Copy,
                         r=[pv[hf].k], w=[v.k])
                cur = v
                for k in range(g + 1):
                    sh = 2 ** k
                    nx = sb_[k % 2]
                    S.op("dve", "tensor_tensor", out=nx[:, sh:TH], in0=cur[:, sh:TH], in1=cur[:, 0:TH - sh], op=ALU.add,
                         r=[cur.k], w=[nx.k])
                    cur = nx
                S.op("dve", "tensor_tensor", out=tmp[:, 0:16], in0=cur[:, 16:32], in1=rc[:, g, :], op=ALU.mult,
                     r=[cur.k, rc.k], w=[tmp.k])
                S.op("dve", "tensor_scalar", out=tmp[:, 16:TPC], in0=cur[:, 32:TH], scalar1=1.0 / 2 ** (g + 1),
                     scalar2=None, op0=ALU.mult, r=[cur.k], w=[tmp.k])
                S.op("dve", "tensor_tensor", out=dT[:, ic, :], in0=tmp[:], in1=v[:, 16:TH], op=ALU.subtract,
                     r=[tmp.k, v.k], w=[dT.k])
        wgv = w_grp[g].rearrange("(ic p) n -> p ic n", p=128)
        for ob in range(4):
            gw_ = gb[0]
            ig += 1
            load_cast(S, gw_, wgv[:, :, ob * 512:(ob + 1) * 512], 16, 512, 2048)
            for zb in range(2):
                w = wb[iw % 2]
                iw += 1
                c0 = 8192 + g * 2048 + ob * 512 + zb * 256
                load_cast(S, w, wv[:, :, c0:c0 + 256], 32, 256, 2048)
                for j in range(2):
                    oc = ob * 4 + zb * 2 + j
                    ch = g * 16 + oc
                    for hf in range(2):
                        for ic in range(16):
                            S.op("pe", "matmul", pm[hf][:], lhsT=gw_[:, ic, (zb * 2 + j) * 128:(zb * 2 + j + 1) * 128],
                                 rhs=dT[:, ic, hf * 512:(hf + 1) * 512], start=(ic == 0), stop=(ic == 15),
                                 r=[gw_.k, dT.k], w=[pm[hf].k])
                        for kc in range(32):
                            S.op("pe", "matmul", pz[hf][:], lhsT=w[:, kc, j * 128:(j + 1) * 128],
                                 rhs=ub[:, kc, 16 + hf * 512:16 + (hf + 1) * 512], start=(kc == 0), stop=(kc == 31),
                                 r=[w.k, ub.k], w=[pz[hf].k])
                        s_ = sz[io % 2]
                        m_ = mx[io % 2]
                        o_ = go[io % 2]
                        io += 1
                        S.op("act", "activation", out=s_[:], in_=pz[hf][:], func=AF.Silu, r=[pz[hf].k], w=[s_.k])
                        S.op("act", "activation", out=m_[:], in_=pm[hf][:], func=AF.Identity, scale=bgs[:, ch, 1:2],
                             bias=bs[:, ch:ch + 1], r=[pm[hf].k, bgs.k, bs.k], w=[m_.k])
                        S.op("dve", "tensor_tensor", out=o_[:], in0=m_[:], in1=s_[:], op=ALU.mult, r=[m_.k, s_.k],
                             w=[o_.k])
                        S.dma("act", gT[ch * 128:(ch + 1) * 128, hf * 512:(hf + 1) * 512], o_[:], r=[o_.k], w=[S.uk()])


def build_pool():
    nc = new_nc()
    uTh = nc.dram_tensor("uTh", [D, TPC + 16], BF16, kind="ExternalInput").ap()
    w_in = nc.dram_tensor("w_in", [D, 16384], F32, kind="ExternalInput").ap()
    w_grp = nc.dram_tensor("w_grp", [4, 2048, 2048], F32, kind="ExternalInput").ap()
    bgs_d = nc.dram_tensor("bgs", [128, 64, 2], F32, kind="ExternalInput").ap()
    rc_d = nc.dram_tensor("rc", [128, 4, 16], F32, kind="ExternalInput").ap()
    w_out = nc.dram_tensor("w_out", [8192, D], F32, kind="ExternalInput").ap()
    hT = nc.dram_tensor("hT", [D, TPC], F32, kind="ExternalInput").ap()
    gw_d = nc.dram_tensor("gw", [128, 32], F32, kind="ExternalInput").ap()
    hT2 = nc.dram_tensor("hT2", [D, TPC], F32, kind="ExternalOutput").ap()
    uT = nc.dram_tensor("uT", [D, TPC], BF16, kind="ExternalOutput").ap()
    gT = nc.dram_tensor("gT", [8192, TPC], BF16).ap()
    with ExitStack() as es:
        es.enter_context(nc.Block())
        S = Sched(nc, es)
        C = mk_consts(S)
        mk_eps(S, C)
        gw = load_small(S, gw_d, [128, 32])
        with ExitStack() as es2:
            S.es = es2
            bgs = load_small(S, bgs_d, [128, 64, 2])
            rc = load_small(S, rc_d, [128, 4, 16])
            emit_pool(S, C, uTh, w_in, w_grp, bgs, rc, gT)
        S.es = es
        S.barrier()
        with ExitStack() as es2:
            S.es = es2
            emit_outproj(S, gT, 64, w_out, hT, hT2)
        S.es = es
        S.barrier()
        with ExitStack() as es2:
            S.es = es2
            emit_norm(S, C, hT2, gw, uT, 0)
        S.es = es
        S.finish()
    return nc


def pool_core_inputs(c, p):
    bgs = np.stack([p["pool1_b_grp"].reshape(-1), p["pool1_scale"]], -1)
    bgs = np.ascontiguousarray(bgs.reshape(64, 128, 2).transpose(1, 0, 2))
    pos = c * TPC + np.arange(16)
    rc = np.stack([1.0 / np.minimum(pos + 1, w) for w in (2, 4, 8, 16)], 0).astype(np.float32)
    rc = np.ascontiguousarray(np.broadcast_to(rc[None], (128, 4, 16)))
    return dict(bgs=bgs.astype(np.float32), rc=rc)


NEGM = 30000.0


def emit_moba_inproj(S, C, uT, wq, qkw, qT, kT, gsT, vtm, NT=SEQ // 512):
    uv = uT.rearrange("(kc p) t -> p kc t", p=128)
    wv = wq.rearrange("(kc p) n -> p kc n", p=128)
    TT = 512
    ub = [S.sb([128, 32, TT], BF16) for _ in range(2)]
    wb = [S.sb([128, 32, 512], BF16) for _ in range(2)]
    sq = [S.sb([128, TT], F32) for _ in range(2)]
    rs = [S.sb([128, TT], F32) for _ in range(2)]
    shi = [S.sb([128, TT], BF16) for _ in range(2)]
    slo = [S.sb([128, TT], BF16) for _ in range(2)]
    ob = [S.sb([128, TT], BF16) for _ in range(2)]
    pb = [S.ps([128, 512], F32) for _ in range(4)]
    pn = [S.ps([128, 512], F32) for _ in range(2)]
    iu = ip = io = 0
    for bi in range(4):
        w = wb[bi % 2]
        load_cast(S, w, wv[:, :, bi * 512:(bi + 1) * 512], 32, 512)
        for tt in range(NT):
            u = ub[iu % 2]
            iu += 1
            S.dma("sp", u[:], uv[:, :, tt * TT:(tt + 1) * TT], r=["uT"], w=[u.k])
            if bi < 3:
                for hh in range(4):
                    p = pb[ip % 4]
                    ip += 1
                    for kc in range(32):
                        S.op("pe", "matmul", p[:], lhsT=w[:, kc, hh * 128:(hh + 1) * 128], rhs=u[:, kc, :],
                             start=(kc == 0), stop=(kc == 31), r=[w.k, u.k], w=[p.k])
                    o = ob[io % 2]
                    if bi < 2:
                        s_ = sq[io % 2]
                        r_ = rs[io % 2]
                        n_ = pn[io % 2]
                        S.op("act", "activation", out=s_[:], in_=p[:], func=AF.Square, r=[p.k], w=[s_.k])
                        hi_ = shi[io % 2]
                        lo_ = slo[io % 2]
                        S.op("dve", "tensor_copy", out=hi_[:], in_=s_[:], r=[s_.k], w=[hi_.k])
                        S.op("dve", "tensor_tensor", out=lo_[:], in0=s_[:], in1=hi_[:], op=ALU.subtract,
                             r=[s_.k, hi_.k], w=[lo_.k])
                        S.op("pe", "matmul", n_[:], lhsT=C["ones_b"][:], rhs=hi_[:], start=True, stop=False,
                             r=[hi_.k, "ones_b"], w=[n_.k])
                        S.op("pe", "matmul", n_[:], lhsT=C["ones_b"][:], rhs=lo_[:], start=False, stop=True,
                             r=[lo_.k, "ones_b"], w=[n_.k])
                        S.op("act", "activation", out=r_[:], in_=n_[:], func=AF.Sqrt, scale=1.0 / 128, bias=C["eps"][:],
                             r=[n_.k, "eps"], w=[r_.k])
                        S.op("dve", "reciprocal", out=r_[:], in_=r_[:], r=[r_.k], w=[r_.k])
                        S.op("dve", "scalar_tensor_tensor", out=o[:], in0=p[:], scalar=qkw[:, bi:bi + 1], in1=r_[:],
                             op0=ALU.mult, op1=ALU.mult, r=[p.k, r_.k, qkw.k], w=[o.k])
                        dst = (qT, kT)[bi]
                    else:
                        S.op("act", "activation", out=o[:], in_=p[:], func=AF.Silu, r=[p.k], w=[o.k])
                        dst = gsT
                    io += 1
                    S.dma("act", dst[hh * 128:(hh + 1) * 128, tt * TT:(tt + 1) * TT], o[:], r=[o.k], w=[S.uk()])
            else:
                for sub in range(4):
                    p = pb[ip % 4]
                    ip += 1
                    for kc in range(32):
                        S.op("pe", "matmul", p[:], lhsT=u[:, kc, sub * 128:(sub + 1) * 128], rhs=w[:, kc, :],
                             start=(kc == 0), stop=(kc == 31), r=[w.k, u.k], w=[p.k])
                    o = ob[io % 2]
                    io += 1
                    S.op("act", "activation", out=o[:], in_=p[:], func=AF.Copy, r=[p.k], w=[o.k])
                    t0 = tt * TT + sub * 128
                    S.dma("act", vtm[t0:t0 + 128, :], o[:], r=[o.k], w=[S.uk()])


def emit_moba_attn(S, C, qT, kT, gsT, vtm, yT, mc, IND, CAUS, NQT=SEQ // 512):
    ID = C["ident"]
    scale = 1.0 / np.sqrt(128.0)
    QTb = [S.sb([128, SEQ], BF16) for _ in range(2)]
    NSb = [S.sb([128, SEQ], BF16) for _ in range(2)]
    KT = S.sb([128, SEQ], BF16)
    V = S.sb([128, 64, 128], BF16)
    selw = S.sb([128, 128], BF16)
    S.op("dve", "memset", selw[:], 0.0, w=[selw.k])
    kmf = S.sb([128, 32], F32)
    km = S.sb([128, 4, 32], BF16)
    gsm = S.sb([128, 4, 32], F32)
    m8 = S.sb([128, 4, 8], F32)
    E = [S.sb([128, 512], BF16) for _ in range(3)]
    rden = S.sb([128, 512], F32)
    gt = [S.sb([128, 512], BF16) for _ in range(2)]
    of = S.sb([128, 512], F32)
    oo = [S.sb([128, 512], BF16) for _ in range(2)]
    pS = [S.ps([128, 512], F32) for _ in range(3)]
    pO = S.ps([128, 512], F32)
    pD = S.ps([128, 512], F32)
    pg = S.ps([128, 512], F32)
    pg2 = S.ps([128, 512], F32)
    S.op("pe", "matmul", pg2[:], lhsT=ID[:], rhs=CAUS[:, 0, :], start=True, stop=True, r=["ident", CAUS.k], w=[pg2.k])
    for hh in range(4):
        S.dma("sp", KT[:], kT[hh * 128:(hh + 1) * 128, :], r=["qkg"], w=[KT.k])
        S.op("dve", "tensor_reduce", out=kmf[:], in_=KT[:].rearrange("p (n j) -> p n j", j=256), axis=AX.X, op=ALU.add,
             r=[KT.k], w=[kmf.k])
        S.op("dve", "tensor_scalar", out=km[:, hh, :], in0=kmf[:], scalar1=1.0 / 256, scalar2=None, op0=ALU.mult,
             r=[kmf.k], w=[km.k])

    def load_q(hh):
        S.dma("sp", QTb[hh % 2][:], qT[hh * 128:(hh + 1) * 128, :], r=["qkg"], w=[QTb[hh % 2].k])

    def sel_group(hh, gi):
        QT = QTb[hh % 2]
        NS = NSb[hh % 2]
        for g in range(4):
            qt = gi * 4 + g
            S.op("pe", "matmul", pg[:, g * 32:(g + 1) * 32], lhsT=QT[:, qt * 128:(qt + 1) * 128], rhs=km[:, hh, :],
                 start=True, stop=True, r=[QT.k, km.k], w=[pg.k])
        S.op("dve", "tensor_tensor", out=gsm[:], in0=pg[:, 0:128].rearrange("p (g n) -> p g n", n=32),
             in1=mc[:, gi * 4:gi * 4 + 4, 0:32], op=ALU.add, r=[pg.k, mc.k], w=[gsm.k])
        for g in range(4):
            S.op("dve", "max", out=m8[:, g, :], in_=gsm[:, g, :], r=[gsm.k], w=[m8.k])
        sv = selw[:].rearrange("p (g n) -> p g n", n=32)
        S.op("dve", "tensor_tensor", out=sv, in0=gsm[:], in1=m8[:, :, 2:3].to_broadcast([128, 4, 32]), op=ALU.is_ge,
             r=[gsm.k, m8.k], w=[selw.k])
        S.op("dve", "tensor_tensor", out=sv, in0=sv, in1=mc[:, gi * 4:gi * 4 + 4, 32:64], op=ALU.mult,
             r=[selw.k, mc.k], w=[selw.k])
        S.op("dve", "tensor_tensor", out=sv, in0=sv, in1=mc[:, gi * 4:gi * 4 + 4, 64:96], op=ALU.add,
             r=[selw.k, mc.k], w=[selw.k])
        S.op("dve", "tensor_scalar", out=selw[:], in0=selw[:], scalar1=-1.0, scalar2=NEGM, op0=ALU.add,
             op1=ALU.mult, r=[selw.k], w=[selw.k])
        for g in range(4):
            S.op("pe", "matmul", pg2[0:32, g * 128:(g + 1) * 128], lhsT=selw[:, g * 32:(g + 1) * 32], rhs=ID[:],
                 start=True, stop=True, r=[selw.k, "ident"], w=[pg2.k])
        S.op("act", "activation", out=NS[:, gi * 512:(gi + 1) * 512], in_=pg2[:], func=AF.Copy, r=[pg2.k], w=[NS.k])

    load_q(0)
    for gi in range(NQT):
        sel_group(0, gi)
    ie = 0
    for hh in range(4):
        rows = slice(hh * 128, (hh + 1) * 128)
        QT = QTb[hh % 2]
        NS = NSb[hh % 2]
        S.dma("sp", KT[:], kT[rows, :], r=["qkg"], w=[KT.k])
        S.dma("sp", V[:], vtm[:, rows].rearrange("(jt p) d -> p jt d", p=128), r=["qkg"], w=[V.k])
        if hh + 1 < 4:
            load_q(hh + 1)
        for m in range(NQT):
            q0 = m * 512
            g_ = gt[m % 2]
            S.dma("sp", g_[:], gsT[rows, q0:q0 + 512], r=["qkg"], w=[g_.k])
            nkt = 4 * (m + 1)
            for kt in range(nkt):
                ps = pS[ie % 3]
                e_ = E[ie % 3]
                ie += 1
                diag = kt >= 4 * m
                S.op("pe", "matmul", ps[:], lhsT=KT[:, kt * 128:(kt + 1) * 128], rhs=QT[:, q0:q0 + 512], start=True,
                     stop=False, r=[KT.k, QT.k], w=[ps.k])
                S.op("pe", "matmul", ps[:], lhsT=IND[:, kt // 2, :], rhs=NS[0:32, q0:q0 + 512], start=False,
                     stop=(not diag), r=[IND.k, NS.k], w=[ps.k])
                if diag:
                    S.op("pe", "matmul", ps[:], lhsT=ID[:], rhs=CAUS[:, kt - 4 * m, :], start=False, stop=True,
                         r=["ident", CAUS.k], w=[ps.k])
                S.op("act", "activation", out=e_[:], in_=ps[:], func=AF.Exp, scale=float(scale), r=[ps.k], w=[e_.k])
                S.op("pe", "matmul", pO[:], lhsT=V[:, kt, :], rhs=e_[:], start=(kt == 0), stop=(kt == nkt - 1),
                     r=[V.k, e_.k], w=[pO.k])
                S.op("pe", "matmul", pD[:], lhsT=C["ones_b"][:], rhs=e_[:], start=(kt == 0), stop=(kt == nkt - 1),
                     r=["ones_b", e_.k], w=[pD.k])
            S.op("dve", "reciprocal", out=rden[:], in_=pD[:], r=[pD.k], w=[rden.k])
            S.op("dve", "tensor_tensor", out=of[:], in0=pO[:], in1=rden[:], op=ALU.mult, r=[pO.k, rden.k], w=[of.k])
            o_ = oo[m % 2]
            S.op("dve", "tensor_tensor", out=o_[:], in0=of[:], in1=g_[:], op=ALU.mult, r=[of.k, g_.k], w=[o_.k])
            S.dma("act", yT[rows, q0:q0 + 512], o_[:], r=[o_.k], w=[S.uk()])
            if hh + 1 < 4:
                sel_group(hh + 1, m)


def moba_host_consts():
    n = np.arange(32)
    mc = np.zeros((128, 64, 96), np.float32)
    for qt in range(64):
        qb = qt // 2
        mc[:, qt, 0:32] = np.where(n < qb, 0.0, -1e30)[None]
        mc[:, qt, 32:64] = (n < qb).astype(np.float32)[None]
        mc[:, qt, 64:96] = (n == qb).astype(np.float32)[None]
    ind = np.zeros((32, 32, 128), np.float32)
    for k in range(32):
        ind[k, k, :] = 1.0
    p = np.arange(128)[:, None]
    i = np.arange(512)[None, :]
    caus = np.stack([np.where(kk * 128 + p <= i, 0.0, -NEGM) for kk in range(4)], 1).astype(np.float32)
    return dict(mc=mc, ind=ind.astype(NPBF), caus=np.ascontiguousarray(caus).astype(NPBF))


def build_moba(NQT=SEQ // 512, phases="ia"):
    nc = new_nc()
    uT = nc.dram_tensor("uT", [D, SEQ], BF16, kind="ExternalInput").ap()
    wq = nc.dram_tensor("wq", [D, 2048], F32, kind="ExternalInput").ap()
    qkw_d = nc.dram_tensor("qkw", [128, 2], F32, kind="ExternalInput").ap()
    cst_d = nc.dram_tensor("cst", [128, 384], F32, kind="ExternalInput").ap()
    mc_d = nc.dram_tensor("mc", [128, 64, 96], F32, kind="ExternalInput").ap()
    ind_d = nc.dram_tensor("ind", [32, 32, 128], BF16, kind="ExternalInput").ap()
    caus_d = nc.dram_tensor("caus", [128, 4, 512], BF16, kind="ExternalInput").ap()
    yT = nc.dram_tensor("yT", [512, SEQ], BF16, kind="ExternalOutput").ap()
    qT = nc.dram_tensor("qT", [512, SEQ], BF16).ap()
    kT = nc.dram_tensor("kT", [512, SEQ], BF16).ap()
    gsT = nc.dram_tensor("gsT", [512, SEQ], BF16).ap()
    vtm = nc.dram_tensor("vtm", [SEQ, 512], BF16).ap()
    with ExitStack() as es:
        es.enter_context(nc.Block())
        S = Sched(nc, es)
        C = mk_consts(S)
        mk_eps(S, C)
        load_consts(S, C, cst_d)
        C["identf"] = B(C["cst_sb"].t[:, 256:384], "cst")
        S.lastw["identf"] = S.lastw["cst"]
        qkw = load_small(S, qkw_d, [128, 2])
        with ExitStack() as es2:
            S.es = es2
            if "i" in phases:
                emit_moba_inproj(S, C, uT, wq, qkw, qT, kT, gsT, vtm, NT=NQT)
        S.es = es
        S.barrier()
        with ExitStack() as es2:
            S.es = es2
            mc = load_small(S, mc_d, [128, 64, 96])
            IND = load_small(S, ind_d, [32, 32, 128], BF16)
            CAUS = load_small(S, caus_d, [128, 4, 512], BF16)
            if "a" in phases:
                emit_moba_attn(S, C, qT, kT, gsT, vtm, yT, mc, IND, CAUS, NQT)
        S.es = es
        S.finish()
    return nc


def moba_core_inputs(c, p):
    w = p["moba2_w_in"]
    cols = slice(c * 512, (c + 1) * 512)
    wq = np.ascontiguousarray(np.concatenate([w[:, 0 * D:1 * D][:, cols], w[:, 1 * D:2 * D][:, cols],
                                              w[:, 3 * D:4 * D][:, cols], w[:, 2 * D:3 * D][:, cols]], 1))
    qkw = np.ascontiguousarray(np.stack([p["moba2_q_norm"], p["moba2_k_norm"]], 1)).astype(np.float32)
    d = dict(wq=wq, qkw=qkw, cst=host_consts())
    d.update(moba_host_consts())
    return d


_PROGS = {}


def _prog(name, fn, *a):
    key = (name,) + a
    if key not in _PROGS:
        _PROGS[key] = fn(*a)
    return _PROGS[key]


def _tok_split(full):
    return [np.ascontiguousarray(full[:, c * TPC:(c + 1) * TPC]) for c in range(NCORES)]


def run_ssd(p, pre, uT):
    uT_full = np.ascontiguousarray(np.concatenate(uT, 1))
    res = run(_prog("ssd", build_ssd), [dict(uT=uT_full, **ssd_core_inputs(g, p, pre)) for g in range(NCORES)])
    return _tok_split(np.concatenate([r["yT"] for r in res], 0))


def run_moba(p, uT):
    uT_full = np.ascontiguousarray(np.concatenate(uT, 1))
    res = run(_prog("moba", build_moba), [dict(uT=uT_full, **moba_core_inputs(c, p)) for c in range(NCORES)])
    return _tok_split(np.concatenate([r["yT"] for r in res], 0))


def run_outproj(yT, w_out, hT, gnext):
    KC = w_out.shape[0] // 128
    nc = _prog("op", build_outproj_norm, KC, gnext is not None)
    ins = []
    for c in range(NCORES):
        d = dict(yT=yT[c], w_out=w_out, hT=hT[c])
        if gnext is not None:
            d["gw"] = lay_gw(gnext)
        ins.append(d)
    res = run(nc, ins)
    return [r["hT2"] for r in res], ([r["uT"] for r in res] if gnext is not None else None)


def run_pool(p, uT, hT, gnext):
    uT_full = np.concatenate([np.zeros((D, 16), NPBF)] + list(uT), 1)
    ins = []
    for c in range(NCORES):
        d = dict(uTh=np.ascontiguousarray(uT_full[:, c * TPC:c * TPC + TPC + 16]), w_in=p["pool1_w_in"],
                 w_grp=p["pool1_w_grp"], w_out=p["pool1_w_out"], hT=hT[c], gw=lay_gw(gnext))
        d.update(pool_core_inputs(c, p))
        ins.append(d)
    res = run(_prog("pool", build_pool), ins)
    return [r["hT2"] for r in res], [r["uT"] for r in res]


def kernel(**inputs):
    p = {k: np.asarray(v) for k, v in inputs.items()}
    x = p["x"][0]
    hT = [np.ascontiguousarray(x[c * TPC:(c + 1) * TPC].T) for c in range(NCORES)]
    res = run(_prog("norm", build_norm_only), [dict(hT=hT[c], gw=lay_gw(p["norm0"])) for c in range(NCORES)])
    uT = [r["uT"] for r in res]
    yT = run_ssd(p, "ssd0", uT)
    hT, uT = run_outproj(yT, p["ssd0_w_out"], hT, p["norm1"])
    hT, uT = run_pool(p, uT, hT, p["norm2"])
    yT = run_moba(p, uT)
    hT, uT = run_outproj(yT, p["moba2_w_out"], hT, p["norm3"])
    yT = run_ssd(p, "ssd3", uT)
    hT, _ = run_outproj(yT, p["ssd3_w_out"], hT, None)
    out = np.concatenate([h.T for h in hT], 0)[None]
    return np.ascontiguousarray(out.astype(np.float32))
```

```python
import os
import numpy as np
import ml_dtypes
from contextlib import ExitStack
import concourse.bass as bass
import concourse.mybir as mybir
from concourse.bass_utils import run_bass_kernel_spmd

F32 = mybir.dt.float32
BF16 = mybir.dt.bfloat16
AF = mybir.ActivationFunctionType
ALU = mybir.AluOpType
AX = mybir.AxisListType
NPBF = ml_dtypes.bfloat16

NCORES = 8
D = 4096
SEQ = 8192
TPC = SEQ // NCORES
EPS = 1e-6
SAME_SYNC = True


class B:
    def __init__(self, t, k):
        self.t = t
        self.k = k

    def __getitem__(self, idx):
        return self.t[idx]


class Sched:
    NDMA = 32

    def __init__(self, nc, es):
        self.nc = nc
        self.es = es
        self.eng = dict(pe=nc.tensor, dve=nc.vector, act=nc.scalar, pool=nc.gpsimd, sp=nc.sync)
        self.sem = {k: es.enter_context(nc.semaphore("sem_" + k)) for k in self.eng}
        self.cnt = {k: 0 for k in self.eng}
        self.dsem = [es.enter_context(nc.semaphore(f"dsem{i}")) for i in range(self.NDMA)]
        self.duse = [0] * self.NDMA
        self.drr = 0
        self.seen = {k: {} for k in self.eng}
        self.lastw = {}
        self.readers = {}
        self.nbuf = 0

    def sb(self, shape, dtype, name=None):
        self.nbuf += 1
        name = name or f"sb{self.nbuf}"
        t = self.es.enter_context(self.nc.sbuf_tensor(name, list(shape), dtype))
        return B(t, name)

    def ps(self, shape, dtype=F32, name=None):
        self.nbuf += 1
        name = name or f"ps{self.nbuf}"
        t = self.es.enter_context(self.nc.psum_tensor(name, list(shape), dtype))
        return B(t, name)

    def _semof(self, sk):
        return self.dsem[sk[1]] if isinstance(sk, tuple) else self.sem[sk]

    def _wait(self, e, sk, val):
        if val <= 0:
            return
        if sk == e:
            if e == "pe" or e == "sp" or not SAME_SYNC:
                return
        if self.seen[e].get(sk, 0) >= val:
            return
        self.eng[e].wait_ge(self._semof(sk), val)
        self.seen[e][sk] = val

    def _deps(self, e, r, w):
        evs = {}
        for x in r:
            ev = self.lastw.get(x)
            if ev:
                evs[ev[0]] = max(evs.get(ev[0], 0), ev[1])
        for x in w:
            ev = self.lastw.get(x)
            if ev:
                evs[ev[0]] = max(evs.get(ev[0], 0), ev[1])
            for sk, v in self.readers.get(x, {}).items():
                evs[sk] = max(evs.get(sk, 0), v)
        for sk, v in evs.items():
            self._wait(e, sk, v)

    def _record(self, ev, r, w):
        for x in r:
            d = self.readers.setdefault(x, {})
            d[ev[0]] = max(d.get(ev[0], 0), ev[1])
        for x in w:
            self.lastw[x] = ev
            self.readers[x] = {}

    def op(self, e, meth, *args, r=(), w=(), **kw):
        self._deps(e, r, w)
        ins = getattr(self.eng[e], meth)(*args, **kw)
        self.cnt[e] += 1
        ins.then_inc(self.sem[e], 1)
        self._record((e, self.cnt[e]), r, w)
        return ins

    def dma(self, q, out, in_, r=(), w=()):
        k = self.drr
        self.drr = (k + 1) % self.NDMA
        self._deps(q, r, w)
        self._wait(q, ("d", k), 16 * self.duse[k])
        ins = self.eng[q].dma_start(out=out, in_=in_)
        ins.then_inc(self.dsem[k], 16)
        self.duse[k] += 1
        self._record((("d", k), 16 * self.duse[k]), r, w)
        return ins

    def uk(self):
        self.nbuf += 1
        return ("u", self.nbuf)

    def barrier(self):
        for e in ("pe", "dve", "act", "sp"):
            for k in range(self.NDMA):
                if self.duse[k]:
                    self._wait(e, ("d", k), 16 * self.duse[k])
            for e2 in ("pe", "dve", "act", "pool"):
                if e2 != e and self.cnt[e2]:
                    self._wait(e, e2, self.cnt[e2])
        self.lastw = {}
        self.readers = {}

    def finish(self):
        for k in range(self.NDMA):
            if self.duse[k]:
                self._wait("sp", ("d", k), 16 * self.duse[k])
        for e in ("pe", "dve", "act", "pool"):
            if self.cnt[e]:
                self._wait("sp", e, self.cnt[e])


def mk_consts(S):
    c = {}
    c["ones_f"] = S.sb([128, 128], F32, "ones_f")
    S.op("dve", "memset", c["ones_f"][:], 1.0, w=["ones_f"])
    c["ones_b"] = S.sb([128, 128], BF16, "ones_b")
    S.op("dve", "memset", c["ones_b"][:], 1.0, w=["ones_b"])
    return c


def emit_norm(S, C, hT, gw, uT, ucol0, T=TPC, TT=256):
    hv = hT.rearrange("(kc p) t -> p kc t", p=128)
    uv = uT.rearrange("(kc p) t -> p kc t", p=128)
    hb = [S.sb([128, 32, TT], F32) for _ in range(2)]
    ub = [S.sb([128, 32, TT], BF16) for _ in range(2)]
    sq = [S.sb([128, TT], F32) for _ in range(2)]
    shi = [S.sb([128, TT], BF16) for _ in range(2)]
    slo = [S.sb([128, TT], BF16) for _ in range(2)]
    rs = S.sb([128, TT], F32)
    pb = S.ps([128, 512], F32)
    for tt in range(T // TT):
        h = hb[tt % 2]
        u = ub[tt % 2]
        S.dma("sp", h[:], hv[:, :, tt * TT:(tt + 1) * TT], r=["hT"], w=[h.k])
        for kc in range(32):
            s = sq[kc % 2]
            S.op("act", "activation", out=s[:], in_=h[:, kc, :], func=AF.Square, r=[h.k], w=[s.k])
            hi_ = shi[kc % 2]
            lo_ = slo[kc % 2]
            S.op("dve", "tensor_copy", out=hi_[:], in_=s[:], r=[s.k], w=[hi_.k])
            S.op("dve", "tensor_tensor", out=lo_[:], in0=s[:], in1=hi_[:], op=ALU.subtract, r=[s.k, hi_.k], w=[lo_.k])
            S.op("pe", "matmul", pb[:, :TT], lhsT=C["ones_b"][:], rhs=hi_[:], start=(kc == 0), stop=False,
                 r=[hi_.k, "ones_b"], w=[pb.k])
            S.op("pe", "matmul", pb[:, :TT], lhsT=C["ones_b"][:], rhs=lo_[:], start=False, stop=(kc == 31),
                 r=[lo_.k, "ones_b"], w=[pb.k])
        S.op("act", "activation", out=rs[:], in_=pb[:, :TT], func=AF.Sqrt, scale=1.0 / D, bias=C["eps"][:],
             r=[pb.k, "eps"], w=[rs.k])
        S.op("dve", "reciprocal", out=rs[:], in_=rs[:], r=[rs.k], w=[rs.k])
        for kc in range(32):
            S.op("dve", "scalar_tensor_tensor", out=u[:, kc, :], in0=h[:, kc, :], scalar=gw[:, kc:kc + 1],
                 in1=rs[:], op0=ALU.mult, op1=ALU.mult, r=[h.k, rs.k, gw.k], w=[u.k])
        S.dma("act", uv[:, :, ucol0 + tt * TT:ucol0 + (tt + 1) * TT], u[:], r=[u.k], w=[S.uk()])


def emit_outproj(S, yT, KC, w_out, hT_in, hT_out, T=TPC):
    yv = yT.rearrange("(kc p) t -> p kc t", p=128)
    wv = w_out.rearrange("(kc p) n -> p kc n", p=128)
    yb = S.sb([128, KC, T], BF16)
    NP = 4
    KP = KC // NP
    for q in range(NP):
        S.dma("sp", yb[:, q * KP:(q + 1) * KP, :], yv[:, q * KP:(q + 1) * KP, :], r=["yT"], w=[(yb.k, q)])
    wb = [S.sb([128, KC, 128], BF16) for _ in range(2)]
    hb = [S.sb([128, T], F32) for _ in range(2)]
    pb = [S.ps([128, 512], F32) for _ in range(4)]
    ip = 0
    for nch in range(32):
        w = wb[nch % 2]
        h = hb[nch % 2]
        load_cast(S, w, wv[:, :, nch * 128:(nch + 1) * 128], KC, 128, 2048)
        S.dma("sp", h[:], hT_in[nch * 128:(nch + 1) * 128, :], r=["hT"], w=[h.k])
        for hf in range(T // 512):
            p = pb[ip % 4]
            ip += 1
            for kc in range(KC):
                S.op("pe", "matmul", p[:], lhsT=w[:, kc, :], rhs=yb[:, kc, hf * 512:(hf + 1) * 512], start=(kc == 0),
                     stop=(kc == KC - 1), r=[w.k, (yb.k, kc // KP)], w=[p.k])
            S.op("dve", "tensor_tensor", out=h[:, hf * 512:(hf + 1) * 512], in0=p[:], in1=h[:, hf * 512:(hf + 1) * 512],
                 op=ALU.add, r=[p.k, h.k], w=[h.k])
        S.dma("act", hT_out[nch * 128:(nch + 1) * 128, :], h[:], r=[h.k], w=[S.uk()])


def load_cast(S, dst, src, nk, ncols, selems=4096):
    if not hasattr(S, "stg") or S.stg_es is not S.es or S.stg_n != selems:
        S.stg = [S.sb([128, selems], F32) for _ in range(2)]
        S.stg_es = S.es
        S.stg_n = selems
        S.stg_i = 0
    per = max(1, selems // ncols)
    for k0 in range(0, nk, per):
        k1 = min(nk, k0 + per)
        st = S.stg[S.stg_i % 2]
        S.stg_i += 1
        sv = st[:, 0:(k1 - k0) * ncols].rearrange("p (k n) -> p k n", n=ncols)
        S.dma("sp", sv, src[:, k0:k1, :], w=[st.k])
        S.op("act", "activation", out=dst[:, k0:k1, 0:ncols], in_=sv, func=AF.Copy, r=[st.k], w=[dst.k])


def new_nc():
    return bass.Bass("TRN2", target_bir_lowering=False)


def load_small(S, dram_ap, shape, dtype=F32, key=None, q="sp"):
    b = S.sb(shape, dtype)
    S.dma(q, b[:], dram_ap, w=[b.k])
    return b


def mk_eps(S, C):
    C["eps"] = S.sb([128, 1], F32, "eps")
    S.op("dve", "memset", C["eps"][:], EPS, w=["eps"])


def build_norm_only():
    nc = new_nc()
    hT = nc.dram_tensor("hT", [D, TPC], F32, kind="ExternalInput").ap()
    gw_d = nc.dram_tensor("gw", [128, 32], F32, kind="ExternalInput").ap()
    uT = nc.dram_tensor("uT", [D, TPC], BF16, kind="ExternalOutput").ap()
    with ExitStack() as es:
        es.enter_context(nc.Block())
        S = Sched(nc, es)
        C = mk_consts(S)
        mk_eps(S, C)
        gw = load_small(S, gw_d, [128, 32])
        emit_norm(S, C, hT, gw, uT, 0)
        S.finish()
    return nc


def build_outproj_norm(KC, do_norm=True):
    nc = new_nc()
    yT = nc.dram_tensor("yT", [KC * 128, TPC], BF16, kind="ExternalInput").ap()
    w_out = nc.dram_tensor("w_out", [KC * 128, D], F32, kind="ExternalInput").ap()
    hT = nc.dram_tensor("hT", [D, TPC], F32, kind="ExternalInput").ap()
    hT2 = nc.dram_tensor("hT2", [D, TPC], F32, kind="ExternalOutput").ap()
    if do_norm:
        gw_d = nc.dram_tensor("gw", [128, 32], F32, kind="ExternalInput").ap()
        uT = nc.dram_tensor("uT", [D, TPC], BF16, kind="ExternalOutput").ap()
    with ExitStack() as es:
        es.enter_context(nc.Block())
        S = Sched(nc, es)
        with ExitStack() as es2:
            S.es = es2
            emit_outproj(S, yT, KC, w_out, hT, hT2)
        S.es = es
        S.barrier()
        if do_norm:
            C = mk_consts(S)
            mk_eps(S, C)
            gw = load_small(S, gw_d, [128, 32])
            emit_norm(S, C, hT2, gw, uT, 0)
        S.finish()
    return nc


_TIMES = []


def run(nc, in_maps):
    if os.environ.get("KTRACE"):
        res = run_bass_kernel_spmd(nc, in_maps, core_ids=list(range(NCORES)), trace=True)
        _TIMES.append(res.exec_time_ns)
        print("KTRACE exec_time_ns", res.exec_time_ns, flush=True)
        return res.results
    res = run_bass_kernel_spmd(nc, in_maps, core_ids=list(range(NCORES)))
    return res.results


def lay_gw(g):
    return np.ascontiguousarray(g.reshape(32, 128).T)


def bc3(ap, n, d):
    return ap.unsqueeze(2).to_broadcast([128, n, d])


def v3(ap, d):
    return ap.rearrange("p (r d) -> p r d", d=d)


def emit_ssd_inproj(S, C, uT, wg, cw, cb, dtb, nA, xbcT, zs, dta, NT=SEQ // 512):
    uv = uT.rearrange("(kc p) t -> p kc t", p=128)
    wv = wg.rearrange("(kc p) n -> p kc n", p=128)
    TT = 512
    ub = [S.sb([128, 32, TT], BF16) for _ in range(2)]
    wb = [S.sb([128, 32, 512], BF16) for _ in range(2)]
    xr = [S.sb([128, TT + 3], F32) for _ in range(4)]
    acc = [S.sb([128, TT], F32) for _ in range(2)]
    ob = [S.sb([128, TT], BF16) for _ in range(2)]
    zt = [S.sb([128, 512], F32) for _ in range(2)]
    dt1 = S.sb([128, 16], F32)
    dto = [S.sb([128, 32], F32) for _ in range(2)]
    pb = [S.ps([128, 512], F32) for _ in range(4)]
    blocks = [(0, 512, "cm"), (512, 512, "cm"), (1024, 256, "cm"), (2304, 16, "dt")]
    iu = 0
    ip = 0
    io = 0
    for bi, (c0, ncol, kind) in enumerate(blocks):
        w = wb[bi % 2]
        load_cast(S, w, wv[:, :, c0:c0 + ncol], 32, ncol)
        for tt in range(NT):
            u = ub[iu % 2]
            iu += 1
            S.dma("sp", u[:], uv[:, :, tt * TT:(tt + 1) * TT], r=["uT"], w=[u.k])
            if kind == "cm":
                for j in range(ncol // 128):
                    ch = c0 // 128 + j
                    p = pb[ip % 4]
                    ip += 1
                    for kc in range(32):
                        S.op("pe", "matmul", p[:], lhsT=w[:, kc, j * 128:(j + 1) * 128], rhs=u[:, kc, :],
                             start=(kc == 0), stop=(kc == 31), r=[w.k, u.k], w=[p.k])
                    x = xr[j]
                    if tt == 0:
                        S.op("dve", "memset", x[:, 0:3], 0.0, w=[x.k])
                    S.op("act", "activation", out=x[:, 3:TT + 3], in_=p[:], func=AF.Copy, r=[p.k], w=[x.k])
                    a = acc[io % 2]
                    o = ob[io % 2]
                    io += 1
                    S.op("act", "activation", out=a[:], in_=x[:, 3:TT + 3], func=AF.Identity,
                         scale=cw[:, ch, 3:4], bias=cb[:, ch:ch + 1], r=[x.k, cw.k, cb.k], w=[a.k])
                    for k in (2, 1, 0):
                        S.op("dve", "scalar_tensor_tensor", out=a[:], in0=x[:, k:k + TT], scalar=cw[:, ch, k:k + 1],
                             in1=a[:], op0=ALU.mult, op1=ALU.add, r=[x.k, a.k, cw.k], w=[a.k])
                    S.op("act", "activation", out=o[:], in_=a[:], func=AF.Silu, r=[a.k], w=[o.k])
                    S.op("dve", "tensor_copy", out=x[:, 0:3], in_=x[:, TT:TT + 3], r=[x.k], w=[x.k])
                    S.dma("act", xbcT[ch * 128:(ch + 1) * 128, tt * TT:(tt + 1) * TT], o[:], r=[o.k], w=[S.uk()])
            elif kind == "z":
                zc0 = c0 - 1280
                for sub in range(4):
                    p = pb[ip % 4]
                    ip += 1
                    for kc in range(32):
                        S.op("pe", "matmul", p[:], lhsT=u[:, kc, sub * 128:(sub + 1) * 128], rhs=w[:, kc, :],
                             start=(kc == 0), stop=(kc == 31), r=[w.k, u.k], w=[p.k])
                    z = zt[io % 2]
                    io += 1
                    S.op("act", "activation", out=z[:], in_=p[:], func=AF.Silu, r=[p.k], w=[z.k])
                    t0 = tt * TT + sub * 128
                    S.dma("act", zs[t0:t0 + 128, zc0:zc0 + 512], z[:], r=[z.k], w=[S.uk()])
            else:
                for sub in range(4):
                    p = pb[ip % 4]
                    ip += 1
                    for kc in range(32):
                        S.op("pe", "matmul", p[:, 0:16], lhsT=u[:, kc, sub * 128:(sub + 1) * 128], rhs=w[:, kc, 0:16],
                             start=(kc == 0), stop=(kc == 31), r=[w.k, u.k], w=[p.k])
                    d = dto[io % 2]
                    io += 1
                    S.op("dve", "tensor_tensor", out=dt1[:], in0=p[:, 0:16], in1=dtb[:], op=ALU.add,
                         r=[p.k, dtb.k], w=[dt1.k])
                    S.op("act", "activation", out=dt1[:], in_=dt1[:], func=AF.Exp, r=[dt1.k], w=[dt1.k])
                    S.op("act", "activation", out=d[:, 0:16], in_=dt1[:], func=AF.Ln, bias=C["ones_f"][:, 0:1],
                         r=[dt1.k, "ones_f"], w=[d.k])
                    S.op("dve", "tensor_tensor", out=d[:, 16:32], in0=d[:, 0:16], in1=nA[:], op=ALU.mult,
                         r=[d.k, nA.k], w=[d.k])
                    t0 = tt * TT + sub * 128
                    S.dma("act", dta[t0:t0 + 128, :], d[:], r=[d.k], w=[S.uk()])


def emit_ssd_scan(S, C, xbcT, uT, wg, dta, dsk, nw, yT, NCH=SEQ // 128):
    U = C["U"]
    NEG = C["NEG"]
    ID = C["ident"]
    xv = xbcT[0:1024, :].rearrange("(j p) t -> p j t", p=128)
    bcv = xbcT[1024:1280, :].rearrange("(j p) t -> p j t", p=128)
    uv = uT.rearrange("(kc p) t -> p kc t", p=128)
    wv = wg.rearrange("(kc p) n -> p kc n", p=128)
    wz = [S.sb([128, 32, 512], BF16) for _ in range(2)]
    for blk in range(2):
        load_cast(S, wz[blk], wv[:, :, 1280 + blk * 512:1280 + (blk + 1) * 512], 32, 512, 2048)
    ubz = [S.sb([128, 32, 128], BF16) for _ in range(2)]
    xT = [S.sb([128, 8, 128], BF16) for _ in range(2)]
    bcT = [S.sb([128, 2, 128], BF16) for _ in range(2)]
    da = [S.sb([128, 32], F32) for _ in range(2)]
    zt = [S.sb([128, 1024], F32) for _ in range(2)]
    xs = S.sb([128, 1024], F32)
    Btm = S.sb([128, 128], BF16)
    acs = S.sb([128, 32], F32)
    e32 = S.sb([128, 32], F32)
    nacs = S.sb([128, 16], F32)
    dte = S.sb([128, 16], F32)
    dd = S.sb([128, 16], F32)
    xdt = S.sb([128, 1024], BF16)
    xde = S.sb([128, 1024], BF16)
    a3 = S.sb([128, 3, 16], BF16)
    r1 = S.sb([128, 16], F32)
    a_b = S.sb([128, 48, 128], BF16)
    CBm = S.sb([128, 128], F32)
    y = S.sb([128, 1024], F32)
    tmp = S.sb([128, 1024], F32)
    tb = [S.sb([128, 512], F32) for _ in range(2)]
    dec = [S.sb([128, 512], F32) for _ in range(2)]
    MT = [S.sb([128, 512], BF16) for _ in range(2)]
    hst = S.sb([128, 1024], F32)
    prev = S.sb([128, 1024], BF16)
    ss = S.sb([128, 1], F32)
    rstd = S.sb([128, 1], F32)
    yn = S.sb([128, 1024], BF16)
    yo = [S.sb([128, 4, 128], BF16) for _ in range(2)]
    bigA = [S.ps([128, 512], F32) for _ in range(2)]
    bigB = [S.ps([128, 512], F32) for _ in range(2)]
    seg = [S.ps([128, 512], F32) for _ in range(2)]
    misc = S.ps([128, 512], F32)
    pout = misc
    pz = S.ps([128, 512], F32)

    def emit_z_load(c):
        u_ = ubz[c % 2]
        S.dma("sp", u_[:], uv[:, :, c * 128:(c + 1) * 128], r=["uT"], w=[u_.k])

    def emit_z_piece(c, i):
        u_ = ubz[c % 2]
        zz = zt[c % 2]
        blk = i // 2
        for kc in range((i % 2) * 16, (i % 2) * 16 + 16):
            S.op("pe", "matmul", pz[:], lhsT=u_[:, kc, :], rhs=wz[blk][:, kc, :], start=(kc == 0), stop=(kc == 31),
                 r=[u_.k, wz[blk].k], w=[pz.k])
        if i % 2 == 1:
            S.op("act", "activation", out=zz[:, blk * 512:(blk + 1) * 512], in_=pz[:], func=AF.Silu, r=[pz.k],
                 w=[zz.k])

    def emit_z(c):
        emit_z_load(c)
        for i in range(4):
            emit_z_piece(c, i)

    emit_z(0)
    S.op("dve", "memset", hst[:], 0.0, w=[hst.k])
    S.op("dve", "memset", prev[:], 0.0, w=[prev.k])
    for c in range(NCH):
        t0 = c * 128
        x_ = xT[c % 2]
        bc = bcT[c % 2]
        d_ = da[c % 2]
        z_ = zt[c % 2]
        S.dma("sp", x_[:], xv[:, :, t0:t0 + 128], r=["xbcT"], w=[x_.k])
        S.dma("sp", bc[:], bcv[:, :, t0:t0 + 128], r=["xbcT"], w=[bc.k])
        S.dma("sp", d_[:], dta[t0:t0 + 128, :], r=["dta"], w=[d_.k])
        if c + 1 < NCH:
            emit_z_load(c + 1)
        BT = bc[:, 0, :]
        CT = bc[:, 1, :]
        for j in range(8):
            S.op("pe", "matmul", bigA[j // 4][:, (j % 4) * 128:(j % 4 + 1) * 128], lhsT=x_[:, j, :], rhs=ID[:],
                 start=True, stop=True, r=[x_.k, "ident"], w=[bigA[j // 4].k])
        for hf in range(2):
            S.op("act", "activation", out=xs[:, hf * 512:(hf + 1) * 512], in_=bigA[hf][:], func=AF.Copy,
                 r=[bigA[hf].k], w=[xs.k])
        S.op("pe", "matmul", misc[:, 128:256], lhsT=BT, rhs=ID[:], start=True, stop=True, r=[bc.k, "ident"],
             w=[misc.k])
        S.op("act", "activation", out=Btm[:], in_=misc[:, 128:256], func=AF.Copy, r=[misc.k], w=[Btm.k])
        S.op("dve", "tensor_copy", out=a3[:, 0, :], in_=d_[:, 16:32], r=[d_.k], w=[a3.k])
        S.op("dve", "tensor_tensor", out=r1[:], in0=d_[:, 16:32], in1=a3[:, 0, :], op=ALU.subtract, r=[d_.k, a3.k],
             w=[r1.k])
        S.op("dve", "tensor_copy", out=a3[:, 1, :], in_=r1[:], r=[r1.k], w=[a3.k])
        S.op("dve", "tensor_tensor", out=r1[:], in0=r1[:], in1=a3[:, 1, :], op=ALU.subtract, r=[r1.k, a3.k], w=[r1.k])
        S.op("dve", "tensor_copy", out=a3[:, 2, :], in_=r1[:], r=[r1.k], w=[a3.k])
        for i3 in range(3):
            S.op("pe", "matmul", misc[:, 0:16], lhsT=C["U_b"][:], rhs=a3[:, i3, :], start=(i3 == 0), stop=(i3 == 2),
                 r=[a3.k, "U_b"], w=[misc.k])
        for i3 in range(3):
            S.op("pe", "matmul", misc[:, 16:32], lhsT=C["ones_b"][:], rhs=a3[:, i3, :], start=(i3 == 0),
                 stop=(i3 == 2), r=[a3.k, "ones_b"], w=[misc.k])
        S.op("dve", "tensor_copy", out=acs[:], in_=misc[:, 0:32], r=[misc.k], w=[acs.k])
        S.op("act", "activation", out=e32[:], in_=acs[:], func=AF.Exp, r=[acs.k], w=[e32.k])
        S.op("dve", "tensor_scalar", out=nacs[:], in0=acs[:, 0:16], scalar1=-1.0, scalar2=None, op0=ALU.mult,
             r=[acs.k], w=[nacs.k])
        S.op("dve", "tensor_tensor", out=dte[:], in0=acs[:, 16:32], in1=acs[:, 0:16], op=ALU.subtract, r=[acs.k],
             w=[dte.k])
        S.op("act", "activation", out=dte[:], in_=dte[:], func=AF.Exp, r=[dte.k], w=[dte.k])
        S.op("dve", "tensor_tensor", out=dd[:], in0=dte[:], in1=d_[:, 0:16], op=ALU.mult, r=[dte.k, d_.k], w=[dd.k])
        S.op("dve", "tensor_tensor", out=v3(xdt[:], 64), in0=v3(xs[:], 64), in1=bc3(d_[:, 0:16], 16, 64), op=ALU.mult,
             r=[xs.k, d_.k], w=[xdt.k])
        S.op("dve", "tensor_tensor", out=v3(xde[:], 64), in0=v3(xs[:], 64), in1=bc3(dd[:], 16, 64), op=ALU.mult,
             r=[xs.k, dd.k], w=[xde.k])
        S.op("dve", "tensor_copy", out=a_b[:], in_=bc3(a3[:].rearrange("p i r -> p (i r)"), 48, 128), r=[a3.k],
             w=[a_b.k])
        S.op("pe", "matmul", misc[:, 256:384], lhsT=BT, rhs=CT, start=True, stop=True, r=[bc.k], w=[misc.k])
        S.op("dve", "tensor_tensor", out=CBm[:], in0=misc[:, 256:384], in1=U[:], op=ALU.mult, r=[misc.k, "U"],
             w=[CBm.k])
        for hf in range(2):
            S.op("pe", "matmul", bigB[hf][:], lhsT=CT, rhs=prev[:, hf * 512:(hf + 1) * 512], start=True, stop=True,
                 r=[bc.k, prev.k], w=[bigB[hf].k])
            S.op("dve", "tensor_tensor", out=v3(y[:, hf * 512:(hf + 1) * 512], 64), in0=v3(bigB[hf][:], 64),
                 in1=bc3(e32[:, hf * 8:hf * 8 + 8], 8, 64), op=ALU.mult, r=[bigB[hf].k, e32.k], w=[y.k])
        def emit_seg(hg):
            sg = seg[hg % 2]
            for q in range(4):
                r_ = hg * 4 + q
                for i3 in range(3):
                    S.op("pe", "matmul", sg[:, q * 128:(q + 1) * 128], lhsT=a_b[:, i3 * 16 + r_, :], rhs=C["U_b"][:],
                         start=(i3 == 0), stop=(i3 == 2), r=[a_b.k, "U_b"], w=[sg.k])

        emit_seg(0)
        for hg in range(4):
            sg = seg[hg % 2]
            t_ = tb[hg % 2]
            de = dec[hg % 2]
            m_ = MT[hg % 2]
            if hg + 1 < 4:
                emit_seg(hg + 1)
            if c + 1 < NCH:
                emit_z_piece(c + 1, hg)
            for q in range(4):
                r_ = hg * 4 + q
                S.op("dve", "scalar_tensor_tensor", out=t_[:, q * 128:(q + 1) * 128], in0=sg[:, q * 128:(q + 1) * 128],
                     scalar=nacs[:, r_:r_ + 1], in1=NEG[:], op0=ALU.add, op1=ALU.add, r=[sg.k, nacs.k, "NEG"],
                     w=[t_.k])
            S.op("act", "activation", out=de[:], in_=t_[:], func=AF.Exp, r=[t_.k], w=[de.k])
            S.op("dve", "tensor_tensor", out=v3(m_[:], 128), in0=v3(de[:], 128),
                 in1=CBm[:].unsqueeze(1).to_broadcast([128, 4, 128]), op=ALU.mult, r=[de.k, CBm.k], w=[m_.k])
            for q in range(4):
                r_ = hg * 4 + q
                S.op("pe", "matmul", bigA[r_ // 8][:, (r_ % 8) * 64:(r_ % 8 + 1) * 64], lhsT=m_[:, q * 128:(q + 1) * 128],
                     rhs=xdt[:, r_ * 64:(r_ + 1) * 64], start=True, stop=True, r=[m_.k, xdt.k], w=[bigA[r_ // 8].k])
        for hf in range(2):
            S.op("dve", "tensor_tensor", out=y[:, hf * 512:(hf + 1) * 512], in0=bigA[hf][:],
                 in1=y[:, hf * 512:(hf + 1) * 512], op=ALU.add, r=[bigA[hf].k, y.k], w=[y.k])
        S.op("dve", "tensor_tensor", out=v3(tmp[:], 64), in0=v3(xs[:], 64), in1=bc3(dsk[:], 16, 64), op=ALU.mult,
             r=[xs.k, dsk.k], w=[tmp.k])
        S.op("dve", "tensor_tensor", out=y[:], in0=y[:], in1=tmp[:], op=ALU.add, r=[y.k, tmp.k], w=[y.k])
        for hf in range(2):
            S.op("pe", "matmul", bigB[hf][:], lhsT=Btm[:], rhs=xde[:, hf * 512:(hf + 1) * 512], start=True, stop=True,
                 r=[Btm.k, xde.k], w=[bigB[hf].k])
        S.op("dve", "tensor_tensor", out=v3(hst[:], 64), in0=v3(hst[:], 64), in1=bc3(e32[:, 16:32], 16, 64),
             op=ALU.mult, r=[hst.k, e32.k], w=[hst.k])
        for hf in range(2):
            S.op("dve", "tensor_tensor", out=hst[:, hf * 512:(hf + 1) * 512], in0=bigB[hf][:],
                 in1=hst[:, hf * 512:(hf + 1) * 512], op=ALU.add, r=[bigB[hf].k, hst.k], w=[hst.k])
        S.op("act", "activation", out=prev[:], in_=hst[:], func=AF.Copy, r=[hst.k], w=[prev.k])
        S.op("dve", "tensor_tensor", out=y[:], in0=y[:], in1=z_[:], op=ALU.mult, r=[y.k, z_.k], w=[y.k])
        S.op("dve", "memset", ss[:], 0.0, w=[ss.k])
        S.op("act", "activation", out=tmp[:], in_=y[:], func=AF.Square, accum_out=ss[:], r=[y.k, ss.k], w=[tmp.k, ss.k])
        S.op("act", "activation", out=rstd[:], in_=ss[:], func=AF.Sqrt, scale=1.0 / 1024, bias=C["eps"][:],
             r=[ss.k, "eps"], w=[rstd.k])
        S.op("dve", "reciprocal", out=rstd[:], in_=rstd[:], r=[rstd.k], w=[rstd.k])
        S.op("dve", "scalar_tensor_tensor", out=yn[:], in0=y[:], scalar=rstd[:, 0:1], in1=nw[:], op0=ALU.mult,
             op1=ALU.mult, r=[y.k, rstd.k, nw.k], w=[yn.k])
        for q in range(2):
            o = yo[q]
            for j in range(4):
                S.op("pe", "matmul", pout[:, j * 128:(j + 1) * 128], lhsT=yn[:, (q * 4 + j) * 128:(q * 4 + j + 1) * 128],
                     rhs=ID[:], start=True, stop=True, r=[yn.k, "ident"], w=[pout.k])
            S.op("act", "activation", out=o[:], in_=v3(pout[:], 128), func=AF.Copy, r=[pout.k], w=[o.k])
            S.dma("act", yT[q * 512:(q + 1) * 512, :].rearrange("(j p) t -> p j t", p=128)[:, :, t0:t0 + 128], o[:],
                  r=[o.k], w=[S.uk()])


def load_consts(S, C, cst_d):
    cst = S.sb([128, 384], F32, "cst_sb")
    S.dma("sp", cst[:], cst_d, w=["cst"])
    C["U"] = B(cst.t[:, 0:128], "cst")
    C["NEG"] = B(cst.t[:, 128:256], "cst")
    C["cst_sb"] = cst
    idb = S.sb([128, 128], BF16, "ident")
    S.op("dve", "tensor_copy", out=idb[:], in_=cst[:, 256:384], r=["cst"], w=["ident"])
    C["ident"] = idb
    ub_ = S.sb([128, 128], BF16, "U_b")
    S.op("dve", "tensor_copy", out=ub_[:], in_=cst[:, 0:128], r=["cst"], w=["U_b"])
    C["U_b"] = ub_
    S.lastw["U"] = S.lastw["cst"]
    S.lastw["NEG"] = S.lastw["cst"]


def host_consts():
    j = np.arange(128)
    U = (j[:, None] <= j[None, :]).astype(np.float32)
    NEG = (U - 1.0) * 1e4
    I = np.eye(128, dtype=np.float32)
    return np.ascontiguousarray(np.concatenate([U, NEG, I], 1))


def build_ssd(NCH=SEQ // 128, phases="is"):
    nc = new_nc()
    uT = nc.dram_tensor("uT", [D, SEQ], BF16, kind="ExternalInput").ap()
    wg = nc.dram_tensor("wg", [D, 2320], F32, kind="ExternalInput").ap()
    cw_d = nc.dram_tensor("cw", [128, 10, 4], F32, kind="ExternalInput").ap()
    cb_d = nc.dram_tensor("cb", [128, 10], F32, kind="ExternalInput").ap()
    sm_d = nc.dram_tensor("sm", [128, 48], F32, kind="ExternalInput").ap()
    nw_d = nc.dram_tensor("nw", [128, 1024], F32, kind="ExternalInput").ap()
    cst_d = nc.dram_tensor("cst", [128, 384], F32, kind="ExternalInput").ap()
    yT = nc.dram_tensor("yT", [1024, SEQ], BF16, kind="ExternalOutput").ap()
    xbcT = nc.dram_tensor("xbcT", [1280, SEQ], BF16).ap()
    zs = nc.dram_tensor("zs", [SEQ, 1024], F32).ap()
    dta = nc.dram_tensor("dta", [SEQ, 32], F32).ap()
    with ExitStack() as es:
        es.enter_context(nc.Block())
        S = Sched(nc, es)
        C = mk_consts(S)
        mk_eps(S, C)
        load_consts(S, C, cst_d)
        cw = load_small(S, cw_d, [128, 10, 4])
        cb = load_small(S, cb_d, [128, 10])
        sm = load_small(S, sm_d, [128, 48])
        nw = load_small(S, nw_d, [128, 1024])
        nA = S.sb([128, 16], F32)
        S.op("act", "activation", out=nA[:], in_=sm[:, 16:32], func=AF.Exp, r=[sm.k], w=[nA.k])
        S.op("dve", "tensor_scalar", out=nA[:], in0=nA[:], scalar1=-1.0, scalar2=None, op0=ALU.mult, r=[nA.k], w=[nA.k])
        dtb = B(sm.t[:, 0:16], sm.k)
        dsk = B(sm.t[:, 32:48], sm.k)
        with ExitStack() as es2:
            S.es = es2
            if "i" in phases:
                emit_ssd_inproj(S, C, uT, wg, cw, cb, dtb, nA, xbcT, zs, dta, NT=NCH // 4)
        S.es = es
        S.barrier()
        with ExitStack() as es2:
            S.es = es2
            if "s" in phases:
                emit_ssd_scan(S, C, xbcT, uT, wg, dta, dsk, nw, yT, NCH)
        S.es = es
        S.finish()
    return nc


def ssd_core_inputs(g, p, pre):
    w_in = p[pre + "_w_in"]
    DI = 8192
    xs = slice(DI + g * 1024, DI + (g + 1) * 1024)
    bs = slice(2 * DI + g * 128, 2 * DI + (g + 1) * 128)
    cs = slice(2 * DI + 1024 + g * 128, 2 * DI + 1024 + (g + 1) * 128)
    z_ = slice(g * 1024, (g + 1) * 1024)
    dts = slice(2 * DI + 2048 + g * 16, 2 * DI + 2048 + (g + 1) * 16)
    wg = np.ascontiguousarray(np.concatenate([w_in[:, xs], w_in[:, bs], w_in[:, cs], w_in[:, z_], w_in[:, dts]], 1))
    ccols = np.concatenate([np.arange(g * 1024, (g + 1) * 1024), 8192 + np.arange(g * 128, (g + 1) * 128),
                            8192 + 1024 + np.arange(g * 128, (g + 1) * 128)])
    cwf = p[pre + "_conv_w"][:, ccols]
    cw = np.ascontiguousarray(cwf.T.reshape(10, 128, 4).transpose(1, 0, 2))
    cb = np.ascontiguousarray(p[pre + "_conv_b"][ccols].reshape(10, 128).T)
    hs = slice(g * 16, (g + 1) * 16)
    sm = np.concatenate([p[pre + "_dt_bias"][hs], p[pre + "_a_log"][hs], p[pre + "_d"][hs]])[None, :]
    sm = np.ascontiguousarray(np.broadcast_to(sm, (128, 48))).astype(np.float32)
    nw = np.ascontiguousarray(np.broadcast_to(p[pre + "_norm_w"][None, g * 1024:(g + 1) * 1024], (128, 1024)))
    return dict(wg=wg, cw=cw, cb=cb, sm=sm, nw=nw, cst=host_consts())


def emit_pool(S, C, uTh, w_in, w_grp, bgs, rc, gT):
    uv = uTh.rearrange("(kc p) t -> p kc t", p=128)
    wv = w_in.rearrange("(kc p) n -> p kc n", p=128)
    TH = TPC + 16
    ub = S.sb([128, 32, TH], BF16)
    S.dma("sp", ub[:], uv, r=["uTh"], w=[ub.k])
    dT = S.sb([128, 16, TPC], BF16)
    wb = [S.sb([128, 32, 256], BF16) for _ in range(2)]
    gb = [S.sb([128, 16, 512], BF16) for _ in range(1)]
    vt = [S.sb([128, TH], F32) for _ in range(2)]
    sb_ = [S.sb([128, TH], F32) for _ in range(2)]
    tmp = S.sb([128, TPC], F32)
    sz = [S.sb([128, 512], F32) for _ in range(2)]
    mx = [S.sb([128, 512], F32) for _ in range(2)]
    go = [S.sb([128, 512], BF16) for _ in range(2)]
    for b_ in sb_:
        S.op("dve", "memset", b_[:], 0.0, w=[b_.k])
    bs = S.sb([128, 64], F32)
    S.op("dve", "tensor_tensor", out=bs[:], in0=bgs[:, :, 0], in1=bgs[:, :, 1], op=ALU.mult, r=[bgs.k], w=[bs.k])
    ph = S.ps([128, 512], F32)
    pv = [S.ps([128, 512], F32) for _ in range(2)]
    pm = [S.ps([128, 512], F32) for _ in range(2)]
    pz = [S.ps([128, 512], F32) for _ in range(2)]
    iw = 0
    ig = 0
    io = 0
    for g in range(4):
        for blk in range(8):
            w = wb[iw % 2]
            iw += 1
            c0 = g * 2048 + blk * 256
            load_cast(S, w, wv[:, :, c0:c0 + 256], 32, 256, 2048)
            for j in range(2):
                ic = blk * 2 + j
                lw = lambda kc: w[:, kc, j * 128:(j + 1) * 128]
                for kc in range(32):
                    S.op("pe", "matmul", ph[:, 0:16], lhsT=lw(kc), rhs=ub[:, kc, 0:16], start=(kc == 0), stop=(kc == 31),
                         r=[w.k, ub.k], w=[ph.k])
                for hf in range(2):
                    for kc in range(32):
                        S.op("pe", "matmul", pv[hf][:], lhsT=lw(kc), rhs=ub[:, kc, 16 + hf * 512:16 + (hf + 1) * 512],
                             start=(kc == 0), stop=(kc == 31), r=[w.k, ub.k], w=[pv[hf].k])
                v = vt[ic % 2]
                S.op("act", "activation", out=v[:, 0:16], in_=ph[:, 0:16], func=AF.Copy, r=[ph.k], w=[v.k])
                for hf in range(2):
                    S.op("act", "activation", out=v[:, 16 + hf * 512:16 + (hf + 1) * 512], in_=pv[hf][:], func=AF.Copy,
                         r=[pv[hf].k], w=[v.k])
                cur = v
                for k in range(g + 1):
                    sh = 2 ** k
                    nx = sb_[k % 2]
                    S.op("dve", "tensor_tensor", out=nx[:, sh:TH], in0=cur[:, sh:TH], in1=cur[:, 0:TH - sh], op=ALU.add,
                         r=[cur.k], w=[nx.k])
                    cur = nx
                S.op("dve", "tensor_tensor", out=tmp[:, 0:16], in0=cur[:, 16:32], in1=rc[:, g, :], op=ALU.mult,
                     r=[cur.k, rc.k], w=[tmp.k])
                S.op("dve", "tensor_scalar", out=tmp[:, 16:TPC], in0=cur[:, 32:TH], scalar1=1.0 / 2 ** (g + 1),
                     scalar2=None, op0=ALU.mult, r=[cur.k], w=[tmp.k])
                S.op("dve", "tensor_tensor", out=dT[:, ic, :], in0=tmp[:], in1=v[:, 16:TH], op=ALU.subtract,
                     r=[tmp.k, v.k], w=[dT.k])
        wgv = w_grp[g].rearrange("(ic p) n -> p ic n", p=128)
        for ob in range(4):
            gw_ = gb[0]
            ig += 1
            load_cast(S, gw_, wgv[:, :, ob * 512:(ob + 1) * 512], 16, 512, 2048)
            for zb in range(2):
                w = wb[iw % 2]
                iw += 1
                c0 = 8192 + g * 2048 + ob * 512 + zb * 256
                load_cast(S, w, wv[:, :, c0:c0 + 256], 32, 256, 2048)
                for j in range(2):
                    oc = ob * 4 + zb * 2 + j
                    ch = g * 16 + oc
                    for hf in range(2):
                        for ic in range(16):
                            S.op("pe", "matmul", pm[hf][:], lhsT=gw_[:, ic, (zb * 2 + j) * 128:(zb * 2 + j + 1) * 128],
                                 rhs=dT[:, ic, hf * 512:(hf + 1) * 512], start=(ic == 0), stop=(ic == 15),
                                 r=[gw_.k, dT.k], w=[pm[hf].k])
                        for kc in range(32):
                            S.op("pe", "matmul", pz[hf][:], lhsT=w[:, kc, j * 128:(j + 1) * 128],
                                 rhs=ub[:, kc, 16 + hf * 512:16 + (hf + 1) * 512], start=(kc == 0), stop=(kc == 31),
                                 r=[w.k, ub.k], w=[pz[hf].k])
                        s_ = sz[io % 2]
                        m_ = mx[io % 2]
                        o_ = go[io % 2]
                        io += 1
                        S.op("act", "activation", out=s_[:], in_=pz[hf][:], func=AF.Silu, r=[pz[hf].k], w=[s_.k])
                        S.op("act", "activation", out=m_[:], in_=pm[hf][:], func=AF.Identity, scale=bgs[:, ch, 1:2],
                             bias=bs[:, ch:ch + 1], r=[pm[hf].k, bgs.k, bs.k], w=[m_.k])
                        S.op("dve", "tensor_tensor", out=o_[:], in0=m_[:], in1=s_[:], op=ALU.mult, r=[m_.k, s_.k],
                             w=[o_.k])
                        S.dma("act", gT[ch * 128:(ch + 1) * 128, hf * 512:(hf + 1) * 512], o_[:], r=[o_.k], w=[S.uk()])


def build_pool():
    nc = new_nc()
    uTh = nc.dram_tensor("uTh", [D, TPC + 16], BF16, kind="ExternalInput").ap()
    w_in = nc.dram_tensor("w_in", [D, 16384], F32, kind="ExternalInput").ap()
    w_grp = nc.dram_tensor("w_grp", [4, 2048, 2048], F32, kind="ExternalInput").ap()
    bgs_d = nc.dram_tensor("bgs", [128, 64, 2], F32, kind="ExternalInput").ap()
    rc_d = nc.dram_tensor("rc", [128, 4, 16], F32, kind="ExternalInput").ap()
    w_out = nc.dram_tensor("w_out", [8192, D], F32, kind="ExternalInput").ap()
    hT = nc.dram_tensor("hT", [D, TPC], F32, kind="ExternalInput").ap()
    gw_d = nc.dram_tensor("gw", [128, 32], F32, kind="ExternalInput").ap()
    hT2 = nc.dram_tensor("hT2", [D, TPC], F32, kind="ExternalOutput").ap()
    uT = nc.dram_tensor("uT", [D, TPC], BF16, kind="ExternalOutput").ap()
    gT = nc.dram_tensor("gT", [8192, TPC], BF16).ap()
    with ExitStack() as es:
        es.enter_context(nc.Block())
        S = Sched(nc, es)
        C = mk_consts(S)
        mk_eps(S, C)
        gw = load_small(S, gw_d, [128, 32])
        with ExitStack() as es2:
            S.es = es2
            bgs = load_small(S, bgs_d, [128, 64, 2])
            rc = load_small(S, rc_d, [128, 4, 16])
            emit_pool(S, C, uTh, w_in, w_grp, bgs, rc, gT)
        S.es = es
        S.barrier()
        with ExitStack() as es2:
            S.es = es2
            emit_outproj(S, gT, 64, w_out, hT, hT2)
        S.es = es
        S.barrier()
        with ExitStack() as es2:
            S.es = es2
            emit_norm(S, C, hT2, gw, uT, 0)
        S.es = es
        S.finish()
    return nc


def pool_core_inputs(c, p):
    bgs = np.stack([p["pool1_b_grp"].reshape(-1), p["pool1_scale"]], -1)
    bgs = np.ascontiguousarray(bgs.reshape(64, 128, 2).transpose(1, 0, 2))
    pos = c * TPC + np.arange(16)
    rc = np.stack([1.0 / np.minimum(pos + 1, w) for w in (2, 4, 8, 16)], 0).astype(np.float32)
    rc = np.ascontiguousarray(np.broadcast_to(rc[None], (128, 4, 16)))
    return dict(bgs=bgs.astype(np.float32), rc=rc)


NEGM = 30000.0


def emit_moba_inproj(S, C, uT, wq, qkw, qT, kT, gsT, vtm, NT=SEQ // 512):
    uv = uT.rearrange("(kc p) t -> p kc t", p=128)
    wv = wq.rearrange("(kc p) n -> p kc n", p=128)
    TT = 512
    ub = [S.sb([128, 32, TT], BF16) for _ in range(2)]
    wb = [S.sb([128, 32, 512], BF16) for _ in range(2)]
    sq = [S.sb([128, TT], F32) for _ in range(2)]
    rs = [S.sb([128, TT], F32) for _ in range(2)]
    shi = [S.sb([128, TT], BF16) for _ in range(2)]
    slo = [S.sb([128, TT], BF16) for _ in range(2)]
    ob = [S.sb([128, TT], BF16) for _ in range(2)]
    pb = [S.ps([128, 512], F32) for _ in range(4)]
    pn = [S.ps([128, 512], F32) for _ in range(2)]
    iu = ip = io = 0
    for bi in range(4):
        w = wb[bi % 2]
        load_cast(S, w, wv[:, :, bi * 512:(bi + 1) * 512], 32, 512)
        for tt in range(NT):
            u = ub[iu % 2]
            iu += 1
            S.dma("sp", u[:], uv[:, :, tt * TT:(tt + 1) * TT], r=["uT"], w=[u.k])
            if bi < 3:
                for hh in range(4):
                    p = pb[ip % 4]
                    ip += 1
                    for kc in range(32):
                        S.op("pe", "matmul", p[:], lhsT=w[:, kc, hh * 128:(hh + 1) * 128], rhs=u[:, kc, :],
                             start=(kc == 0), stop=(kc == 31), r=[w.k, u.k], w=[p.k])
                    o = ob[io % 2]
                    if bi < 2:
                        s_ = sq[io % 2]
                        r_ = rs[io % 2]
                        n_ = pn[io % 2]
                        S.op("act", "activation", out=s_[:], in_=p[:], func=AF.Square, r=[p.k], w=[s_.k])
                        hi_ = shi[io % 2]
                        lo_ = slo[io % 2]
                        S.op("dve", "tensor_copy", out=hi_[:], in_=s_[:], r=[s_.k], w=[hi_.k])
                        S.op("dve", "tensor_tensor", out=lo_[:], in0=s_[:], in1=hi_[:], op=ALU.subtract,
                             r=[s_.k, hi_.k], w=[lo_.k])
                        S.op("pe", "matmul", n_[:], lhsT=C["ones_b"][:], rhs=hi_[:], start=True, stop=False,
                             r=[hi_.k, "ones_b"], w=[n_.k])
                        S.op("pe", "matmul", n_[:], lhsT=C["ones_b"][:], rhs=lo_[:], start=False, stop=True,
                             r=[lo_.k, "ones_b"], w=[n_.k])
                        S.op("act", "activation", out=r_[:], in_=n_[:], func=AF.Sqrt, scale=1.0 / 128, bias=C["eps"][:],
                             r=[n_.k, "eps"], w=[r_.k])
                        S.op("dve", "reciprocal", out=r_[:], in_=r_[:], r=[r_.k], w=[r_.k])
                        S.op("dve", "scalar_tensor_tensor", out=o[:], in0=p[:], scalar=qkw[:, bi:bi + 1], in1=r_[:],
                             op0=ALU.mult, op1=ALU.mult, r=[p.k, r_.k, qkw.k], w=[o.k])
                        dst = (qT, kT)[bi]
                    else:
                        S.op("act", "activation", out=o[:], in_=p[:], func=AF.Silu, r=[p.k], w=[o.k])
                        dst = gsT
                    io += 1
                    S.dma("act", dst[hh * 128:(hh + 1) * 128, tt * TT:(tt + 1) * TT], o[:], r=[o.k], w=[S.uk()])
            else:
                for sub in range(4):
                    p = pb[ip % 4]
                    ip += 1
                    for kc in range(32):
                        S.op("pe", "matmul", p[:], lhsT=u[:, kc, sub * 128:(sub + 1) * 128], rhs=w[:, kc, :],
                             start=(kc == 0), stop=(kc == 31), r=[w.k, u.k], w=[p.k])
                    o = ob[io % 2]
                    io += 1
                    S.op("act", "activation", out=o[:], in_=p[:], func=AF.Copy, r=[p.k], w=[o.k])
                    t0 = tt * TT + sub * 128
                    S.dma("act", vtm[t0:t0 + 128, :], o[:], r=[o.k], w=[S.uk()])


def emit_moba_attn(S, C, qT, kT, gsT, vtm, yT, mc, IND, CAUS, NQT=SEQ // 512):
    ID = C["ident"]
    scale = 1.0 / np.sqrt(128.0)
    QTb = [S.sb([128, SEQ], BF16) for _ in range(2)]
    NSb = [S.sb([128, SEQ], BF16) for _ in range(2)]
    KT = S.sb([128, SEQ], BF16)
    V = S.sb([128, 64, 128], BF16)
    selw = S.sb([128, 128], BF16)
    S.op("dve", "memset", selw[:], 0.0, w=[selw.k])
    kmf = S.sb([128, 32], F32)
    km = S.sb([128, 4, 32], BF16)
    gsm = S.sb([128, 4, 32], F32)
    m8 = S.sb([128, 4, 8], F32)
    E = [S.sb([128, 512], BF16) for _ in range(3)]
    rden = S.sb([128, 512], F32)
    gt = [S.sb([128, 512], BF16) for _ in range(2)]
    of = S.sb([128, 512], F32)
    oo = [S.sb([128, 512], BF16) for _ in range(2)]
    pS = [S.ps([128, 512], F32) for _ in range(3)]
    pO = S.ps([128, 512], F32)
    pD = S.ps([128, 512], F32)
    pg = S.ps([128, 512], F32)
    pg2 = S.ps([128, 512], F32)
    S.op("pe", "matmul", pg2[:], lhsT=ID[:], rhs=CAUS[:, 0, :], start=True, stop=True, r=["ident", CAUS.k], w=[pg2.k])
    for hh in range(4):
        S.dma("sp", KT[:], kT[hh * 128:(hh + 1) * 128, :], r=["qkg"], w=[KT.k])
        S.op("dve", "tensor_reduce", out=kmf[:], in_=KT[:].rearrange("p (n j) -> p n j", j=256), axis=AX.X, op=ALU.add,
             r=[KT.k], w=[kmf.k])
        S.op("dve", "tensor_scalar", out=km[:, hh, :], in0=kmf[:], scalar1=1.0 / 256, scalar2=None, op0=ALU.mult,
             r=[kmf.k], w=[km.k])

    def load_q(hh):
        S.dma("sp", QTb[hh % 2][:], qT[hh * 128:(hh + 1) * 128, :], r=["qkg"], w=[QTb[hh % 2].k])

    def sel_group(hh, gi):
        QT = QTb[hh % 2]
        NS = NSb[hh % 2]
        for g in range(4):
            qt = gi * 4 + g
            S.op("pe", "matmul", pg[:, g * 32:(g + 1) * 32], lhsT=QT[:, qt * 128:(qt + 1) * 128], rhs=km[:, hh, :],
                 start=True, stop=True, r=[QT.k, km.k], w=[pg.k])
        S.op("dve", "tensor_tensor", out=gsm[:], in0=pg[:, 0:128].rearrange("p (g n) -> p g n", n=32),
             in1=mc[:, gi * 4:gi * 4 + 4, 0:32], op=ALU.add, r=[pg.k, mc.k], w=[gsm.k])
        for g in range(4):
            S.op("dve", "max", out=m8[:, g, :], in_=gsm[:, g, :], r=[gsm.k], w=[m8.k])
        sv = selw[:].rearrange("p (g n) -> p g n", n=32)
        S.op("dve", "tensor_tensor", out=sv, in0=gsm[:], in1=m8[:, :, 2:3].to_broadcast([128, 4, 32]), op=ALU.is_ge,
             r=[gsm.k, m8.k], w=[selw.k])
        S.op("dve", "tensor_tensor", out=sv, in0=sv, in1=mc[:, gi * 4:gi * 4 + 4, 32:64], op=ALU.mult,
             r=[selw.k, mc.k], w=[selw.k])
        S.op("dve", "tensor_tensor", out=sv, in0=sv, in1=mc[:, gi * 4:gi * 4 + 4, 64:96], op=ALU.add,
             r=[selw.k, mc.k], w=[selw.k])
        S.op("dve", "tensor_scalar", out=selw[:], in0=selw[:], scalar1=-1.0, scalar2=NEGM, op0=ALU.add,
             op1=ALU.mult, r=[selw.k], w=[selw.k])
        for g in range(4):
            S.op("pe", "matmul", pg2[0:32, g * 128:(g + 1) * 128], lhsT=selw[:, g * 32:(g + 1) * 32], rhs=ID[:],
                 start=True, stop=True, r=[selw.k, "ident"], w=[pg2.k])
        S.op("act", "activation", out=NS[:, gi * 512:(gi + 1) * 512], in_=pg2[:], func=AF.Copy, r=[pg2.k], w=[NS.k])

    load_q(0)
    for gi in range(NQT):
        sel_group(0, gi)
    ie = 0
    for hh in range(4):
        rows = slice(hh * 128, (hh + 1) * 128)
        QT = QTb[hh % 2]
        NS = NSb[hh % 2]
        S.dma("sp", KT[:], kT[rows, :], r=["qkg"], w=[KT.k])
        S.dma("sp", V[:], vtm[:, rows].rearrange("(jt p) d -> p jt d", p=128), r=["qkg"], w=[V.k])
        if hh + 1 < 4:
            load_q(hh + 1)
        tiles = [(m, kt) for m in range(NQT) for kt in range(4 * (m + 1))]
        rec = {}

        def emit_s(i):
            nonlocal ie
            m, kt = tiles[i]
            q0 = m * 512
            if kt == 0:
                S.dma("sp", gt[m % 2][:], gsT[rows, q0:q0 + 512], r=["qkg"], w=[gt[m % 2].k])
            ps = pS[ie % 3]
            e_ = E[ie % 3]
            ie += 1
            diag = kt >= 4 * m
            S.op("pe", "matmul", ps[:], lhsT=KT[:, kt * 128:(kt + 1) * 128], rhs=QT[:, q0:q0 + 512], start=True,
                 stop=False, r=[KT.k, QT.k], w=[ps.k])
            S.op("pe", "matmul", ps[:], lhsT=IND[:, kt // 2, :], rhs=NS[0:32, q0:q0 + 512], start=False,
                 stop=(not diag), r=[IND.k, NS.k], w=[ps.k])
            if diag:
                S.op("pe", "matmul", ps[:], lhsT=ID[:], rhs=CAUS[:, kt - 4 * m, :], start=False, stop=True,
                     r=["ident", CAUS.k], w=[ps.k])
            S.op("act", "activation", out=e_[:], in_=ps[:], func=AF.Exp, scale=float(scale), r=[ps.k], w=[e_.k])
            rec[i] = e_

        def emit_pv(i):
            m, kt = tiles[i]
            q0 = m * 512
            nkt = 4 * (m + 1)
            e_ = rec.pop(i)
            S.op("pe", "matmul", pO[:], lhsT=V[:, kt, :], rhs=e_[:], start=(kt == 0), stop=(kt == nkt - 1),
                 r=[V.k, e_.k], w=[pO.k])
            S.op("pe", "matmul", pD[:], lhsT=C["ones_b"][:], rhs=e_[:], start=(kt == 0), stop=(kt == nkt - 1),
                 r=["ones_b", e_.k], w=[pD.k])
            if kt == nkt - 1:
                g_ = gt[m % 2]
                S.op("dve", "reciprocal", out=rden[:], in_=pD[:], r=[pD.k], w=[rden.k])
                S.op("dve", "tensor_tensor", out=of[:], in0=pO[:], in1=rden[:], op=ALU.mult, r=[pO.k, rden.k],
                     w=[of.k])
                o_ = oo[m % 2]
                S.op("dve", "tensor_tensor", out=o_[:], in0=of[:], in1=g_[:], op=ALU.mult, r=[of.k, g_.k], w=[o_.k])
                S.dma("act", yT[rows, q0:q0 + 512], o_[:], r=[o_.k], w=[S.uk()])
                if hh + 1 < 4:
                    sel_group(hh + 1, m)

        emit_s(0)
        for i in range(len(tiles)):
            if i + 1 < len(tiles):
                emit_s(i + 1)
            emit_pv(i)


def moba_host_consts():
    n = np.arange(32)
    mc = np.zeros((128, 64, 96), np.float32)
    for qt in range(64):
        qb = qt // 2
        mc[:, qt, 0:32] = np.where(n < qb, 0.0, -1e30)[None]
        mc[:, qt, 32:64] = (n < qb).astype(np.float32)[None]
        mc[:, qt, 64:96] = (n == qb).astype(np.float32)[None]
    ind = np.zeros((32, 32, 128), np.float32)
    for k in range(32):
        ind[k, k, :] = 1.0
    p = np.arange(128)[:, None]
    i = np.arange(512)[None, :]
    caus = np.stack([np.where(kk * 128 + p <= i, 0.0, -NEGM) for kk in range(4)], 1).astype(np.float32)
    return dict(mc=mc, ind=ind.astype(NPBF), caus=np.ascontiguousarray(caus).astype(NPBF))


def build_moba(NQT=SEQ // 512, phases="ia"):
    nc = new_nc()
    uT = nc.dram_tensor("uT", [D, SEQ], BF16, kind="ExternalInput").ap()
    wq = nc.dram_tensor("wq", [D, 2048], F32, kind="ExternalInput").ap()
    qkw_d = nc.dram_tensor("qkw", [128, 2], F32, kind="ExternalInput").ap()
    cst_d = nc.dram_tensor("cst", [128, 384], F32, kind="ExternalInput").ap()
    mc_d = nc.dram_tensor("mc", [128, 64, 96], F32, kind="ExternalInput").ap()
    ind_d = nc.dram_tensor("ind", [32, 32, 128], BF16, kind="ExternalInput").ap()
    caus_d = nc.dram_tensor("caus", [128, 4, 512], BF16, kind="ExternalInput").ap()
    yT = nc.dram_tensor("yT", [512, SEQ], BF16, kind="ExternalOutput").ap()
    qT = nc.dram_tensor("qT", [512, SEQ], BF16).ap()
    kT = nc.dram_tensor("kT", [512, SEQ], BF16).ap()
    gsT = nc.dram_tensor("gsT", [512, SEQ], BF16).ap()
    vtm = nc.dram_tensor("vtm", [SEQ, 512], BF16).ap()
    with ExitStack() as es:
        es.enter_context(nc.Block())
        S = Sched(nc, es)
        C = mk_consts(S)
        mk_eps(S, C)
        load_consts(S, C, cst_d)
        C["identf"] = B(C["cst_sb"].t[:, 256:384], "cst")
        S.lastw["identf"] = S.lastw["cst"]
        qkw = load_small(S, qkw_d, [128, 2])
        with ExitStack() as es2:
            S.es = es2
            if "i" in phases:
                emit_moba_inproj(S, C, uT, wq, qkw, qT, kT, gsT, vtm, NT=NQT)
        S.es = es
        S.barrier()
        with ExitStack() as es2:
            S.es = es2
            mc = load_small(S, mc_d, [128, 64, 96])
            IND = load_small(S, ind_d, [32, 32, 128], BF16)
            CAUS = load_small(S, caus_d, [128, 4, 512], BF16)
            if "a" in phases:
                emit_moba_attn(S, C, qT, kT, gsT, vtm, yT, mc, IND, CAUS, NQT)
        S.es = es
        S.finish()
    return nc


def moba_core_inputs(c, p):
    w = p["moba2_w_in"]
    cols = slice(c * 512, (c + 1) * 512)
    wq = np.ascontiguousarray(np.concatenate([w[:, 0 * D:1 * D][:, cols], w[:, 1 * D:2 * D][:, cols],
                                              w[:, 3 * D:4 * D][:, cols], w[:, 2 * D:3 * D][:, cols]], 1))
    qkw = np.ascontiguousarray(np.stack([p["moba2_q_norm"], p["moba2_k_norm"]], 1)).astype(np.float32)
    d = dict(wq=wq, qkw=qkw, cst=host_consts())
    d.update(moba_host_consts())
    return d


_PROGS = {}


def _prog(name, fn, *a):
    key = (name,) + a
    if key not in _PROGS:
        _PROGS[key] = fn(*a)
    return _PROGS[key]


def _tok_split(full):
    return [np.ascontiguousarray(full[:, c * TPC:(c + 1) * TPC]) for c in range(NCORES)]


def run_ssd(p, pre, uT):
    uT_full = np.ascontiguousarray(np.concatenate(uT, 1))
    res = run(_prog("ssd", build_ssd), [dict(uT=uT_full, **ssd_core_inputs(g, p, pre)) for g in range(NCORES)])
    return _tok_split(np.concatenate([r["yT"] for r in res], 0))


def run_moba(p, uT):
    uT_full = np.ascontiguousarray(np.concatenate(uT, 1))
    res = run(_prog("moba", build_moba), [dict(uT=uT_full, **moba_core_inputs(c, p)) for c in range(NCORES)])
    return _tok_split(np.concatenate([r["yT"] for r in res], 0))


def run_outproj(yT, w_out, hT, gnext):
    KC = w_out.shape[0] // 128
    nc = _prog("op", build_outproj_norm, KC, gnext is not None)
    ins = []
    for c in range(NCORES):
        d = dict(yT=yT[c], w_out=w_out, hT=hT[c])
        if gnext is not None:
            d["gw"] = lay_gw(gnext)
        ins.append(d)
    res = run(nc, ins)
    return [r["hT2"] for r in res], ([r["uT"] for r in res] if gnext is not None else None)


def run_pool(p, uT, hT, gnext):
    uT_full = np.concatenate([np.zeros((D, 16), NPBF)] + list(uT), 1)
    ins = []
    for c in range(NCORES):
        d = dict(uTh=np.ascontiguousarray(uT_full[:, c * TPC:c * TPC + TPC + 16]), w_in=p["pool1_w_in"],
                 w_grp=p["pool1_w_grp"], w_out=p["pool1_w_out"], hT=hT[c], gw=lay_gw(gnext))
        d.update(pool_core_inputs(c, p))
        ins.append(d)
    res = run(_prog("pool", build_pool), ins)
    return [r["hT2"] for r in res], [r["uT"] for r in res]


def kernel(**inputs):
    p = {k: np.asarray(v) for k, v in inputs.items()}
    x = p["x"][0]
    hT = [np.ascontiguousarray(x[c * TPC:(c + 1) * TPC].T) for c in range(NCORES)]
    res = run(_prog("norm", build_norm_only), [dict(hT=hT[c], gw=lay_gw(p["norm0"])) for c in range(NCORES)])
    uT = [r["uT"] for r in res]
    yT = run_ssd(p, "ssd0", uT)
    hT, uT = run_outproj(yT, p["ssd0_w_out"], hT, p["norm1"])
    hT, uT = run_pool(p, uT, hT, p["norm2"])
    yT = run_moba(p, uT)
    hT, uT = run_outproj(yT, p["moba2_w_out"], hT, p["norm3"])
    yT = run_ssd(p, "ssd3", uT)
    hT, _ = run_outproj(yT, p["ssd3_w_out"], hT, None)
    out = np.concatenate([h.T for h in hT], 0)[None]
    return np.ascontiguousarray(out.astype(np.float32))
```

```python
import os
import numpy as np
import ml_dtypes
from contextlib import ExitStack
import concourse.bass as bass
import concourse.mybir as mybir
from concourse.bass_utils import run_bass_kernel_spmd

F32 = mybir.dt.float32
BF16 = mybir.dt.bfloat16
AF = mybir.ActivationFunctionType
ALU = mybir.AluOpType
AX = mybir.AxisListType
NPBF = ml_dtypes.bfloat16

NCORES = 8
D = 4096
SEQ = 8192
TPC = SEQ // NCORES
EPS = 1e-6
SAME_SYNC = True


class B:
    def __init__(self, t, k):
        self.t = t
        self.k = k

    def __getitem__(self, idx):
        return self.t[idx]


class Sched:
    NDMA = 32

    def __init__(self, nc, es):
        self.nc = nc
        self.es = es
        self.eng = dict(pe=nc.tensor, dve=nc.vector, act=nc.scalar, pool=nc.gpsimd, sp=nc.sync)
        self.sem = {k: es.enter_context(nc.semaphore("sem_" + k)) for k in self.eng}
        self.cnt = {k: 0 for k in self.eng}
        self.dsem = [es.enter_context(nc.semaphore(f"dsem{i}")) for i in range(self.NDMA)]
        self.duse = [0] * self.NDMA
        self.drr = 0
        self.seen = {k: {} for k in self.eng}
        self.lastw = {}
        self.readers = {}
        self.nbuf = 0

    def sb(self, shape, dtype, name=None):
        self.nbuf += 1
        name = name or f"sb{self.nbuf}"
        t = self.es.enter_context(self.nc.sbuf_tensor(name, list(shape), dtype))
        return B(t, name)

    def ps(self, shape, dtype=F32, name=None):
        self.nbuf += 1
        name = name or f"ps{self.nbuf}"
        t = self.es.enter_context(self.nc.psum_tensor(name, list(shape), dtype))
        return B(t, name)

    def _semof(self, sk):
        return self.dsem[sk[1]] if isinstance(sk, tuple) else self.sem[sk]

    def _wait(self, e, sk, val):
        if val <= 0:
            return
        if sk == e:
            if e == "pe" or e == "sp" or not SAME_SYNC:
                return
        if self.seen[e].get(sk, 0) >= val:
            return
        self.eng[e].wait_ge(self._semof(sk), val)
        self.seen[e][sk] = val

    def _deps(self, e, r, w):
        evs = {}
        for x in r:
            ev = self.lastw.get(x)
            if ev:
                evs[ev[0]] = max(evs.get(ev[0], 0), ev[1])
        for x in w:
            ev = self.lastw.get(x)
            if ev:
                evs[ev[0]] = max(evs.get(ev[0], 0), ev[1])
            for sk, v in self.readers.get(x, {}).items():
                evs[sk] = max(evs.get(sk, 0), v)
        for sk, v in evs.items():
            self._wait(e, sk, v)

    def _record(self, ev, r, w):
        for x in r:
            d = self.readers.setdefault(x, {})
            d[ev[0]] = max(d.get(ev[0], 0), ev[1])
        for x in w:
            self.lastw[x] = ev
            self.readers[x] = {}

    def op(self, e, meth, *args, r=(), w=(), **kw):
        self._deps(e, r, w)
        ins = getattr(self.eng[e], meth)(*args, **kw)
        self.cnt[e] += 1
        ins.then_inc(self.sem[e], 1)
        self._record((e, self.cnt[e]), r, w)
        return ins

    def dma(self, q, out, in_, r=(), w=()):
        k = self.drr
        self.drr = (k + 1) % self.NDMA
        self._deps(q, r, w)
        self._wait(q, ("d", k), 16 * self.duse[k])
        ins = self.eng[q].dma_start(out=out, in_=in_)
        ins.then_inc(self.dsem[k], 16)
        self.duse[k] += 1
        self._record((("d", k), 16 * self.duse[k]), r, w)
        return ins

    def uk(self):
        self.nbuf += 1
        return ("u", self.nbuf)

    def barrier(self):
        for e in ("pe", "dve", "act", "sp"):
            for k in range(self.NDMA):
                if self.duse[k]:
                    self._wait(e, ("d", k), 16 * self.duse[k])
            for e2 in ("pe", "dve", "act", "pool"):
                if e2 != e and self.cnt[e2]:
                    self._wait(e, e2, self.cnt[e2])
        self.lastw = {}
        self.readers = {}

    def finish(self):
        for k in range(self.NDMA):
            if self.duse[k]:
                self._wait("sp", ("d", k), 16 * self.duse[k])
        for e in ("pe", "dve", "act", "pool"):
            if self.cnt[e]:
                self._wait("sp", e, self.cnt[e])


def mk_consts(S):
    c = {}
    c["ones_f"] = S.sb([128, 128], F32, "ones_f")
    S.op("dve", "memset", c["ones_f"][:], 1.0, w=["ones_f"])
    c["ones_b"] = S.sb([128, 128], BF16, "ones_b")
    S.op("dve", "memset", c["ones_b"][:], 1.0, w=["ones_b"])
    return c


def emit_norm(S, C, hT, gw, uT, ucol0, T=TPC, TT=256):
    hv = hT.rearrange("(kc p) t -> p kc t", p=128)
    uv = uT.rearrange("(kc p) t -> p kc t", p=128)
    hb = [S.sb([128, 32, TT], F32) for _ in range(2)]
    ub = [S.sb([128, 32, TT], BF16) for _ in range(2)]
    sq = [S.sb([128, TT], F32) for _ in range(2)]
    shi = [S.sb([128, TT], BF16) for _ in range(2)]
    slo = [S.sb([128, TT], BF16) for _ in range(2)]
    rs = S.sb([128, TT], F32)
    pb = S.ps([128, 512], F32)
    for tt in range(T // TT):
        h = hb[tt % 2]
        u = ub[tt % 2]
        S.dma("sp", h[:], hv[:, :, tt * TT:(tt + 1) * TT], r=["hT"], w=[h.k])
        for kc in range(32):
            s = sq[kc % 2]
            S.op("act", "activation", out=s[:], in_=h[:, kc, :], func=AF.Square, r=[h.k], w=[s.k])
            hi_ = shi[kc % 2]
            lo_ = slo[kc % 2]
            S.op("dve", "tensor_copy", out=hi_[:], in_=s[:], r=[s.k], w=[hi_.k])
            S.op("dve", "tensor_tensor", out=lo_[:], in0=s[:], in1=hi_[:], op=ALU.subtract, r=[s.k, hi_.k], w=[lo_.k])
            S.op("pe", "matmul", pb[:, :TT], lhsT=C["ones_b"][:], rhs=hi_[:], start=(kc == 0), stop=False,
                 r=[hi_.k, "ones_b"], w=[pb.k])
            S.op("pe", "matmul", pb[:, :TT], lhsT=C["ones_b"][:], rhs=lo_[:], start=False, stop=(kc == 31),
                 r=[lo_.k, "ones_b"], w=[pb.k])
        S.op("act", "activation", out=rs[:], in_=pb[:, :TT], func=AF.Sqrt, scale=1.0 / D, bias=C["eps"][:],
             r=[pb.k, "eps"], w=[rs.k])
        S.op("dve", "reciprocal", out=rs[:], in_=rs[:], r=[rs.k], w=[rs.k])
        for kc in range(32):
            S.op("dve", "scalar_tensor_tensor", out=u[:, kc, :], in0=h[:, kc, :], scalar=gw[:, kc:kc + 1],
                 in1=rs[:], op0=ALU.mult, op1=ALU.mult, r=[h.k, rs.k, gw.k], w=[u.k])
        S.dma("act", uv[:, :, ucol0 + tt * TT:ucol0 + (tt + 1) * TT], u[:], r=[u.k], w=[S.uk()])


def emit_outproj(S, yT, KC, w_out, hT_in, hT_out, T=TPC):
    yv = yT.rearrange("(kc p) t -> p kc t", p=128)
    wv = w_out.rearrange("(kc p) n -> p kc n", p=128)
    yb = S.sb([128, KC, T], BF16)
    NP = 4
    KP = KC // NP
    for q in range(NP):
        S.dma("sp", yb[:, q * KP:(q + 1) * KP, :], yv[:, q * KP:(q + 1) * KP, :], r=["yT"], w=[(yb.k, q)])
    wb = [S.sb([128, KC, 128], BF16) for _ in range(2)]
    hb = [S.sb([128, T], F32) for _ in range(2)]
    pb = [S.ps([128, 512], F32) for _ in range(4)]
    ip = 0
    for nch in range(32):
        w = wb[nch % 2]
        h = hb[nch % 2]
        load_cast(S, w, wv[:, :, nch * 128:(nch + 1) * 128], KC, 128, 2048)
        S.dma("sp", h[:], hT_in[nch * 128:(nch + 1) * 128, :], r=["hT"], w=[h.k])
        for hf in range(T // 512):
            p = pb[ip % 4]
            ip += 1
            for kc in range(KC):
                S.op("pe", "matmul", p[:], lhsT=w[:, kc, :], rhs=yb[:, kc, hf * 512:(hf + 1) * 512], start=(kc == 0),
                     stop=(kc == KC - 1), r=[w.k, (yb.k, kc // KP)], w=[p.k])
            S.op("dve", "tensor_tensor", out=h[:, hf * 512:(hf + 1) * 512], in0=p[:], in1=h[:, hf * 512:(hf + 1) * 512],
                 op=ALU.add, r=[p.k, h.k], w=[h.k])
        S.dma("act", hT_out[nch * 128:(nch + 1) * 128, :], h[:], r=[h.k], w=[S.uk()])


def load_cast(S, dst, src, nk, ncols, selems=4096):
    if not hasattr(S, "stg") or S.stg_es is not S.es or S.stg_n != selems:
        S.stg = [S.sb([128, selems], F32) for _ in range(2)]
        S.stg_es = S.es
        S.stg_n = selems
        S.stg_i = 0
    per = max(1, selems // ncols)
    for k0 in range(0, nk, per):
        k1 = min(nk, k0 + per)
        st = S.stg[S.stg_i % 2]
        S.stg_i += 1
        sv = st[:, 0:(k1 - k0) * ncols].rearrange("p (k n) -> p k n", n=ncols)
        S.dma("sp", sv, src[:, k0:k1, :], w=[st.k])
        S.op("act", "activation", out=dst[:, k0:k1, 0:ncols], in_=sv, func=AF.Copy, r=[st.k], w=[dst.k])


def new_nc():
    return bass.Bass("TRN2", target_bir_lowering=False)


def load_small(S, dram_ap, shape, dtype=F32, key=None, q="sp"):
    b = S.sb(shape, dtype)
    S.dma(q, b[:], dram_ap, w=[b.k])
    return b


def mk_eps(S, C):
    C["eps"] = S.sb([128, 1], F32, "eps")
    S.op("dve", "memset", C["eps"][:], EPS, w=["eps"])


def build_norm_only():
    nc = new_nc()
    hT = nc.dram_tensor("hT", [D, TPC], F32, kind="ExternalInput").ap()
    gw_d = nc.dram_tensor("gw", [128, 32], F32, kind="ExternalInput").ap()
    uT = nc.dram_tensor("uT", [D, TPC], BF16, kind="ExternalOutput").ap()
    with ExitStack() as es:
        es.enter_context(nc.Block())
        S = Sched(nc, es)
        C = mk_consts(S)
        mk_eps(S, C)
        gw = load_small(S, gw_d, [128, 32])
        emit_norm(S, C, hT, gw, uT, 0)
        S.finish()
    return nc


def build_outproj_norm(KC, do_norm=True):
    nc = new_nc()
    yT = nc.dram_tensor("yT", [KC * 128, TPC], BF16, kind="ExternalInput").ap()
    w_out = nc.dram_tensor("w_out", [KC * 128, D], F32, kind="ExternalInput").ap()
    hT = nc.dram_tensor("hT", [D, TPC], F32, kind="ExternalInput").ap()
    hT2 = nc.dram_tensor("hT2", [D, TPC], F32, kind="ExternalOutput").ap()
    if do_norm:
        gw_d = nc.dram_tensor("gw", [128, 32], F32, kind="ExternalInput").ap()
        uT = nc.dram_tensor("uT", [D, TPC], BF16, kind="ExternalOutput").ap()
    with ExitStack() as es:
        es.enter_context(nc.Block())
        S = Sched(nc, es)
        with ExitStack() as es2:
            S.es = es2
            emit_outproj(S, yT, KC, w_out, hT, hT2)
        S.es = es
        S.barrier()
        if do_norm:
            C = mk_consts(S)
            mk_eps(S, C)
            gw = load_small(S, gw_d, [128, 32])
            emit_norm(S, C, hT2, gw, uT, 0)
        S.finish()
    return nc


_TIMES = []


def run(nc, in_maps):
    if os.environ.get("KTRACE"):
        res = run_bass_kernel_spmd(nc, in_maps, core_ids=list(range(NCORES)), trace=True)
        _TIMES.append(res.exec_time_ns)
        print("KTRACE exec_time_ns", res.exec_time_ns, flush=True)
        return res.results
    res = run_bass_kernel_spmd(nc, in_maps, core_ids=list(range(NCORES)))
    return res.results


def lay_gw(g):
    return np.ascontiguousarray(g.reshape(32, 128).T)


def bc3(ap, n, d):
    return ap.unsqueeze(2).to_broadcast([128, n, d])


def v3(ap, d):
    return ap.rearrange("p (r d) -> p r d", d=d)


def emit_ssd_inproj(S, C, uT, wg, cw, cb, dtb, nA, xbcT, zs, dta, NT=SEQ // 512):
    uv = uT.rearrange("(kc p) t -> p kc t", p=128)
    wv = wg.rearrange("(kc p) n -> p kc n", p=128)
    TT = 512
    ub = [S.sb([128, 32, TT], BF16) for _ in range(2)]
    wb = [S.sb([128, 32, 512], BF16) for _ in range(2)]
    xr = [S.sb([128, TT + 3], F32) for _ in range(4)]
    acc = [S.sb([128, TT], F32) for _ in range(2)]
    ob = [S.sb([128, TT], BF16) for _ in range(2)]
    zt = [S.sb([128, 512], F32) for _ in range(2)]
    dt1 = S.sb([128, 16], F32)
    dto = [S.sb([128, 32], F32) for _ in range(2)]
    pb = [S.ps([128, 512], F32) for _ in range(4)]
    blocks = [(0, 512, "cm"), (512, 512, "cm"), (1024, 256, "cm"), (2304, 16, "dt")]
    iu = 0
    ip = 0
    io = 0
    for bi, (c0, ncol, kind) in enumerate(blocks):
        w = wb[bi % 2]
        load_cast(S, w, wv[:, :, c0:c0 + ncol], 32, ncol)
        for tt in range(NT):
            u = ub[iu % 2]
            iu += 1
            S.dma("sp", u[:], uv[:, :, tt * TT:(tt + 1) * TT], r=["uT"], w=[u.k])
            if kind == "cm":
                for j in range(ncol // 128):
                    ch = c0 // 128 + j
                    p = pb[ip % 4]
                    ip += 1
                    for kc in range(32):
                        S.op("pe", "matmul", p[:], lhsT=w[:, kc, j * 128:(j + 1) * 128], rhs=u[:, kc, :],
                             start=(kc == 0), stop=(kc == 31), r=[w.k, u.k], w=[p.k])
                    x = xr[j]
                    if tt == 0:
                        S.op("dve", "memset", x[:, 0:3], 0.0, w=[x.k])
                    S.op("act", "activation", out=x[:, 3:TT + 3], in_=p[:], func=AF.Copy, r=[p.k], w=[x.k])
                    a = acc[io % 2]
                    o = ob[io % 2]
                    io += 1
                    S.op("act", "activation", out=a[:], in_=x[:, 3:TT + 3], func=AF.Identity,
                         scale=cw[:, ch, 3:4], bias=cb[:, ch:ch + 1], r=[x.k, cw.k, cb.k], w=[a.k])
                    for k in (2, 1, 0):
                        S.op("dve", "scalar_tensor_tensor", out=a[:], in0=x[:, k:k + TT], scalar=cw[:, ch, k:k + 1],
                             in1=a[:], op0=ALU.mult, op1=ALU.add, r=[x.k, a.k, cw.k], w=[a.k])
                    S.op("act", "activation", out=o[:], in_=a[:], func=AF.Silu, r=[a.k], w=[o.k])
                    S.op("dve", "tensor_copy", out=x[:, 0:3], in_=x[:, TT:TT + 3], r=[x.k], w=[x.k])
                    S.dma("act", xbcT[ch * 128:(ch + 1) * 128, tt * TT:(tt + 1) * TT], o[:], r=[o.k], w=[S.uk()])
            elif kind == "z":
                zc0 = c0 - 1280
                for sub in range(4):
                    p = pb[ip % 4]
                    ip += 1
                    for kc in range(32):
                        S.op("pe", "matmul", p[:], lhsT=u[:, kc, sub * 128:(sub + 1) * 128], rhs=w[:, kc, :],
                             start=(kc == 0), stop=(kc == 31), r=[w.k, u.k], w=[p.k])
                    z = zt[io % 2]
                    io += 1
                    S.op("act", "activation", out=z[:], in_=p[:], func=AF.Silu, r=[p.k], w=[z.k])
                    t0 = tt * TT + sub * 128
                    S.dma("act", zs[t0:t0 + 128, zc0:zc0 + 512], z[:], r=[z.k], w=[S.uk()])
            else:
                for sub in range(4):
                    p = pb[ip % 4]
                    ip += 1
                    for kc in range(32):
                        S.op("pe", "matmul", p[:, 0:16], lhsT=u[:, kc, sub * 128:(sub + 1) * 128], rhs=w[:, kc, 0:16],
                             start=(kc == 0), stop=(kc == 31), r=[w.k, u.k], w=[p.k])
                    d = dto[io % 2]
                    io += 1
                    S.op("dve", "tensor_tensor", out=dt1[:], in0=p[:, 0:16], in1=dtb[:], op=ALU.add,
                         r=[p.k, dtb.k], w=[dt1.k])
                    S.op("act", "activation", out=dt1[:], in_=dt1[:], func=AF.Exp, r=[dt1.k], w=[dt1.k])
                    S.op("act", "activation", out=d[:, 0:16], in_=dt1[:], func=AF.Ln, bias=C["ones_f"][:, 0:1],
                         r=[dt1.k, "ones_f"], w=[d.k])
                    S.op("dve", "tensor_tensor", out=d[:, 16:32], in0=d[:, 0:16], in1=nA[:], op=ALU.mult,
                         r=[d.k, nA.k], w=[d.k])
                    t0 = tt * TT + sub * 128
                    S.dma("act", dta[t0:t0 + 128, :], d[:], r=[d.k], w=[S.uk()])


def emit_ssd_scan(S, C, xbcT, uT, wg, dta, dsk, nw, yT, NCH=SEQ // 128):
    U = C["U"]
    NEG = C["NEG"]
    ID = C["ident"]
    xv = xbcT[0:1024, :].rearrange("(j p) t -> p j t", p=128)
    bcv = xbcT[1024:1280, :].rearrange("(j p) t -> p j t", p=128)
    uv = uT.rearrange("(kc p) t -> p kc t", p=128)
    wv = wg.rearrange("(kc p) n -> p kc n", p=128)
    wz = [S.sb([128, 32, 512], BF16) for _ in range(2)]
    for blk in range(2):
        load_cast(S, wz[blk], wv[:, :, 1280 + blk * 512:1280 + (blk + 1) * 512], 32, 512, 2048)
    ubz = [S.sb([128, 32, 128], BF16) for _ in range(2)]
    xT = [S.sb([128, 8, 128], BF16) for _ in range(2)]
    bcT = [S.sb([128, 2, 128], BF16) for _ in range(2)]
    da = [S.sb([128, 32], F32) for _ in range(2)]
    zt = [S.sb([128, 1024], F32) for _ in range(2)]
    xs = S.sb([128, 1024], F32)
    Btm = S.sb([128, 128], BF16)
    acs = S.sb([128, 32], F32)
    e32 = S.sb([128, 32], F32)
    nacs = S.sb([128, 16], F32)
    dte = S.sb([128, 16], F32)
    dd = S.sb([128, 16], F32)
    xdt = S.sb([128, 1024], BF16)
    xde = S.sb([128, 1024], BF16)
    a3 = S.sb([128, 3, 16], BF16)
    r1 = S.sb([128, 16], F32)
    a_b = S.sb([128, 48, 128], BF16)
    CBm = S.sb([128, 128], F32)
    y = S.sb([128, 1024], F32)
    tmp = S.sb([128, 1024], F32)
    tb = [S.sb([128, 512], F32) for _ in range(2)]
    dec = [S.sb([128, 512], F32) for _ in range(2)]
    MT = [S.sb([128, 512], BF16) for _ in range(2)]
    hst = S.sb([128, 1024], F32)
    prev = S.sb([128, 1024], BF16)
    ss = S.sb([128, 1], F32)
    rstd = S.sb([128, 1], F32)
    yn = S.sb([128, 1024], BF16)
    yo = [S.sb([128, 4, 128], BF16) for _ in range(2)]
    bigA = [S.ps([128, 512], F32) for _ in range(2)]
    bigB = [S.ps([128, 512], F32) for _ in range(2)]
    seg = [S.ps([128, 512], F32) for _ in range(2)]
    misc = S.ps([128, 512], F32)
    pout = misc
    pz = S.ps([128, 512], F32)

    def emit_z_load(c):
        u_ = ubz[c % 2]
        S.dma("sp", u_[:], uv[:, :, c * 128:(c + 1) * 128], r=["uT"], w=[u_.k])

    def emit_z_piece(c, i):
        u_ = ubz[c % 2]
        zz = zt[c % 2]
        blk = i // 2
        for kc in range((i % 2) * 16, (i % 2) * 16 + 16):
            S.op("pe", "matmul", pz[:], lhsT=u_[:, kc, :], rhs=wz[blk][:, kc, :], start=(kc == 0), stop=(kc == 31),
                 r=[u_.k, wz[blk].k], w=[pz.k])
        if i % 2 == 1:
            S.op("act", "activation", out=zz[:, blk * 512:(blk + 1) * 512], in_=pz[:], func=AF.Silu, r=[pz.k],
                 w=[zz.k])

    def emit_z(c):
        emit_z_load(c)
        for i in range(4):
            emit_z_piece(c, i)

    emit_z(0)
    S.op("dve", "memset", hst[:], 0.0, w=[hst.k])
    S.op("dve", "memset", prev[:], 0.0, w=[prev.k])
    for c in range(NCH):
        t0 = c * 128
        x_ = xT[c % 2]
        bc = bcT[c % 2]
        d_ = da[c % 2]
        z_ = zt[c % 2]
        S.dma("sp", x_[:], xv[:, :, t0:t0 + 128], r=["xbcT"], w=[x_.k])
        S.dma("sp", bc[:], bcv[:, :, t0:t0 + 128], r=["xbcT"], w=[bc.k])
        S.dma("sp", d_[:], dta[t0:t0 + 128, :], r=["dta"], w=[d_.k])
        if c + 1 < NCH:
            emit_z_load(c + 1)
        BT = bc[:, 0, :]
        CT = bc[:, 1, :]
        for j in range(8):
            S.op("pe", "matmul", bigA[j // 4][:, (j % 4) * 128:(j % 4 + 1) * 128], lhsT=x_[:, j, :], rhs=ID[:],
                 start=True, stop=True, r=[x_.k, "ident"], w=[bigA[j // 4].k])
        for hf in range(2):
            S.op("act", "activation", out=xs[:, hf * 512:(hf + 1) * 512], in_=bigA[hf][:], func=AF.Copy,
                 r=[bigA[hf].k], w=[xs.k])
        S.op("pe", "matmul", misc[:, 128:256], lhsT=BT, rhs=ID[:], start=True, stop=True, r=[bc.k, "ident"],
             w=[misc.k])
        S.op("act", "activation", out=Btm[:], in_=misc[:, 128:256], func=AF.Copy, r=[misc.k], w=[Btm.k])
        S.op("dve", "tensor_copy", out=a3[:, 0, :], in_=d_[:, 16:32], r=[d_.k], w=[a3.k])
        S.op("dve", "tensor_tensor", out=r1[:], in0=d_[:, 16:32], in1=a3[:, 0, :], op=ALU.subtract, r=[d_.k, a3.k],
             w=[r1.k])
        S.op("dve", "tensor_copy", out=a3[:, 1, :], in_=r1[:], r=[r1.k], w=[a3.k])
        S.op("dve", "tensor_tensor", out=r1[:], in0=r1[:], in1=a3[:, 1, :], op=ALU.subtract, r=[r1.k, a3.k], w=[r1.k])
        S.op("dve", "tensor_copy", out=a3[:, 2, :], in_=r1[:], r=[r1.k], w=[a3.k])
        for i3 in range(3):
            S.op("pe", "matmul", misc[:, 0:16], lhsT=C["U_b"][:], rhs=a3[:, i3, :], start=(i3 == 0), stop=(i3 == 2),
                 r=[a3.k, "U_b"], w=[misc.k])
        for i3 in range(3):
            S.op("pe", "matmul", misc[:, 16:32], lhsT=C["ones_b"][:], rhs=a3[:, i3, :], start=(i3 == 0),
                 stop=(i3 == 2), r=[a3.k, "ones_b"], w=[misc.k])
        S.op("dve", "tensor_copy", out=acs[:], in_=misc[:, 0:32], r=[misc.k], w=[acs.k])
        S.op("act", "activation", out=e32[:], in_=acs[:], func=AF.Exp, r=[acs.k], w=[e32.k])
        S.op("dve", "tensor_scalar", out=nacs[:], in0=acs[:, 0:16], scalar1=-1.0, scalar2=None, op0=ALU.mult,
             r=[acs.k], w=[nacs.k])
        S.op("dve", "tensor_tensor", out=dte[:], in0=acs[:, 16:32], in1=acs[:, 0:16], op=ALU.subtract, r=[acs.k],
             w=[dte.k])
        S.op("act", "activation", out=dte[:], in_=dte[:], func=AF.Exp, r=[dte.k], w=[dte.k])
        S.op("dve", "tensor_tensor", out=dd[:], in0=dte[:], in1=d_[:, 0:16], op=ALU.mult, r=[dte.k, d_.k], w=[dd.k])
        S.op("dve", "tensor_tensor", out=v3(xdt[:], 64), in0=v3(xs[:], 64), in1=bc3(d_[:, 0:16], 16, 64), op=ALU.mult,
             r=[xs.k, d_.k], w=[xdt.k])
        S.op("dve", "tensor_tensor", out=v3(xde[:], 64), in0=v3(xs[:], 64), in1=bc3(dd[:], 16, 64), op=ALU.mult,
             r=[xs.k, dd.k], w=[xde.k])
        S.op("dve", "tensor_copy", out=a_b[:], in_=bc3(a3[:].rearrange("p i r -> p (i r)"), 48, 128), r=[a3.k],
             w=[a_b.k])
        S.op("pe", "matmul", misc[:, 256:384], lhsT=BT, rhs=CT, start=True, stop=True, r=[bc.k], w=[misc.k])
        S.op("dve", "tensor_tensor", out=CBm[:], in0=misc[:, 256:384], in1=U[:], op=ALU.mult, r=[misc.k, "U"],
             w=[CBm.k])
        for hf in range(2):
            S.op("pe", "matmul", bigB[hf][:], lhsT=CT, rhs=prev[:, hf * 512:(hf + 1) * 512], start=True, stop=True,
                 r=[bc.k, prev.k], w=[bigB[hf].k])
            S.op("dve", "tensor_tensor", out=v3(y[:, hf * 512:(hf + 1) * 512], 64), in0=v3(bigB[hf][:], 64),
                 in1=bc3(e32[:, hf * 8:hf * 8 + 8], 8, 64), op=ALU.mult, r=[bigB[hf].k, e32.k], w=[y.k])
        def emit_seg(hg):
            sg = seg[hg % 2]
            for q in range(4):
                r_ = hg * 4 + q
                for i3 in range(3):
                    S.op("pe", "matmul", sg[:, q * 128:(q + 1) * 128], lhsT=a_b[:, i3 * 16 + r_, :], rhs=C["U_b"][:],
                         start=(i3 == 0), stop=(i3 == 2), r=[a_b.k, "U_b"], w=[sg.k])

        emit_seg(0)
        for hg in range(4):
            sg = seg[hg % 2]
            t_ = tb[hg % 2]
            de = dec[hg % 2]
            m_ = MT[hg % 2]
            if hg + 1 < 4:
                emit_seg(hg + 1)
            if c + 1 < NCH:
                emit_z_piece(c + 1, hg)
            for q in range(4):
                r_ = hg * 4 + q
                S.op("dve", "scalar_tensor_tensor", out=t_[:, q * 128:(q + 1) * 128], in0=sg[:, q * 128:(q + 1) * 128],
                     scalar=nacs[:, r_:r_ + 1], in1=NEG[:], op0=ALU.add, op1=ALU.add, r=[sg.k, nacs.k, "NEG"],
                     w=[t_.k])
            S.op("act", "activation", out=de[:], in_=t_[:], func=AF.Exp, r=[t_.k], w=[de.k])
            S.op("dve", "tensor_tensor", out=v3(m_[:], 128), in0=v3(de[:], 128),
                 in1=CBm[:].unsqueeze(1).to_broadcast([128, 4, 128]), op=ALU.mult, r=[de.k, CBm.k], w=[m_.k])
            for q in range(4):
                r_ = hg * 4 + q
                S.op("pe", "matmul", bigA[r_ // 8][:, (r_ % 8) * 64:(r_ % 8 + 1) * 64], lhsT=m_[:, q * 128:(q + 1) * 128],
                     rhs=xdt[:, r_ * 64:(r_ + 1) * 64], start=True, stop=True, r=[m_.k, xdt.k], w=[bigA[r_ // 8].k])
        for hf in range(2):
            S.op("dve", "tensor_tensor", out=y[:, hf * 512:(hf + 1) * 512], in0=bigA[hf][:],
                 in1=y[:, hf * 512:(hf + 1) * 512], op=ALU.add, r=[bigA[hf].k, y.k], w=[y.k])
        S.op("dve", "tensor_tensor", out=v3(tmp[:], 64), in0=v3(xs[:], 64), in1=bc3(dsk[:], 16, 64), op=ALU.mult,
             r=[xs.k, dsk.k], w=[tmp.k])
        S.op("dve", "tensor_tensor", out=y[:], in0=y[:], in1=tmp[:], op=ALU.add, r=[y.k, tmp.k], w=[y.k])
        for hf in range(2):
            S.op("pe", "matmul", bigB[hf][:], lhsT=Btm[:], rhs=xde[:, hf * 512:(hf + 1) * 512], start=True, stop=True,
                 r=[Btm.k, xde.k], w=[bigB[hf].k])
        S.op("dve", "tensor_tensor", out=v3(hst[:], 64), in0=v3(hst[:], 64), in1=bc3(e32[:, 16:32], 16, 64),
             op=ALU.mult, r=[hst.k, e32.k], w=[hst.k])
        for hf in range(2):
            S.op("dve", "tensor_tensor", out=hst[:, hf * 512:(hf + 1) * 512], in0=bigB[hf][:],
                 in1=hst[:, hf * 512:(hf + 1) * 512], op=ALU.add, r=[bigB[hf].k, hst.k], w=[hst.k])
        S.op("act", "activation", out=prev[:], in_=hst[:], func=AF.Copy, r=[hst.k], w=[prev.k])
        S.op("dve", "tensor_tensor", out=y[:], in0=y[:], in1=z_[:], op=ALU.mult, r=[y.k, z_.k], w=[y.k])
        S.op("dve", "memset", ss[:], 0.0, w=[ss.k])
        S.op("act", "activation", out=tmp[:], in_=y[:], func=AF.Square, accum_out=ss[:], r=[y.k, ss.k], w=[tmp.k, ss.k])
        S.op("act", "activation", out=rstd[:], in_=ss[:], func=AF.Sqrt, scale=1.0 / 1024, bias=C["eps"][:],
             r=[ss.k, "eps"], w=[rstd.k])
        S.op("dve", "reciprocal", out=rstd[:], in_=rstd[:], r=[rstd.k], w=[rstd.k])
        S.op("dve", "scalar_tensor_tensor", out=yn[:], in0=y[:], scalar=rstd[:, 0:1], in1=nw[:], op0=ALU.mult,
             op1=ALU.mult, r=[y.k, rstd.k, nw.k], w=[yn.k])
        for q in range(2):
            o = yo[q]
            for j in range(4):
                S.op("pe", "matmul", pout[:, j * 128:(j + 1) * 128], lhsT=yn[:, (q * 4 + j) * 128:(q * 4 + j + 1) * 128],
                     rhs=ID[:], start=True, stop=True, r=[yn.k, "ident"], w=[pout.k])
            S.op("act", "activation", out=o[:], in_=v3(pout[:], 128), func=AF.Copy, r=[pout.k], w=[o.k])
            S.dma("act", yT[q * 512:(q + 1) * 512, :].rearrange("(j p) t -> p j t", p=128)[:, :, t0:t0 + 128], o[:],
                  r=[o.k], w=[S.uk()])


def load_consts(S, C, cst_d):
    cst = S.sb([128, 384], F32, "cst_sb")
    S.dma("sp", cst[:], cst_d, w=["cst"])
    C["U"] = B(cst.t[:, 0:128], "cst")
    C["NEG"] = B(cst.t[:, 128:256], "cst")
    C["cst_sb"] = cst
    idb = S.sb([128, 128], BF16, "ident")
    S.op("dve", "tensor_copy", out=idb[:], in_=cst[:, 256:384], r=["cst"], w=["ident"])
    C["ident"] = idb
    ub_ = S.sb([128, 128], BF16, "U_b")
    S.op("dve", "tensor_copy", out=ub_[:], in_=cst[:, 0:128], r=["cst"], w=["U_b"])
    C["U_b"] = ub_
    S.lastw["U"] = S.lastw["cst"]
    S.lastw["NEG"] = S.lastw["cst"]


def host_consts():
    j = np.arange(128)
    U = (j[:, None] <= j[None, :]).astype(np.float32)
    NEG = (U - 1.0) * 1e4
    I = np.eye(128, dtype=np.float32)
    return np.ascontiguousarray(np.concatenate([U, NEG, I], 1))


def build_ssd(NCH=SEQ // 128, phases="is"):
    nc = new_nc()
    uT = nc.dram_tensor("uT", [D, SEQ], BF16, kind="ExternalInput").ap()
    wg = nc.dram_tensor("wg", [D, 2320], F32, kind="ExternalInput").ap()
    cw_d = nc.dram_tensor("cw", [128, 10, 4], F32, kind="ExternalInput").ap()
    cb_d = nc.dram_tensor("cb", [128, 10], F32, kind="ExternalInput").ap()
    sm_d = nc.dram_tensor("sm", [128, 48], F32, kind="ExternalInput").ap()
    nw_d = nc.dram_tensor("nw", [128, 1024], F32, kind="ExternalInput").ap()
    cst_d = nc.dram_tensor("cst", [128, 384], F32, kind="ExternalInput").ap()
    yT = nc.dram_tensor("yT", [1024, SEQ], BF16, kind="ExternalOutput").ap()
    xbcT = nc.dram_tensor("xbcT", [1280, SEQ], BF16).ap()
    zs = nc.dram_tensor("zs", [SEQ, 1024], F32).ap()
    dta = nc.dram_tensor("dta", [SEQ, 32], F32).ap()
    with ExitStack() as es:
        es.enter_context(nc.Block())
        S = Sched(nc, es)
        C = mk_consts(S)
        mk_eps(S, C)
        load_consts(S, C, cst_d)
        cw = load_small(S, cw_d, [128, 10, 4])
        cb = load_small(S, cb_d, [128, 10])
        sm = load_small(S, sm_d, [128, 48])
        nw = load_small(S, nw_d, [128, 1024])
        nA = S.sb([128, 16], F32)
        S.op("act", "activation", out=nA[:], in_=sm[:, 16:32], func=AF.Exp, r=[sm.k], w=[nA.k])
        S.op("dve", "tensor_scalar", out=nA[:], in0=nA[:], scalar1=-1.0, scalar2=None, op0=ALU.mult, r=[nA.k], w=[nA.k])
        dtb = B(sm.t[:, 0:16], sm.k)
        dsk = B(sm.t[:, 32:48], sm.k)
        with ExitStack() as es2:
            S.es = es2
            if "i" in phases:
                emit_ssd_inproj(S, C, uT, wg, cw, cb, dtb, nA, xbcT, zs, dta, NT=NCH // 4)
        S.es = es
        S.barrier()
        with ExitStack() as es2:
            S.es = es2
            if "s" in phases:
                emit_ssd_scan(S, C, xbcT, uT, wg, dta, dsk, nw, yT, NCH)
        S.es = es
        S.finish()
    return nc


def ssd_core_inputs(g, p, pre):
    w_in = p[pre + "_w_in"]
    DI = 8192
    xs = slice(DI + g * 1024, DI + (g + 1) * 1024)
    bs = slice(2 * DI + g * 128, 2 * DI + (g + 1) * 128)
    cs = slice(2 * DI + 1024 + g * 128, 2 * DI + 1024 + (g + 1) * 128)
    z_ = slice(g * 1024, (g + 1) * 1024)
    dts = slice(2 * DI + 2048 + g * 16, 2 * DI + 2048 + (g + 1) * 16)
    wg = np.ascontiguousarray(np.concatenate([w_in[:, xs], w_in[:, bs], w_in[:, cs], w_in[:, z_], w_in[:, dts]], 1))
    ccols = np.concatenate([np.arange(g * 1024, (g + 1) * 1024), 8192 + np.arange(g * 128, (g + 1) * 128),
                            8192 + 1024 + np.arange(g * 128, (g + 1) * 128)])
    cwf = p[pre + "_conv_w"][:, ccols]
    cw = np.ascontiguousarray(cwf.T.reshape(10, 128, 4).transpose(1, 0, 2))
    cb = np.ascontiguousarray(p[pre + "_conv_b"][ccols].reshape(10, 128).T)
    hs = slice(g * 16, (g + 1) * 16)
    sm = np.concatenate([p[pre + "_dt_bias"][hs], p[pre + "_a_log"][hs], p[pre + "_d"][hs]])[None, :]
    sm = np.ascontiguousarray(np.broadcast_to(sm, (128, 48))).astype(np.float32)
    nw = np.ascontiguousarray(np.broadcast_to(p[pre + "_norm_w"][None, g * 1024:(g + 1) * 1024], (128, 1024)))
    return dict(wg=wg, cw=cw, cb=cb, sm=sm, nw=nw, cst=host_consts())


def emit_pool(S, C, uTh, w_in, w_grp, bgs, rc, gT):
    uv = uTh.rearrange("(kc p) t -> p kc t", p=128)
    wv = w_in.rearrange("(kc p) n -> p kc n", p=128)
    TH = TPC + 16
    ub = S.sb([128, 32, TH], BF16)
    S.dma("sp", ub[:], uv, r=["uTh"], w=[ub.k])
    dT = S.sb([128, 16, TPC], BF16)
    wb = [S.sb([128, 32, 256], BF16) for _ in range(2)]
    gb = [S.sb([128, 16, 512], BF16) for _ in range(1)]
    vt = [S.sb([128, TH], F32) for _ in range(2)]
    sb_ = [S.sb([128, TH], F32) for _ in range(2)]
    tmp = S.sb([128, TPC], F32)
    sz = [S.sb([128, 512], F32) for _ in range(2)]
    mx = [S.sb([128, 512], F32) for _ in range(2)]
    go = [S.sb([128, 512], BF16) for _ in range(2)]
    for b_ in sb_:
        S.op("dve", "memset", b_[:], 0.0, w=[b_.k])
    bs = S.sb([128, 64], F32)
    S.op("dve", "tensor_tensor", out=bs[:], in0=bgs[:, :, 0], in1=bgs[:, :, 1], op=ALU.mult, r=[bgs.k], w=[bs.k])
    ph = S.ps([128, 512], F32)
    pv = [S.ps([128, 512], F32) for _ in range(2)]
    pm = [S.ps([128, 512], F32) for _ in range(2)]
    pz = [S.ps([128, 512], F32) for _ in range(2)]
    iw = 0
    ig = 0
    io = 0
    for g in range(4):
        for blk in range(8):
            w = wb[iw % 2]
            iw += 1
            c0 = g * 2048 + blk * 256
            load_cast(S, w, wv[:, :, c0:c0 + 256], 32, 256, 2048)
            for j in range(2):
                ic = blk * 2 + j
                lw = lambda kc: w[:, kc, j * 128:(j + 1) * 128]
                for kc in range(32):
                    S.op("pe", "matmul", ph[:, 0:16], lhsT=lw(kc), rhs=ub[:, kc, 0:16], start=(kc == 0), stop=(kc == 31),
                         r=[w.k, ub.k], w=[ph.k])
                for hf in range(2):
                    for kc in range(32):
                        S.op("pe", "matmul", pv[hf][:], lhsT=lw(kc), rhs=ub[:, kc, 16 + hf * 512:16 + (hf + 1) * 512],
                             start=(kc == 0), stop=(kc == 31), r=[w.k, ub.k], w=[pv[hf].k])
                v = vt[ic % 2]
                S.op("act", "activation", out=v[:, 0:16], in_=ph[:, 0:16], func=AF.Copy, r=[ph.k], w=[v.k])
                for hf in range(2):
                    S.op("act", "activation", out=v[:, 16 + hf * 512:16 + (hf + 1) * 512], in_=pv[hf][:], func=AF.Copy,
                         r=[pv[hf].k], w=[v.k])
                cur = v
                for k in range(g + 1):
                    sh = 2 ** k
                    nx = sb_[k % 2]
                    S.op("dve", "tensor_tensor", out=nx[:, sh:TH], in0=cur[:, sh:TH], in1=cur[:, 0:TH - sh], op=ALU.add,
                         r=[cur.k], w=[nx.k])
                    cur = nx
                S.op("dve", "tensor_tensor", out=tmp[:, 0:16], in0=cur[:, 16:32], in1=rc[:, g, :], op=ALU.mult,
                     r=[cur.k, rc.k], w=[tmp.k])
                S.op("dve", "tensor_scalar", out=tmp[:, 16:TPC], in0=cur[:, 32:TH], scalar1=1.0 / 2 ** (g + 1),
                     scalar2=None, op0=ALU.mult, r=[cur.k], w=[tmp.k])
                S.op("dve", "tensor_tensor", out=dT[:, ic, :], in0=tmp[:], in1=v[:, 16:TH], op=ALU.subtract,
                     r=[tmp.k, v.k], w=[dT.k])
        wgv = w_grp[g].rearrange("(ic p) n -> p ic n", p=128)
        for ob in range(4):
            gw_ = gb[0]
            ig += 1
            load_cast(S, gw_, wgv[:, :, ob * 512:(ob + 1) * 512], 16, 512, 2048)
            for zb in range(2):
                w = wb[iw % 2]
                iw += 1
                c0 = 8192 + g * 2048 + ob * 512 + zb * 256
                load_cast(S, w, wv[:, :, c0:c0 + 256], 32, 256, 2048)
                for j in range(2):
                    oc = ob * 4 + zb * 2 + j
                    ch = g * 16 + oc
                    for hf in range(2):
                        for ic in range(16):
                            S.op("pe", "matmul", pm[hf][:], lhsT=gw_[:, ic, (zb * 2 + j) * 128:(zb * 2 + j + 1) * 128],
                                 rhs=dT[:, ic, hf * 512:(hf + 1) * 512], start=(ic == 0), stop=(ic == 15),
                                 r=[gw_.k, dT.k], w=[pm[hf].k])
                        for kc in range(32):
                            S.op("pe", "matmul", pz[hf][:], lhsT=w[:, kc, j * 128:(j + 1) * 128],
                                 rhs=ub[:, kc, 16 + hf * 512:16 + (hf + 1) * 512], start=(kc == 0), stop=(kc == 31),
                                 r=[w.k, ub.k], w=[pz[hf].k])
                        s_ = sz[io % 2]
                        m_ = mx[io % 2]
                        o_ = go[io % 2]
                        io += 1
                        S.op("act", "activation", out=s_[:], in_=pz[hf][:], func=AF.Silu, r=[pz[hf].k], w=[s_.k])
                        S.op("act", "activation", out=m_[:], in_=pm[hf][:], func=AF.Identity, scale=bgs[:, ch, 1:2],
                             bias=bs[:, ch:ch + 1], r=[pm[hf].k, bgs.k, bs.k], w=[m_.k])
                        S.op("dve", "tensor_tensor", out=o_[:], in0=m_[:], in1=s_[:], op=ALU.mult, r=[m_.k, s_.k],
                             w=[o_.k])
                        S.dma("act", gT[ch * 128:(ch + 1) * 128, hf * 512:(hf + 1) * 512], o_[:], r=[o_.k], w=[S.uk()])


def build_pool():
    nc = new_nc()
    uTh = nc.dram_tensor("uTh", [D, TPC + 16], BF16, kind="ExternalInput").ap()
    w_in = nc.dram_tensor("w_in", [D, 16384], F32, kind="ExternalInput").ap()
    w_grp = nc.dram_tensor("w_grp", [4, 2048, 2048], F32, kind="ExternalInput").ap()
    bgs_d = nc.dram_tensor("bgs", [128, 64, 2], F32, kind="ExternalInput").ap()
    rc_d = nc.dram_tensor("rc", [128, 4, 16], F32, kind="ExternalInput").ap()
    w_out = nc.dram_tensor("w_out", [8192, D], F32, kind="ExternalInput").ap()
    hT = nc.dram_tensor("hT", [D, TPC], F32, kind="ExternalInput").ap()
    gw_d = nc.dram_tensor("gw", [128, 32], F32, kind="ExternalInput").ap()
    hT2 = nc.dram_tensor("hT2", [D, TPC], F32, kind="ExternalOutput").ap()
    uT = nc.dram_tensor("uT", [D, TPC], BF16, kind="ExternalOutput").ap()
    gT = nc.dram_tensor("gT", [8192, TPC], BF16).ap()
    with ExitStack() as es:
        es.enter_context(nc.Block())
        S = Sched(nc, es)
        C = mk_consts(S)
        mk_eps(S, C)
        gw = load_small(S, gw_d, [128, 32])
        with ExitStack() as es2:
            S.es = es2
            bgs = load_small(S, bgs_d, [128, 64, 2])
            rc = load_small(S, rc_d, [128, 4, 16])
            emit_pool(S, C, uTh, w_in, w_grp, bgs, rc, gT)
        S.es = es
        S.barrier()
        with ExitStack() as es2:
            S.es = es2
            emit_outproj(S, gT, 64, w_out, hT, hT2)
        S.es = es
        S.barrier()
        with ExitStack() as es2:
            S.es = es2
            emit_norm(S, C, hT2, gw, uT, 0)
        S.es = es
        S.finish()
    return nc


def pool_core_inputs(c, p):
    bgs = np.stack([p["pool1_b_grp"].reshape(-1), p["pool1_scale"]], -1)
    bgs = np.ascontiguousarray(bgs.reshape(64, 128, 2).transpose(1, 0, 2))
    pos = c * TPC + np.arange(16)
    rc = np.stack([1.0 / np.minimum(pos + 1, w) for w in (2, 4, 8, 16)], 0).astype(np.float32)
    rc = np.ascontiguousarray(np.broadcast_to(rc[None], (128, 4, 16)))
    return dict(bgs=bgs.astype(np.float32), rc=rc)


NEGM = 30000.0


def emit_moba_inproj(S, C, uT, wq, qkw, qT, kT, gsT, vtm, NT=SEQ // 512):
    uv = uT.rearrange("(kc p) t -> p kc t", p=128)
    wv = wq.rearrange("(kc p) n -> p kc n", p=128)
    TT = 512
    ub = [S.sb([128, 32, TT], BF16) for _ in range(2)]
    wb = [S.sb([128, 32, 512], BF16) for _ in range(2)]
    sq = [S.sb([128, TT], F32) for _ in range(2)]
    rs = [S.sb([128, TT], F32) for _ in range(2)]
    shi = [S.sb([128, TT], BF16) for _ in range(2)]
    slo = [S.sb([128, TT], BF16) for _ in range(2)]
    ob = [S.sb([128, TT], BF16) for _ in range(2)]
    pb = [S.ps([128, 512], F32) for _ in range(4)]
    pn = [S.ps([128, 512], F32) for _ in range(2)]
    iu = ip = io = 0
    for bi in range(4):
        w = wb[bi % 2]
        load_cast(S, w, wv[:, :, bi * 512:(bi + 1) * 512], 32, 512)
        for tt in range(NT):
            u = ub[iu % 2]
            iu += 1
            S.dma("sp", u[:], uv[:, :, tt * TT:(tt + 1) * TT], r=["uT"], w=[u.k])
            if bi < 3:
                for hh in range(4):
                    p = pb[ip % 4]
                    ip += 1
                    for kc in range(32):
                        S.op("pe", "matmul", p[:], lhsT=w[:, kc, hh * 128:(hh + 1) * 128], rhs=u[:, kc, :],
                             start=(kc == 0), stop=(kc == 31), r=[w.k, u.k], w=[p.k])
                    o = ob[io % 2]
                    if bi < 2:
                        s_ = sq[io % 2]
                        r_ = rs[io % 2]
                        n_ = pn[io % 2]
                        S.op("act", "activation", out=s_[:], in_=p[:], func=AF.Square, r=[p.k], w=[s_.k])
                        hi_ = shi[io % 2]
                        lo_ = slo[io % 2]
                        S.op("dve", "tensor_copy", out=hi_[:], in_=s_[:], r=[s_.k], w=[hi_.k])
                        S.op("dve", "tensor_tensor", out=lo_[:], in0=s_[:], in1=hi_[:], op=ALU.subtract,
                             r=[s_.k, hi_.k], w=[lo_.k])
                        S.op("pe", "matmul", n_[:], lhsT=C["ones_b"][:], rhs=hi_[:], start=True, stop=False,
                             r=[hi_.k, "ones_b"], w=[n_.k])
                        S.op("pe", "matmul", n_[:], lhsT=C["ones_b"][:], rhs=lo_[:], start=False, stop=True,
                             r=[lo_.k, "ones_b"], w=[n_.k])
                        S.op("act", "activation", out=r_[:], in_=n_[:], func=AF.Sqrt, scale=1.0 / 128, bias=C["eps"][:],
                             r=[n_.k, "eps"], w=[r_.k])
                        S.op("dve", "reciprocal", out=r_[:], in_=r_[:], r=[r_.k], w=[r_.k])
                        S.op("dve", "scalar_tensor_tensor", out=o[:], in0=p[:], scalar=qkw[:, bi:bi + 1], in1=r_[:],
                             op0=ALU.mult, op1=ALU.mult, r=[p.k, r_.k, qkw.k], w=[o.k])
                        dst = (qT, kT)[bi]
                    else:
                        S.op("act", "activation", out=o[:], in_=p[:], func=AF.Silu, r=[p.k], w=[o.k])
                        dst = gsT
                    io += 1
                    S.dma("act", dst[hh * 128:(hh + 1) * 128, tt * TT:(tt + 1) * TT], o[:], r=[o.k], w=[S.uk()])
            else:
                for sub in range(4):
                    p = pb[ip % 4]
                    ip += 1
                    for kc in range(32):
                        S.op("pe", "matmul", p[:], lhsT=u[:, kc, sub * 128:(sub + 1) * 128], rhs=w[:, kc, :],
                             start=(kc == 0), stop=(kc == 31), r=[w.k, u.k], w=[p.k])
                    o = ob[io % 2]
                    io += 1
                    S.op("act", "activation", out=o[:], in_=p[:], func=AF.Copy, r=[p.k], w=[o.k])
                    t0 = tt * TT + sub * 128
                    S.dma("act", vtm[t0:t0 + 128, :], o[:], r=[o.k], w=[S.uk()])


def emit_moba_attn(S, C, qT, kT, gsT, vtm, yT, mc, IND, CAUS, NQT=SEQ // 512):
    ID = C["ident"]
    scale = 1.0 / np.sqrt(128.0)
    QTb = [S.sb([128, SEQ], BF16) for _ in range(2)]
    NSb = [S.sb([128, SEQ], BF16) for _ in range(2)]
    KT = S.sb([128, SEQ], BF16)
    V = S.sb([128, 64, 128], BF16)
    selw = S.sb([128, 128], BF16)
    S.op("dve", "memset", selw[:], 0.0, w=[selw.k])
    kmf = S.sb([128, 32], F32)
    km = S.sb([128, 4, 32], BF16)
    gsm = S.sb([128, 4, 32], F32)
    m8 = S.sb([128, 4, 8], F32)
    E = [S.sb([128, 512], BF16) for _ in range(3)]
    rden = S.sb([128, 512], F32)
    gt = [S.sb([128, 512], BF16) for _ in range(2)]
    of = S.sb([128, 512], F32)
    oo = [S.sb([128, 512], BF16) for _ in range(2)]
    pS = [S.ps([128, 512], F32) for _ in range(3)]
    pO = S.ps([128, 512], F32)
    pD = S.ps([128, 512], F32)
    pg = S.ps([128, 512], F32)
    pg2 = S.ps([128, 512], F32)
    S.op("pe", "matmul", pg2[:], lhsT=ID[:], rhs=CAUS[:, 0, :], start=True, stop=True, r=["ident", CAUS.k], w=[pg2.k])
    for hh in range(4):
        S.dma("sp", KT[:], kT[hh * 128:(hh + 1) * 128, :], r=["qkg"], w=[KT.k])
        S.op("dve", "tensor_reduce", out=kmf[:], in_=KT[:].rearrange("p (n j) -> p n j", j=256), axis=AX.X, op=ALU.add,
             r=[KT.k], w=[kmf.k])
        S.op("dve", "tensor_scalar", out=km[:, hh, :], in0=kmf[:], scalar1=1.0 / 256, scalar2=None, op0=ALU.mult,
             r=[kmf.k], w=[km.k])

    def load_q(hh):
        S.dma("sp", QTb[hh % 2][:], qT[hh * 128:(hh + 1) * 128, :], r=["qkg"], w=[QTb[hh % 2].k])

    def sel_group(hh, gi):
        QT = QTb[hh % 2]
        NS = NSb[hh % 2]
        for g in range(4):
            qt = gi * 4 + g
            S.op("pe", "matmul", pg[:, g * 32:(g + 1) * 32], lhsT=QT[:, qt * 128:(qt + 1) * 128], rhs=km[:, hh, :],
                 start=True, stop=True, r=[QT.k, km.k], w=[pg.k])
        S.op("dve", "tensor_tensor", out=gsm[:], in0=pg[:, 0:128].rearrange("p (g n) -> p g n", n=32),
             in1=mc[:, gi * 4:gi * 4 + 4, 0:32], op=ALU.add, r=[pg.k, mc.k], w=[gsm.k])
        for g in range(4):
            S.op("dve", "max", out=m8[:, g, :], in_=gsm[:, g, :], r=[gsm.k], w=[m8.k])
        sv = selw[:].rearrange("p (g n) -> p g n", n=32)
        S.op("dve", "tensor_tensor", out=sv, in0=gsm[:], in1=m8[:, :, 2:3].to_broadcast([128, 4, 32]), op=ALU.is_ge,
             r=[gsm.k, m8.k], w=[selw.k])
        S.op("dve", "tensor_tensor", out=sv, in0=sv, in1=mc[:, gi * 4:gi * 4 + 4, 32:64], op=ALU.mult,
             r=[selw.k, mc.k], w=[selw.k])
        S.op("dve", "tensor_tensor", out=sv, in0=sv, in1=mc[:, gi * 4:gi * 4 + 4, 64:96], op=ALU.add,
             r=[selw.k, mc.k], w=[selw.k])
        S.op("dve", "tensor_scalar", out=selw[:], in0=selw[:], scalar1=-1.0, scalar2=NEGM, op0=ALU.add,
             op1=ALU.mult, r=[selw.k], w=[selw.k])
        for g in range(4):
            S.op("pe", "matmul", pg2[0:32, g * 128:(g + 1) * 128], lhsT=selw[:, g * 32:(g + 1) * 32], rhs=ID[:],
                 start=True, stop=True, r=[selw.k, "ident"], w=[pg2.k])
        S.op("act", "activation", out=NS[:, gi * 512:(gi + 1) * 512], in_=pg2[:], func=AF.Copy, r=[pg2.k], w=[NS.k])

    load_q(0)
    for gi in range(NQT):
        sel_group(0, gi)
    ie = 0
    for hh in range(4):
        rows = slice(hh * 128, (hh + 1) * 128)
        QT = QTb[hh % 2]
        NS = NSb[hh % 2]
        S.dma("sp", KT[:], kT[rows, :], r=["qkg"], w=[KT.k])
        S.dma("sp", V[:], vtm[:, rows].rearrange("(jt p) d -> p jt d", p=128), r=["qkg"], w=[V.k])
        if hh + 1 < 4:
            load_q(hh + 1)
        tiles = [(m, kt) for m in range(NQT) for kt in range(4 * (m + 1))]
        rec = {}

        def emit_s(i):
            nonlocal ie
            m, kt = tiles[i]
            q0 = m * 512
            if kt == 0:
                S.dma("sp", gt[m % 2][:], gsT[rows, q0:q0 + 512], r=["qkg"], w=[gt[m % 2].k])
            ps = pS[ie % 3]
            e_ = E[ie % 3]
            ie += 1
            diag = kt >= 4 * m
            S.op("pe", "matmul", ps[:], lhsT=KT[:, kt * 128:(kt + 1) * 128], rhs=QT[:, q0:q0 + 512], start=True,
                 stop=False, r=[KT.k, QT.k], w=[ps.k])
            S.op("pe", "matmul", ps[:], lhsT=IND[:, kt // 2, :], rhs=NS[0:32, q0:q0 + 512], start=False,
                 stop=(not diag), r=[IND.k, NS.k], w=[ps.k])
            if diag:
                S.op("pe", "matmul", ps[:], lhsT=ID[:], rhs=CAUS[:, kt - 4 * m, :], start=False, stop=True,
                     r=["ident", CAUS.k], w=[ps.k])
            S.op("act", "activation", out=e_[:], in_=ps[:], func=AF.Exp, scale=float(scale), r=[ps.k], w=[e_.k])
            rec[i] = e_

        def emit_pv(i):
            m, kt = tiles[i]
            q0 = m * 512
            nkt = 4 * (m + 1)
            e_ = rec.pop(i)
            S.op("pe", "matmul", pO[:], lhsT=V[:, kt, :], rhs=e_[:], start=(kt == 0), stop=(kt == nkt - 1),
                 r=[V.k, e_.k], w=[pO.k])
            S.op("pe", "matmul", pD[:], lhsT=C["ones_b"][:], rhs=e_[:], start=(kt == 0), stop=(kt == nkt - 1),
                 r=["ones_b", e_.k], w=[pD.k])
            if kt == nkt - 1:
                g_ = gt[m % 2]
                S.op("dve", "reciprocal", out=rden[:], in_=pD[:], r=[pD.k], w=[rden.k])
                S.op("dve", "tensor_tensor", out=of[:], in0=pO[:], in1=rden[:], op=ALU.mult, r=[pO.k, rden.k],
                     w=[of.k])
                o_ = oo[m % 2]
                S.op("dve", "tensor_tensor", out=o_[:], in0=of[:], in1=g_[:], op=ALU.mult, r=[of.k, g_.k], w=[o_.k])
                S.dma("act", yT[rows, q0:q0 + 512], o_[:], r=[o_.k], w=[S.uk()])
                if hh + 1 < 4:
                    sel_group(hh + 1, m)

        LA = 2
        for i in range(min(LA, len(tiles))):
            emit_s(i)
        for i in range(len(tiles)):
            if i + LA < len(tiles):
                emit_s(i + LA)
            emit_pv(i)


def moba_host_consts():
    n = np.arange(32)
    mc = np.zeros((128, 64, 96), np.float32)
    for qt in range(64):
        qb = qt // 2
        mc[:, qt, 0:32] = np.where(n < qb, 0.0, -1e30)[None]
        mc[:, qt, 32:64] = (n < qb).astype(np.float32)[None]
        mc[:, qt, 64:96] = (n == qb).astype(np.float32)[None]
    ind = np.zeros((32, 32, 128), np.float32)
    for k in range(32):
        ind[k, k, :] = 1.0
    p = np.arange(128)[:, None]
    i = np.arange(512)[None, :]
    caus = np.stack([np.where(kk * 128 + p <= i, 0.0, -NEGM) for kk in range(4)], 1).astype(np.float32)
    return dict(mc=mc, ind=ind.astype(NPBF), caus=np.ascontiguousarray(caus).astype(NPBF))


def build_moba(NQT=SEQ // 512, phases="ia"):
    nc = new_nc()
    uT = nc.dram_tensor("uT", [D, SEQ], BF16, kind="ExternalInput").ap()
    wq = nc.dram_tensor("wq", [D, 2048], F32, kind="ExternalInput").ap()
    qkw_d = nc.dram_tensor("qkw", [128, 2], F32, kind="ExternalInput").ap()
    cst_d = nc.dram_tensor("cst", [128, 384], F32, kind="ExternalInput").ap()
    mc_d = nc.dram_tensor("mc", [128, 64, 96], F32, kind="ExternalInput").ap()
    ind_d = nc.dram_tensor("ind", [32, 32, 128], BF16, kind="ExternalInput").ap()
    caus_d = nc.dram_tensor("caus", [128, 4, 512], BF16, kind="ExternalInput").ap()
    yT = nc.dram_tensor("yT", [512, SEQ], BF16, kind="ExternalOutput").ap()
    qT = nc.dram_tensor("qT", [512, SEQ], BF16).ap()
    kT = nc.dram_tensor("kT", [512, SEQ], BF16).ap()
    gsT = nc.dram_tensor("gsT", [512, SEQ], BF16).ap()
    vtm = nc.dram_tensor("vtm", [SEQ, 512], BF16).ap()
    with ExitStack() as es:
        es.enter_context(nc.Block())
        S = Sched(nc, es)
        C = mk_consts(S)
        mk_eps(S, C)
        load_consts(S, C, cst_d)
        C["identf"] = B(C["cst_sb"].t[:, 256:384], "cst")
        S.lastw["identf"] = S.lastw["cst"]
        qkw = load_small(S, qkw_d, [128, 2])
        with ExitStack() as es2:
            S.es = es2
            if "i" in phases:
                emit_moba_inproj(S, C, uT, wq, qkw, qT, kT, gsT, vtm, NT=NQT)
        S.es = es
        S.barrier()
        with ExitStack() as es2:
            S.es = es2
            mc = load_small(S, mc_d, [128, 64, 96])
            IND = load_small(S, ind_d, [32, 32, 128], BF16)
            CAUS = load_small(S, caus_d, [128, 4, 512], BF16)
            if "a" in phases:
                emit_moba_attn(S, C, qT, kT, gsT, vtm, yT, mc, IND, CAUS, NQT)
        S.es = es
        S.finish()
    return nc


def moba_core_inputs(c, p):
    w = p["moba2_w_in"]
    cols = slice(c * 512, (c + 1) * 512)
    wq = np.ascontiguousarray(np.concatenate([w[:, 0 * D:1 * D][:, cols], w[:, 1 * D:2 * D][:, cols],
                                              w[:, 3 * D:4 * D][:, cols], w[:, 2 * D:3 * D][:, cols]], 1))
    qkw = np.ascontiguousarray(np.stack([p["moba2_q_norm"], p["moba2_k_norm"]], 1)).astype(np.float32)
    d = dict(wq=wq, qkw=qkw, cst=host_consts())
    d.update(moba_host_consts())
    return d


_PROGS = {}


def _prog(name, fn, *a):
    key = (name,) + a
    if key not in _PROGS:
        _PROGS[key] = fn(*a)
    return _PROGS[key]


def _tok_split(full):
    return [np.ascontiguousarray(full[:, c * TPC:(c + 1) * TPC]) for c in range(NCORES)]


def run_ssd(p, pre, uT):
    uT_full = np.ascontiguousarray(np.concatenate(uT, 1))
    res = run(_prog("ssd", build_ssd), [dict(uT=uT_full, **ssd_core_inputs(g, p, pre)) for g in range(NCORES)])
    return _tok_split(np.concatenate([r["yT"] for r in res], 0))


def run_moba(p, uT):
    uT_full = np.ascontiguousarray(np.concatenate(uT, 1))
    res = run(_prog("moba", build_moba), [dict(uT=uT_full, **moba_core_inputs(c, p)) for c in range(NCORES)])
    return _tok_split(np.concatenate([r["yT"] for r in res], 0))


def run_outproj(yT, w_out, hT, gnext):
    KC = w_out.shape[0] // 128
    nc = _prog("op", build_outproj_norm, KC, gnext is not None)
    ins = []
    for c in range(NCORES):
        d = dict(yT=yT[c], w_out=w_out, hT=hT[c])
        if gnext is not None:
            d["gw"] = lay_gw(gnext)
        ins.append(d)
    res = run(nc, ins)
    return [r["hT2"] for r in res], ([r["uT"] for r in res] if gnext is not None else None)


def run_pool(p, uT, hT, gnext):
    uT_full = np.concatenate([np.zeros((D, 16), NPBF)] + list(uT), 1)
    ins = []
    for c in range(NCORES):
        d = dict(uTh=np.ascontiguousarray(uT_full[:, c * TPC:c * TPC + TPC + 16]), w_in=p["pool1_w_in"],
                 w_grp=p["pool1_w_grp"], w_out=p["pool1_w_out"], hT=hT[c], gw=lay_gw(gnext))
        d.update(pool_core_inputs(c, p))
        ins.append(d)
    res = run(_prog("pool", build_pool), ins)
    return [r["hT2"] for r in res], [r["uT"] for r in res]


def kernel(**inputs):
    p = {k: np.asarray(v) for k, v in inputs.items()}
    x = p["x"][0]
    hT = [np.ascontiguousarray(x[c * TPC:(c + 1) * TPC].T) for c in range(NCORES)]
    res = run(_prog("norm", build_norm_only), [dict(hT=hT[c], gw=lay_gw(p["norm0"])) for c in range(NCORES)])
    uT = [r["uT"] for r in res]
    yT = run_ssd(p, "ssd0", uT)
    hT, uT = run_outproj(yT, p["ssd0_w_out"], hT, p["norm1"])
    hT, uT = run_pool(p, uT, hT, p["norm2"])
    yT = run_moba(p, uT)
    hT, uT = run_outproj(yT, p["moba2_w_out"], hT, p["norm3"])
    yT = run_ssd(p, "ssd3", uT)
    hT, _ = run_outproj(yT, p["ssd3_w_out"], hT, None)
    out = np.concatenate([h.T for h in hT], 0)[None]
    return np.ascontiguousarray(out.astype(np.float32))
```
